# Optimizing a Trainium2 kernel written in Bass

```python
import math
import jax
import jax.numpy as jnp
from jax import lax
import numpy as np

D_MODEL = 2048
BATCH = 1
SEQ = 8192
DEPTH = 4

D_FF = 5632
CONV_K = 4
SSD_HEADS = 32
SSD_HEAD_DIM = 64
SSD_D_INNER = SSD_HEADS * SSD_HEAD_DIM
SSD_GROUPS = 4
SSD_STATE = 128
SSD_CHUNK = 128
GDN_HEADS = 16
GDN_DK = 128
GDN_DV = 128
GDN_CHUNK = 64
ATTN_HEADS = 16
ATTN_HEAD_DIM = 128
MOBA_BLOCK = 256
MOBA_TOPK = 3
MOBA_Q_BLOCK = 32
REL_BUCKETS = 32
REL_MAX_DIST = 4096
N_BRANCHES = 3
N_MOD = 9
EPS = 1e-6
NEG_INF = -1e30

SSD_XBC = SSD_D_INNER + 2 * SSD_GROUPS * SSD_STATE
GDN_QKV = GDN_HEADS * (2 * GDN_DK + GDN_DV)
ATTN_QKV = 3 * ATTN_HEADS * ATTN_HEAD_DIM
IN_SPLITS = (SSD_D_INNER, SSD_XBC, SSD_HEADS, GDN_QKV, GDN_HEADS * GDN_DV, GDN_HEADS, GDN_HEADS, ATTN_QKV, N_BRANCHES * D_MODEL)
IN_TOTAL = sum(IN_SPLITS)

kernel_name = 'hybrid_ssd_deltanet_moba_macaron'


def _split(u, sizes):
    cuts = [int(s) for s in np.cumsum(sizes)[:-1]]
    return jnp.split(u, cuts, axis=-1)


def _rms(u):
    uf = u.astype(jnp.float32)
    return (uf * lax.rsqrt(jnp.mean(uf * uf, axis=-1, keepdims=True) + EPS)).astype(u.dtype)


def rms_norm(u, g):
    return _rms(u) * g


def l2_norm(u):
    uf = u.astype(jnp.float32)
    return (uf * lax.rsqrt(jnp.sum(uf * uf, axis=-1, keepdims=True) + EPS)).astype(u.dtype)


def modulate(u, g, shift, scale):
    return rms_norm(u, g) * (1.0 + scale) + shift


def swiglu(u, w_gate, w_up, w_down):
    return (jax.nn.silu(u @ w_gate) * (u @ w_up)) @ w_down


def causal_dwconv(u, w):
    k, ch = w.shape
    return lax.conv_general_dilated(u, w[:, None, :], window_strides=(1,), padding=((k - 1, 0),),
                                    dimension_numbers=('NWC', 'WIO', 'NWC'), feature_group_count=ch)


def t5_bucket(dist):
    n = jnp.maximum(dist, 0)
    exact = REL_BUCKETS // 2
    nf = jnp.maximum(n, 1).astype(jnp.float32)
    large = exact + (jnp.log(nf / exact) / math.log(REL_MAX_DIST / exact) * (REL_BUCKETS - exact)).astype(jnp.int32)
    return jnp.where(n < exact, n, jnp.minimum(large, REL_BUCKETS - 1))


def ssd_chunked(x, dt, a, bm, cm):
    bsz, t, nh, hp = x.shape
    ng, ns = bm.shape[2], bm.shape[3]
    nr = nh // ng
    L = SSD_CHUNK
    nc = t // L
    xdt = (x * dt[..., None].astype(x.dtype)).reshape(bsz, nc, L, ng, nr, hp)
    bc = bm.reshape(bsz, nc, L, ng, ns)
    cc = cm.reshape(bsz, nc, L, ng, ns)
    da = (dt * a).reshape(bsz, nc, L, ng, nr).transpose(0, 1, 3, 4, 2)
    acs = jnp.cumsum(da, axis=-1)
    idx = jnp.arange(L)
    tri = idx[:, None] >= idx[None, :]
    lmat = jnp.exp(jnp.where(tri, acs[..., :, None] - acs[..., None, :], -jnp.inf)).astype(x.dtype)
    cb = jnp.einsum('bclgn,bcsgn->bcgls', cc, bc)
    y_diag = jnp.einsum('bcgrls,bcsgrp->bclgrp', lmat * cb[:, :, :, None], xdt)
    decay_states = jnp.exp(acs[..., -1:] - acs).astype(x.dtype)
    states = jnp.einsum('bclgn,bcgrl,bclgrp->bcgrpn', bc, decay_states, xdt)
    chunk_decay = jnp.exp(acs[..., -1]).astype(x.dtype)

    def step(s, inp):
        st, dec = inp
        return s * dec[..., None, None] + st, s

    s0 = jnp.zeros((bsz, ng, nr, hp, ns), x.dtype)
    _, states_in = lax.scan(step, s0, (jnp.moveaxis(states, 1, 0), jnp.moveaxis(chunk_decay, 1, 0)))
    states_in = jnp.moveaxis(states_in, 0, 1)
    y_off = jnp.einsum('bclgn,bcgrpn,bcgrl->bclgrp', cc, states_in, jnp.exp(acs).astype(x.dtype))
    return (y_diag + y_off).reshape(bsz, t, nh, hp)


def ssd_mixer(z, xbc, dt, conv_w, conv_b, dt_bias, a_log, d_skip, norm_g):
    bsz, t, _ = z.shape
    xbc = jax.nn.silu(causal_dwconv(xbc, conv_w) + conv_b)
    xs, bm, cm = _split(xbc, (SSD_D_INNER, SSD_GROUPS * SSD_STATE, SSD_GROUPS * SSD_STATE))
    dt = jax.nn.softplus((dt + dt_bias).astype(jnp.float32))
    a = -jnp.exp(a_log.astype(jnp.float32))
    xh = xs.reshape(bsz, t, SSD_HEADS, SSD_HEAD_DIM)
    y = ssd_chunked(xh, dt, a, bm.reshape(bsz, t, SSD_GROUPS, SSD_STATE), cm.reshape(bsz, t, SSD_GROUPS, SSD_STATE))
    y = y + xh * d_skip[:, None]
    y = y.reshape(bsz, t, SSD_D_INNER) * jax.nn.silu(z)
    y = _rms(y.reshape(bsz, t, SSD_GROUPS, SSD_D_INNER // SSD_GROUPS)).reshape(bsz, t, SSD_D_INNER)
    return y * norm_g


def gated_delta_rule_chunked(q, k, v, g, beta):
    bsz, t, nh, dk = q.shape
    dv = v.shape[-1]
    L = GDN_CHUNK
    nc = t // L

    def chunks(u):
        return jnp.moveaxis(u.reshape((bsz, nc, L, nh) + u.shape[3:]), 3, 1)

    qc, kc, vc, bc = chunks(q), chunks(k), chunks(v), chunks(beta)
    gc = jnp.cumsum(chunks(g).astype(jnp.float32), axis=-1)
    idx = jnp.arange(L)
    incl = idx[:, None] >= idx[None, :]
    strict = idx[:, None] > idx[None, :]
    decay = jnp.exp(jnp.where(incl, gc[..., :, None] - gc[..., None, :], -jnp.inf)).astype(q.dtype)
    kb = kc * bc[..., None]
    a_mat = jnp.where(strict, jnp.einsum('bhcid,bhcjd->bhcij', kb, kc) * decay, 0.0)
    t_mat = (a_mat + jnp.eye(L, dtype=a_mat.dtype)).astype(jnp.float32)
    rhs = jnp.concatenate([vc * bc[..., None], kb * jnp.exp(gc)[..., None].astype(q.dtype)], axis=-1).astype(jnp.float32)
    sol = lax.linalg.triangular_solve(t_mat, rhs, left_side=True, lower=True, unit_diagonal=True).astype(q.dtype)
    u_new, w = sol[..., :dv], sol[..., dv:]
    qk = jnp.where(incl, jnp.einsum('bhcid,bhcjd->bhcij', qc, kc) * decay, 0.0)
    q_dec = qc * jnp.exp(gc)[..., None].astype(q.dtype)
    k_dec = kc * jnp.exp(gc[..., -1:] - gc)[..., None].astype(q.dtype)
    g_last = jnp.exp(gc[..., -1]).astype(q.dtype)

    def step(s, inp):
        u_c, w_c, q_c, k_c, qk_c, gl_c = inp
        v_c = u_c - jnp.einsum('bhld,bhdv->bhlv', w_c, s)
        o = jnp.einsum('bhld,bhdv->bhlv', q_c, s) + jnp.einsum('bhls,bhsv->bhlv', qk_c, v_c)
        s = s * gl_c[..., None, None] + jnp.einsum('bhld,bhlv->bhdv', k_c, v_c)
        return s, o

    xs = tuple(jnp.moveaxis(arr, 2, 0) for arr in (u_new, w, q_dec, k_dec, qk, g_last))
    s0 = jnp.zeros((bsz, nh, dk, dv), q.dtype)
    _, o = lax.scan(step, s0, xs)
    o = jnp.moveaxis(o, 0, 2).reshape(bsz, nh, t, dv)
    return jnp.moveaxis(o, 1, 2)


def gdn_mixer(qkv, z, a, b, conv_w, dt_bias, a_log, norm_g):
    bsz, t, _ = qkv.shape
    qkv = jax.nn.silu(causal_dwconv(qkv, conv_w))
    q, k, v = _split(qkv, (GDN_HEADS * GDN_DK, GDN_HEADS * GDN_DK, GDN_HEADS * GDN_DV))
    q = l2_norm(q.reshape(bsz, t, GDN_HEADS, GDN_DK)) * (GDN_DK ** -0.5)
    k = l2_norm(k.reshape(bsz, t, GDN_HEADS, GDN_DK))
    v = v.reshape(bsz, t, GDN_HEADS, GDN_DV)
    beta = jax.nn.sigmoid(b)
    g = -jnp.exp(a_log.astype(jnp.float32)) * jax.nn.softplus((a + dt_bias).astype(jnp.float32))
    o = gated_delta_rule_chunked(q, k, v, g, beta)
    o = rms_norm(o, norm_g) * jax.nn.silu(z.reshape(bsz, t, GDN_HEADS, GDN_DV))
    return o.reshape(bsz, t, GDN_HEADS * GDN_DV)


def moba_attention(q, k, v, rel_bias):
    bsz, nh, t, hd = q.shape
    nb = -(-t // MOBA_BLOCK)
    tp = nb * MOBA_BLOCK
    pad = ((0, 0), (0, 0), (0, tp - t), (0, 0))
    q, k, v = (jnp.pad(arr, pad) for arr in (q, k, v))
    kb = k.reshape(bsz, nh, nb, MOBA_BLOCK, hd)
    vb = v.reshape(bsz, nh, nb, MOBA_BLOCK, hd)
    kmean = jnp.mean(kb.astype(jnp.float32), axis=3)
    top = min(MOBA_TOPK, nb)
    nq = tp // MOBA_Q_BLOCK
    qcs = jnp.moveaxis(q.reshape(bsz, nh, nq, MOBA_Q_BLOCK, hd), 2, 0)
    bias_hk = rel_bias.T
    bi = jnp.arange(bsz)[:, None, None, None]
    hi = jnp.arange(nh)[None, :, None, None]
    blk_ar = jnp.arange(nb)
    kar = jnp.arange(MOBA_BLOCK)
    scale = hd ** -0.5

    def one_query_block(args):
        qc, ci = args
        q_pos = ci * MOBA_Q_BLOCK + jnp.arange(MOBA_Q_BLOCK)
        own = (ci * MOBA_Q_BLOCK) // MOBA_BLOCK
        gate = jnp.einsum('bhqd,bhnd->bhqn', qc.astype(jnp.float32), kmean)
        gate = jnp.where(blk_ar < own, gate, NEG_INF)
        _, sel = lax.top_k(gate, top)
        valid = sel < own
        kg = kb[bi, hi, sel]
        vg = vb[bi, hi, sel]
        dist_g = q_pos[None, None, :, None, None] - (sel[..., None] * MOBA_BLOCK + kar)
        lg = jnp.einsum('bhqd,bhqnkd->bhqnk', qc, kg).astype(jnp.float32) * scale + bias_hk[hi[..., None], t5_bucket(dist_g)]
        lg = jnp.where(valid[..., None], lg, NEG_INF).reshape(bsz, nh, MOBA_Q_BLOCK, top * MOBA_BLOCK)
        ko = lax.dynamic_index_in_dim(kb, own, axis=2, keepdims=False)
        vo = lax.dynamic_index_in_dim(vb, own, axis=2, keepdims=False)
        dist_o = q_pos[:, None] - (own * MOBA_BLOCK + kar)[None, :]
        lo = jnp.einsum('bhqd,bhkd->bhqk', qc, ko).astype(jnp.float32) * scale + bias_hk[:, t5_bucket(dist_o)]
        lo = jnp.where(dist_o >= 0, lo, NEG_INF)
        p = jax.nn.softmax(jnp.concatenate([lg, lo], axis=-1), axis=-1).astype(v.dtype)
        out = jnp.einsum('bhqk,bhqkd->bhqd', p[..., :top * MOBA_BLOCK], vg.reshape(bsz, nh, MOBA_Q_BLOCK, top * MOBA_BLOCK, hd))
        return out + jnp.einsum('bhqk,bhkd->bhqd', p[..., top * MOBA_BLOCK:], vo)

    out = lax.map(one_query_block, (qcs, jnp.arange(nq)))
    return jnp.moveaxis(out, 0, 2).reshape(bsz, nh, tp, hd)[:, :, :t]


def moba_mixer(qkv, q_norm, k_norm, rel_bias):
    bsz, t, _ = qkv.shape
    qkv = qkv.reshape(bsz, t, 3, ATTN_HEADS, ATTN_HEAD_DIM)
    q = rms_norm(qkv[:, :, 0], q_norm)
    k = rms_norm(qkv[:, :, 1], k_norm)
    v = qkv[:, :, 2]
    q, k, v = (jnp.moveaxis(arr, 2, 1) for arr in (q, k, v))
    o = moba_attention(q, k, v, rel_bias)
    return jnp.moveaxis(o, 1, 2).reshape(bsz, t, ATTN_HEADS * ATTN_HEAD_DIM)


def hybrid_mixer(u, w_in, ssd_conv_w, ssd_conv_b, ssd_dt_bias, ssd_a_log, ssd_d, ssd_norm, w_o_ssd,
                 gdn_conv_w, gdn_dt_bias, gdn_a_log, gdn_norm, w_o_gdn,
                 attn_q_norm, attn_k_norm, rel_bias, w_o_attn, w_out):
    proj = u @ w_in
    ssd_z, ssd_xbc, ssd_dt, gdn_qkv, gdn_z, gdn_a, gdn_b, attn_qkv, gate_logits = _split(proj, IN_SPLITS)
    y_ssd = ssd_mixer(ssd_z, ssd_xbc, ssd_dt, ssd_conv_w, ssd_conv_b, ssd_dt_bias, ssd_a_log, ssd_d, ssd_norm) @ w_o_ssd
    y_gdn = gdn_mixer(gdn_qkv, gdn_z, gdn_a, gdn_b, gdn_conv_w, gdn_dt_bias, gdn_a_log, gdn_norm) @ w_o_gdn
    y_attn = moba_mixer(attn_qkv, attn_q_norm, attn_k_norm, rel_bias) @ w_o_attn
    g_ssd, g_gdn, g_attn = jnp.split(jax.nn.sigmoid(gate_logits), N_BRANCHES, axis=-1)
    return (g_ssd * y_ssd + g_gdn * y_gdn + g_attn * y_attn) @ w_out


def setup_inputs(seed: int = 0) -> dict:
    key = jax.random.key(seed)
    ks = iter(jax.random.split(key, 48))
    L, D = DEPTH, D_MODEL

    def nrm(shape, s):
        return jax.random.normal(next(ks), shape, jnp.float32) * s

    def gain(shape):
        return 1.0 + nrm(shape, 0.02)

    def dt_bias(n):
        dt = jnp.exp(jax.random.uniform(next(ks), (L, n), jnp.float32, minval=math.log(1e-3), maxval=math.log(1e-1)))
        return dt + jnp.log(-jnp.expm1(-dt))

    def a_log(n):
        return jnp.log(jax.random.uniform(next(ks), (L, n), jnp.float32, minval=1.0, maxval=16.0))

    return {
        'x': nrm((BATCH, SEQ, D), 1.0),
        'c': nrm((BATCH, D), 1.0),
        'w_mod': nrm((L, D, N_MOD * D), 0.5 * D ** -0.5),
        'b_mod': nrm((L, N_MOD * D), 0.02),
        'norm_ffn1': gain((L, D)),
        'ffn1_w_gate': nrm((L, D, D_FF), D ** -0.5),
        'ffn1_w_up': nrm((L, D, D_FF), D ** -0.5),
        'ffn1_w_down': nrm((L, D_FF, D), D_FF ** -0.5),
        'norm_mix': gain((L, D)),
        'w_in': nrm((L, D, IN_TOTAL), D ** -0.5),
        'ssd_conv_w': nrm((L, CONV_K, SSD_XBC), CONV_K ** -0.5),
        'ssd_conv_b': nrm((L, SSD_XBC), 0.02),
        'ssd_dt_bias': dt_bias(SSD_HEADS),
        'ssd_a_log': a_log(SSD_HEADS),
        'ssd_d': gain((L, SSD_HEADS)),
        'ssd_norm': gain((L, SSD_D_INNER)),
        'w_o_ssd': nrm((L, SSD_D_INNER, D), SSD_D_INNER ** -0.5),
        'gdn_conv_w': nrm((L, CONV_K, GDN_QKV), CONV_K ** -0.5),
        'gdn_dt_bias': dt_bias(GDN_HEADS),
        'gdn_a_log': a_log(GDN_HEADS),
        'gdn_norm': gain((L, GDN_DV)),
        'w_o_gdn': nrm((L, GDN_HEADS * GDN_DV, D), (GDN_HEADS * GDN_DV) ** -0.5),
        'attn_q_norm': gain((L, ATTN_HEAD_DIM)),
        'attn_k_norm': gain((L, ATTN_HEAD_DIM)),
        'rel_bias': nrm((REL_BUCKETS, ATTN_HEADS), 0.2),
        'w_o_attn': nrm((L, ATTN_HEADS * ATTN_HEAD_DIM, D), (ATTN_HEADS * ATTN_HEAD_DIM) ** -0.5),
        'w_out': nrm((L, D, D), D ** -0.5),
        'norm_ffn2': gain((L, D)),
        'ffn2_w_gate': nrm((L, D, D_FF), D ** -0.5),
        'ffn2_w_up': nrm((L, D, D_FF), D ** -0.5),
        'ffn2_w_down': nrm((L, D_FF, D), D_FF ** -0.5),
    }


def reference(x, c, w_mod, b_mod, norm_ffn1, ffn1_w_gate, ffn1_w_up, ffn1_w_down, norm_mix, w_in,
              ssd_conv_w, ssd_conv_b, ssd_dt_bias, ssd_a_log, ssd_d, ssd_norm, w_o_ssd,
              gdn_conv_w, gdn_dt_bias, gdn_a_log, gdn_norm, w_o_gdn,
              attn_q_norm, attn_k_norm, rel_bias, w_o_attn, w_out,
              norm_ffn2, ffn2_w_gate, ffn2_w_up, ffn2_w_down):
    bsz = x.shape[0]
    h = x
    for l in range(DEPTH):
        mod = (c @ w_mod[l] + b_mod[l]).reshape(bsz, N_MOD, D_MODEL)
        sh1, sc1, g1, sh2, sc2, g2, sh3, sc3, g3 = [mod[:, i, None, :] for i in range(N_MOD)]
        h = h + 0.5 * g1 * swiglu(modulate(h, norm_ffn1[l], sh1, sc1), ffn1_w_gate[l], ffn1_w_up[l], ffn1_w_down[l])
        h = h + g2 * hybrid_mixer(modulate(h, norm_mix[l], sh2, sc2), w_in[l],
                                  ssd_conv_w[l], ssd_conv_b[l], ssd_dt_bias[l], ssd_a_log[l], ssd_d[l], ssd_norm[l], w_o_ssd[l],
                                  gdn_conv_w[l], gdn_dt_bias[l], gdn_a_log[l], gdn_norm[l], w_o_gdn[l],
                                  attn_q_norm[l], attn_k_norm[l], rel_bias, w_o_attn[l], w_out[l])
        h = h + 0.5 * g3 * swiglu(modulate(h, norm_ffn2[l], sh3, sc3), ffn2_w_gate[l], ffn2_w_up[l], ffn2_w_down[l])
    return h
```

```python
import math
import numpy as np
from contextlib import ExitStack
import concourse.bass as bass
import concourse.mybir as mybir
from concourse.bass_utils import run_bass_kernel_spmd
import ml_dtypes

F32 = mybir.dt.float32
BF16 = mybir.dt.bfloat16
I32 = mybir.dt.int32
AF = mybir.ActivationFunctionType
ALU = mybir.AluOpType
AX = mybir.AxisListType
NPBF16 = ml_dtypes.bfloat16

SAME_ENGINE_SYNC = True
EPOCH_MAX = 30000


class Sched:
    ENGS = ('pe', 'act', 'dve', 'pool', 'sp')

    def __init__(self, nc, ndma=8):
        self.nc = nc
        self.streams = {e: [] for e in self.ENGS}
        self.cnt = {}
        self.known = {e: {} for e in self.ENGS}
        self.lastw = {}
        self.readers = {}
        self.epoch = {e: 0 for e in self.ENGS}
        self.dma_rr = {e: 0 for e in self.ENGS}
        self.dma_epoch = {}
        self.ndma = ndma
        self.keys = []
        self.keyset = set()
        self.out_tokens = []
        self.n_ops = 0

    def _key(self, k):
        if k not in self.keyset:
            self.keyset.add(k)
            self.keys.append(k)
        return k

    def _deps(self, eng, reads, writes, extra=()):
        need = {}

        def add(t):
            if t is None:
                return
            k, v = t
            if not SAME_ENGINE_SYNC or eng == 'pe':
                if k[0] == 'e' and k[1] == eng:
                    return
            if need.get(k, 0) < v:
                need[k] = v
        for r in reads:
            add(self.lastw.get(r))
        for w in writes:
            add(self.lastw.get(w))
            rd = self.readers.get(w)
            if rd:
                for k, v in rd.items():
                    add((k, v))
        for t in extra:
            add(t)
        waits = []
        kn = self.known[eng]
        for k, v in need.items():
            if kn.get(k, 0) >= v:
                continue
            kn[k] = v
            waits.append((k, v))
        return waits

    def _commit(self, tok, reads, writes):
        k, v = tok
        for r in reads:
            d = self.readers.setdefault(r, {})
            if d.get(k, 0) < v:
                d[k] = v
        for w in writes:
            self.lastw[w] = tok
            self.readers[w] = {}

    def op(self, eng, fn, reads=(), writes=()):
        psr = [r for r in reads if isinstance(r, tuple) and r and r[0] == 'ps']
        if psr:
            writes = list(writes) + [r for r in psr if r not in writes]
        waits = self._deps(eng, reads, writes)
        key = self._key(('e', eng, self.epoch[eng]))
        val = self.cnt.get(key, 0) + 1
        self.cnt[key] = val
        if val >= EPOCH_MAX:
            self.epoch[eng] += 1
        tok = (key, val)
        self.streams[eng].append((waits, fn, key, 1))
        self._commit(tok, reads, writes)
        self.n_ops += 1
        return tok

    def dma(self, q, out, in_, reads=(), writes=(), is_output=False, **kw):
        j = self.dma_rr[q] % self.ndma
        self.dma_rr[q] += 1
        ep = self.dma_epoch.get((q, j), 0)
        key = ('d', q, j, ep)
        prev = self.cnt.get(key, 0)
        extra = []
        if prev > 0:
            extra.append((key, prev))
        elif ep > 0:
            pk = ('d', q, j, ep - 1)
            extra.append((pk, self.cnt[pk]))
        waits = self._deps(q, reads, writes, extra)
        self._key(key)
        val = prev + 16
        self.cnt[key] = val
        if val >= EPOCH_MAX:
            self.dma_epoch[(q, j)] = ep + 1
        tok = (key, val)

        def fn(eng, out=out, in_=in_, kw=kw):
            return eng.dma_start(out=out, in_=in_, **kw)
        self.streams[q].append((waits, fn, key, 16))
        self._commit(tok, reads, writes)
        if is_output:
            self.out_tokens.append(tok)
        self.n_ops += 1
        return tok

    def barrier(self):
        toks = [(k, v) for k, v in self.cnt.items()]
        for e in self.ENGS:
            waits = []
            kn = self.known[e]
            for k, v in toks:
                if k[0] == 'e' and k[1] == e:
                    continue
                if kn.get(k, 0) >= v:
                    continue
                kn[k] = v
                waits.append((k, v))
            if waits:
                self.streams[e].append((waits, None, None, 0))
        self.lastw = {}
        self.readers = {}

    def emit(self):
        nc = self.nc
        self.barrier()
        with ExitStack() as st:
            sems = {}
            for i, k in enumerate(self.keys):
                sems[k] = st.enter_context(nc.semaphore("s%d" % i))
            with nc.Block() as block:
                def mk(name):
                    def f(eng):
                        for (waits, fn, key, inc) in self.streams[name]:
                            for (k, v) in waits:
                                eng.wait_ge(sems[k], v)
                            if fn is not None:
                                ins = fn(eng)
                                ins.then_inc(sems[key], inc)
                    return f
                block.tensor(mk('pe'))
                block.scalar(mk('act'))
                block.vector(mk('dve'))
                block.gpsimd(mk('pool'))
                block.sync(mk('sp'))


class Mem:
    def __init__(self, nc, st, sbuf_bytes=200 * 1024):
        self.nc = nc
        self.big = st.enter_context(nc.sbuf_tensor("big", [128, sbuf_bytes // 4], F32))
        self.ps = st.enter_context(nc.psum_tensor("psall", [128, 8 * 512], F32))
        self.off = 0
        self.cap = sbuf_bytes
        self.marks = []

    def push(self):
        self.marks.append(self.off)

    def pop(self):
        self.off = self.marks.pop()

    def alloc(self, free_elems, dtype, parts=128):
        esz = 2 if dtype == BF16 else 4
        nbytes = (free_elems * esz + 63) // 64 * 64
        o = self.off
        self.off += nbytes
        assert self.off <= self.cap, "SBUF overflow %d" % self.off
        v = self.big[0:parts, o // 4:(o + nbytes) // 4]
        if dtype != F32:
            v = v.bitcast(dtype)
        return v[:, 0:free_elems]

    def bank(self, b, dtype=F32, parts=128):
        v = self.ps[0:parts, b * 512:(b + 1) * 512]
        if dtype != F32:
            v = v.bitcast(dtype)
        return v


D = 2048
DFF = 5632
TL = 1024
NK = 16
EPS = 1e-6


def new_nc():
    return bass.Bass("TRN2", target_bir_lowering=False)


def build_mod():
    nc = new_nc()
    wm = nc.dram_tensor("wm", [4, D, 2304], F32, kind="ExternalInput").ap()
    bm = nc.dram_tensor("bm", [4, 128, 18], F32, kind="ExternalInput").ap()
    cT = nc.dram_tensor("cT", [128, NK], F32, kind="ExternalInput").ap()
    out = nc.dram_tensor("modp", [4, 128, 18], F32, kind="ExternalOutput").ap()
    with ExitStack() as st:
        m = Mem(nc, st)
        s = Sched(nc)
        c_sb = m.alloc(NK, F32)
        s.dma('sp', c_sb, cT, writes=['c'])
        b_sb = m.alloc(4 * 18, F32)
        for l in range(4):
            s.dma('sp', b_sb[:, l * 18:(l + 1) * 18], bm[l], writes=[('b', l)])
        o_sb = m.alloc(4 * 18, F32)
        wt = [m.alloc(NK * 384, F32) for _ in range(2)]
        it = 0
        for l in range(4):
            for cb in range(6):
                w = wt[it % 2]
                w3 = w.rearrange("p (k c) -> p k c", k=NK)
                src = wm[l][:, cb * 384:(cb + 1) * 384].rearrange("(k p) c -> p k c", p=128)
                s.dma('sp' if it % 2 == 0 else 'act', w3, src, writes=[('w', it % 2)])
                for cc in range(3):
                    col = cb * 3 + cc
                    ps = m.bank(l % 2)[:, col:col + 1]
                    for k in range(NK):
                        s.op('pe', lambda e, ps=ps, w3=w3, k=k, cc=cc: e.matmul(
                            ps, lhsT=w3[:, k, cc * 128:(cc + 1) * 128], rhs=c_sb[:, k:k + 1],
                            start=(k == 0), stop=(k == NK - 1)),
                            reads=[('w', it % 2), 'c'], writes=[('ps', l % 2)])
                it += 1
            s.op('dve', lambda e, l=l: e.tensor_tensor(out=o_sb[:, l * 18:(l + 1) * 18], in0=m.bank(l % 2)[:, 0:18],
                                                       in1=b_sb[:, l * 18:(l + 1) * 18], op=ALU.add),
                 reads=[('ps', l % 2), ('b', l)], writes=[('o', l)])
            s.dma('sp', out[l], o_sb[:, l * 18:(l + 1) * 18], reads=[('o', l)], is_output=True)
        s.emit()
    return nc


def rms_modulate(s, m, hT, uT, gs, sh, tmpb, tag, ones_f32, psb):
    sq, rstd, tmp = tmpb
    for t in range(TL // 512):
        tsl = slice(t * 512, (t + 1) * 512)
        ps = m.bank(psb)
        for k in range(NK):
            sqb = sq[k % 2]
            s.op('act', lambda e, sqb=sqb, k=k, tsl=tsl: e.activation(out=sqb, in_=hT[:, k, tsl], func=AF.Square),
                 reads=[('hT', k, t)], writes=[('sq', k % 2)])
            s.op('pe', lambda e, ps=ps, sqb=sqb, k=k: e.matmul(ps, lhsT=ones_f32, rhs=sqb, start=(k == 0), stop=(k == NK - 1)),
                 reads=[('sq', k % 2), 'ones'], writes=[('ps', psb)])
        s.op('act', lambda e, ps=ps: e.activation(out=rstd, in_=ps, func=AF.Sqrt, scale=1.0 / D, bias=EPSB[0]),
             reads=[('ps', psb), 'epsb'], writes=['rstd'])
        s.op('dve', lambda e: e.reciprocal(out=rstd, in_=rstd), reads=['rstd'], writes=['rstd'])
        for k in range(NK):
            tb = tmp[k % 2]
            s.op('dve', lambda e, tb=tb, k=k, tsl=tsl: e.tensor_tensor(out=tb, in0=hT[:, k, tsl], in1=rstd, op=ALU.mult),
                 reads=[('hT', k, t), 'rstd'], writes=[('tmp', k % 2)])
            s.op('act', lambda e, tb=tb, k=k, tsl=tsl: e.activation(out=uT[:, k, tsl], in_=tb, func=AF.Identity,
                                                                    scale=gs[:, k:k + 1], bias=sh[:, k:k + 1]),
                 reads=[('tmp', k % 2), tag], writes=[('uT', t)])


EPSB = [None]


def ffn(s, m, hT, uT, utag, wg, wu, wd, ghalf, bufs):
    wgb, wub, wdb, hh, sg = bufs
    NP = DFF // 256
    for j in range(NP):
        b = j % 2
        wg3 = wgb[b].rearrange("p (k c) -> p k c", k=NK)
        wu3 = wub[b].rearrange("p (k c) -> p k c", k=NK)
        wd3 = wdb[b].rearrange("p (j o) -> p j o", j=2)
        hh3 = hh[b].rearrange("p (j t) -> p j t", j=2)
        s.dma('pool', wg3, wg[:, j * 256:(j + 1) * 256].rearrange("(k p) c -> p k c", p=128), writes=[('wg', b)])
        s.dma('pool', wu3, wu[:, j * 256:(j + 1) * 256].rearrange("(k p) c -> p k c", p=128), writes=[('wu', b)])
        s.dma('pool', wd3, wd[j * 256:(j + 1) * 256, :].rearrange("(j p) o -> p j o", p=128), writes=[('wd', b)])
        for t in range(TL // 512):
            tsl = slice(t * 512, (t + 1) * 512)
            for jj in range(2):
                bg = (t * 2 + jj) % 2
                pg = m.bank(bg)
                pu = m.bank(2 + bg)
                for k in range(NK):
                    s.op('pe', lambda e, pg=pg, wg3=wg3, k=k, jj=jj, tsl=tsl: e.matmul(
                        pg, lhsT=wg3[:, k, jj * 128:(jj + 1) * 128], rhs=uT[:, k, tsl], start=(k == 0), stop=(k == NK - 1)),
                        reads=[('wg', b), (utag, t)], writes=[('ps', bg)])
                for k in range(NK):
                    s.op('pe', lambda e, pu=pu, wu3=wu3, k=k, jj=jj, tsl=tsl: e.matmul(
                        pu, lhsT=wu3[:, k, jj * 128:(jj + 1) * 128], rhs=uT[:, k, tsl], start=(k == 0), stop=(k == NK - 1)),
                        reads=[('wu', b), (utag, t)], writes=[('ps', 2 + bg)])
                sgb = sg[bg]
                s.op('act', lambda e, sgb=sgb, pg=pg: e.activation(out=sgb, in_=pg, func=AF.Silu),
                     reads=[('ps', bg)], writes=[('sg', bg)])
                s.op('dve', lambda e, sgb=sgb, pu=pu, hh3=hh3, jj=jj, tsl=tsl: e.tensor_tensor(
                    out=hh3[:, jj, tsl], in0=sgb, in1=pu, op=ALU.mult),
                    reads=[('sg', bg), ('ps', 2 + bg)], writes=[('hh', b, jj, t)])
        i = 0
        for o in range(NK):
            for t in range(TL // 512):
                tsl = slice(t * 512, (t + 1) * 512)
                pb = 4 + (i % 4)
                i += 1
                pd = m.bank(pb)
                for jj in range(2):
                    s.op('pe', lambda e, pd=pd, wd3=wd3, jj=jj, o=o, hh3=hh3, tsl=tsl: e.matmul(
                        pd, lhsT=wd3[:, jj, o * 128:(o + 1) * 128], rhs=hh3[:, jj, tsl], start=(jj == 0), stop=(jj == 1)),
                        reads=[('wd', b), ('hh', b, jj, t)], writes=[('ps', pb)])
                s.op('dve', lambda e, pd=pd, o=o, tsl=tsl: e.scalar_tensor_tensor(
                    out=hT[:, o, tsl], in0=pd, scalar=ghalf[:, o:o + 1], in1=hT[:, o, tsl], op0=ALU.mult, op1=ALU.add),
                    reads=[('ps', pb), ('hT', o, t), 'ghalf'], writes=[('hT', o, t)])


def setup_consts(s, m):
    ones = m.alloc(128, F32)
    s.op('dve', lambda e: e.memset(ones, 1.0), writes=['ones'])
    epsb = m.alloc(1, F32)
    s.op('dve', lambda e: e.memset(epsb, EPS), writes=['epsb'])
    EPSB[0] = epsb
    return ones


def build_A():
    nc = new_nc()
    hin = nc.dram_tensor("hin", [D, TL], F32, kind="ExternalInput").ap()
    modT = nc.dram_tensor("modT", [128, 144], F32, kind="ExternalInput").ap()
    gains = nc.dram_tensor("gains", [128, 2 * NK], F32, kind="ExternalInput").ap()
    wg = nc.dram_tensor("wg", [D, DFF], F32, kind="ExternalInput").ap()
    wu = nc.dram_tensor("wu", [D, DFF], F32, kind="ExternalInput").ap()
    wd = nc.dram_tensor("wd", [DFF, D], F32, kind="ExternalInput").ap()
    hout = nc.dram_tensor("hout", [D, TL], F32, kind="ExternalOutput").ap()
    uout = nc.dram_tensor("uout", [D, TL], BF16, kind="ExternalOutput").ap()
    with ExitStack() as st:
        m = Mem(nc, st)
        s = Sched(nc)
        ones = setup_consts(s, m)
        hT = m.alloc(NK * TL, F32).rearrange("p (k t) -> p k t", k=NK)
        uT = m.alloc(NK * TL, BF16).rearrange("p (k t) -> p k t", k=NK)
        mod = m.alloc(144, F32)
        gn = m.alloc(2 * NK, F32)
        gs = m.alloc(2 * NK, F32)
        gh = m.alloc(NK, F32)
        tmpb = ([m.alloc(512, F32) for _ in range(2)], m.alloc(512, F32), [m.alloc(512, F32) for _ in range(2)])
        bufs = ([m.alloc(NK * 256, BF16) for _ in range(2)], [m.alloc(NK * 256, BF16) for _ in range(2)],
                [m.alloc(2 * D, BF16) for _ in range(2)], [m.alloc(2 * TL, BF16) for _ in range(2)],
                [m.alloc(512, F32) for _ in range(2)])
        for k in range(NK):
            s.dma('sp', hT[:, k, :], hin[k * 128:(k + 1) * 128, :], writes=[('hT', k, 0), ('hT', k, 1)])
        s.dma('sp', mod, modT, writes=['mod'])
        s.dma('sp', gn, gains, writes=['gn'])
        s.op('dve', lambda e: e.scalar_tensor_tensor(out=gs[:, 0:NK], in0=mod[:, 16:32], scalar=1.0, in1=gn[:, 0:NK],
                                                     op0=ALU.add, op1=ALU.mult), reads=['mod', 'gn'], writes=['u1gs'])
        s.op('dve', lambda e: e.scalar_tensor_tensor(out=gs[:, NK:2 * NK], in0=mod[:, 64:80], scalar=1.0, in1=gn[:, NK:2 * NK],
                                                     op0=ALU.add, op1=ALU.mult), reads=['mod', 'gn'], writes=['u2gs'])
        s.op('dve', lambda e: e.tensor_scalar(out=gh, in0=mod[:, 32:48], scalar1=0.5, scalar2=None, op0=ALU.mult),
             reads=['mod'], writes=['ghalf'])
        rms_modulate(s, m, hT, uT, gs[:, 0:NK], mod[:, 0:16], tmpb, 'u1gs', ones, 7)
        ffn(s, m, hT, uT, 'uT', wg, wu, wd, gh, bufs)
        rms_modulate(s, m, hT, uT, gs[:, NK:2 * NK], mod[:, 48:64], tmpb, 'u2gs', ones, 7)
        for k in range(NK):
            s.dma('sp', hout[k * 128:(k + 1) * 128, :], hT[:, k, :], reads=[('hT', k, 0), ('hT', k, 1)], is_output=True)
            s.dma('sp', uout[k * 128:(k + 1) * 128, :], uT[:, k, :], reads=[('uT', 0), ('uT', 1)], is_output=True)
        print("A ops", s.n_ops)
        s.emit()
    return nc


T = 8192


def make_tri_ident(s, m):
    ones_m = m.alloc(128, F32)
    tri = m.alloc(128, F32)
    ident = m.alloc(128, F32)
    s.op('pool', lambda e: e.memset(ones_m, 1.0), writes=['ones_m'])
    s.op('pool', lambda e: e.affine_select(out=tri, in_=ones_m, pattern=[[1, 128]], compare_op=ALU.is_ge, fill=0.0,
                                           base=0, channel_multiplier=-1), reads=['ones_m'], writes=['tri'])
    s.op('pool', lambda e: e.affine_select(out=ident, in_=ones_m, pattern=[[1, 128]], compare_op=ALU.is_equal, fill=0.0,
                                           base=0, channel_multiplier=-1), reads=['ones_m'], writes=['ident'])
    return ones_m, tri, ident


def build_ssd(T=8192):
    nc = new_nc()
    uT_d = nc.dram_tensor("uT", [D, T], BF16, kind="ExternalInput").ap()
    w_d = nc.dram_tensor("w", [D, 772], F32, kind="ExternalInput").ap()
    cw_d = nc.dram_tensor("cw", [128, 4, 5], F32, kind="ExternalInput").ap()
    hp_d = nc.dram_tensor("hp", [128, 3, 16], F32, kind="ExternalInput").ap()
    dc_d = nc.dram_tensor("dcol", [128, 2], F32, kind="ExternalInput").ap()
    y_d = nc.dram_tensor("y", [256, T], BF16, kind="ExternalOutput").ap()
    with ExitStack() as st:
        m = Mem(nc, st)
        s = Sched(nc)
        ones_m, tri, ident = make_tri_ident(s, m)
        onec = ones_m[:, 0:1]
        zeros_m = m.alloc(128, F32)
        s.op('pool', lambda e: e.memset(zeros_m, 0.0), writes=['zeros_m'])
        wb = m.alloc(NK * 772, BF16).rearrange("p (k c) -> p k c", k=NK)
        s.dma('pool', wb, w_d.rearrange("(k p) c -> p k c", p=128), writes=['wb'])
        cw = m.alloc(20, F32).rearrange("p (c k) -> p c k", c=4)
        s.dma('sp', cw, cw_d, writes=['cw'])
        hp = m.alloc(48, F32).rearrange("p (a b) -> p a b", a=3)
        s.dma('sp', hp, hp_d, writes=['hp'])
        dcol = m.alloc(2, F32)
        s.dma('sp', dcol, dc_d, writes=['dcol'])
        a16 = m.alloc(16, F32)
        s.op('act', lambda e: e.activation(out=a16, in_=hp[:, 1, :], func=AF.Exp), reads=['hp'], writes=['a16'])
        s.op('dve', lambda e: e.tensor_scalar(out=a16, in0=a16, scalar1=-1.0, scalar2=None, op0=ALU.mult), reads=['a16'], writes=['a16'])
        ub = [m.alloc(NK * 512, BF16).rearrange("p (k t) -> p k t", k=NK) for _ in range(2)]
        pre = [m.alloc(515, F32) for _ in range(4)]
        for c in range(4):
            s.op('pool', lambda e, c=c: e.memset(pre[c][:, 0:3], 0.0), writes=[('pre', c)])
        acc = [m.alloc(512, F32) for _ in range(2)]
        xT = m.alloc(2 * 512, F32).rearrange("p (c t) -> p c t", c=2)
        BT = m.alloc(512, F32)
        BTb = m.alloc(512, BF16)
        CTf = m.alloc(512, F32)
        CTb = m.alloc(512, BF16)
        sz = m.alloc(2 * 512, F32).rearrange("p (c t) -> p c t", c=2)
        dt_sb = m.alloc(16, F32)
        da_sb = m.alloc(16, F32)
        et = m.alloc(16, F32)
        S_f = m.alloc(256, F32)
        S_b = m.alloc(256, BF16)
        s.op('pool', lambda e: e.memset(S_f, 0.0), writes=['Sf'])
        s.op('pool', lambda e: e.memset(S_b, 0.0), writes=['Sb'])
        darep = [m.alloc(128, F32) for _ in range(2)]
        MG = m.alloc(128, F32)
        targ = [m.alloc(128, F32) for _ in range(2)]
        MTb = [m.alloc(128, BF16) for _ in range(2)]
        eacs = m.alloc(512, F32).rearrange("p (h l) -> p h l", h=4)
        CpT = [m.alloc(128, BF16) for _ in range(2)]
        acol = m.alloc(4, F32)
        nacol = m.alloc(4, F32)
        dec = m.alloc(4, F32)
        dtdec = m.alloc(4, F32)
        cdec = m.alloc(4, F32)
        xdt = m.alloc(256, BF16)
        xdd = m.alloc(256, BF16)
        Btok = m.alloc(128, BF16)
        ytmp = m.alloc(256, F32).rearrange("p (c l) -> p c l", c=2)
        yout = [m.alloc(2 * 512, BF16).rearrange("p (c t) -> p c t", c=2) for _ in range(2)]
        Stmp = m.alloc(256, F32)
        NT = T // 512
        for ti in range(NT):
            u = ub[ti % 2]
            ur = ('u', ti % 2)
            s.dma('sp', u, uT_d[:, ti * 512:(ti + 1) * 512].rearrange("(k p) t -> p k t", p=128), writes=[ur])
            for g in range(6):
                pb = g % 2
                ps = m.bank(pb)
                for k in range(NK):
                    s.op('pe', lambda e, ps=ps, g=g, k=k, u=u: e.matmul(ps, lhsT=wb[:, k, g * 128:(g + 1) * 128], rhs=u[:, k, :],
                                                                     start=(k == 0), stop=(k == NK - 1)),
                         reads=['wb', ur], writes=[('ps', pb)])
                if g < 2:
                    s.op('act', lambda e, ps=ps, g=g: e.activation(out=sz[:, g, :], in_=ps, func=AF.Silu),
                         reads=[('ps', pb)], writes=[('sz', g)])
                else:
                    c = g - 2
                    s.op('act', lambda e, ps=ps, c=c: e.activation(out=pre[c][:, 3:515], in_=ps, func=AF.Copy),
                         reads=[('ps', pb)], writes=[('pre', c)])
                    a = acc[c % 2]
                    s.op('dve', lambda e, a=a, c=c: e.tensor_scalar(out=a, in0=pre[c][:, 0:512], scalar1=cw[:, c, 0:1], scalar2=cw[:, c, 4:5],
                                                                    op0=ALU.mult, op1=ALU.add), reads=[('pre', c), 'cw'], writes=[('acc', c % 2)])
                    for kk in range(1, 4):
                        s.op('dve', lambda e, a=a, c=c, kk=kk: e.scalar_tensor_tensor(out=a, in0=pre[c][:, kk:kk + 512], scalar=cw[:, c, kk:kk + 1],
                                                                                      in1=a, op0=ALU.mult, op1=ALU.add),
                             reads=[('pre', c), 'cw', ('acc', c % 2)], writes=[('acc', c % 2)])
                    s.op('pool', lambda e, c=c: e.tensor_copy(out=pre[c][:, 0:3], in_=pre[c][:, 512:515]), reads=[('pre', c)], writes=[('pre', c)])
                    if c < 2:
                        s.op('act', lambda e, a=a, c=c: e.activation(out=xT[:, c, :], in_=a, func=AF.Silu), reads=[('acc', c % 2)], writes=[('xT', c)])
                    elif c == 2:
                        s.op('act', lambda e, a=a: e.activation(out=BT, in_=a, func=AF.Silu), reads=[('acc', 0)], writes=['BT'])
                        s.op('pool', lambda e: e.tensor_copy(out=BTb, in_=BT), reads=['BT'], writes=['BTb'])
                    else:
                        s.op('act', lambda e, a=a: e.activation(out=CTf, in_=a, func=AF.Silu), reads=[('acc', 1)], writes=['CTf'])
                        s.op('pool', lambda e: e.tensor_copy(out=CTb, in_=CTf), reads=['CTf'], writes=['CTb'])
            pdt = m.bank(2)
            for ch in range(4):
                for k in range(NK):
                    s.op('pe', lambda e, ch=ch, k=k, u=u: e.matmul(pdt[:, ch * 4:(ch + 1) * 4], lhsT=u[:, k, ch * 128:(ch + 1) * 128],
                                                               rhs=wb[:, k, 768:772], start=(k == 0), stop=(k == NK - 1)),
                         reads=['wb', ur], writes=[('ps', 2)])
            s.op('dve', lambda e: e.tensor_tensor(out=et, in0=pdt[:, 0:16], in1=hp[:, 0, :], op=ALU.add), reads=[('ps', 2), 'hp'], writes=['et'])
            s.op('act', lambda e: e.activation(out=et, in_=et, func=AF.Exp), reads=['et'], writes=['et'])
            s.op('act', lambda e: e.activation(out=dt_sb, in_=et, func=AF.Ln, bias=onec, scale=1.0), reads=['et', 'ones_m'], writes=['dt'])
            s.op('dve', lambda e: e.tensor_tensor(out=da_sb, in0=dt_sb, in1=a16, op=ALU.mult), reads=['dt', 'a16'], writes=['da'])
            for ch in range(4):
                csl = slice(ch * 128, (ch + 1) * 128)
                dsl = slice(ch * 4, (ch + 1) * 4)
                pG = m.bank(3)[:, 0:128]
                s.op('pe', lambda e, pG=pG, csl=csl: e.matmul(pG, lhsT=BTb[:, csl], rhs=CTb[:, csl], start=True, stop=True),
                     reads=['BTb', 'CTb'], writes=[('ps', 3)])
                s.op('dve', lambda e, pG=pG: e.tensor_tensor(out=MG, in0=pG, in1=tri, op=ALU.mult), reads=[('ps', 3), 'tri'], writes=['MG'])
                pcol = m.bank(5)[:, 384:388]
                s.op('pe', lambda e, pcol=pcol, dsl=dsl: e.matmul(pcol, lhsT=tri, rhs=da_sb[:, dsl], start=True, stop=True),
                     reads=['tri', 'da'], writes=[('ps', 5)])
                palast = m.bank(5)[:, 388:392]
                s.op('pe', lambda e, palast=palast, dsl=dsl: e.matmul(palast, lhsT=ones_m, rhs=da_sb[:, dsl], start=True, stop=True),
                     reads=['ones_m', 'da'], writes=[('ps', 5)])
                s.op('dve', lambda e, pcol=pcol: e.tensor_copy(out=acol, in_=pcol), reads=[('ps', 5)], writes=['acol'])
                s.op('dve', lambda e: e.tensor_scalar(out=nacol, in0=acol, scalar1=-1.0, scalar2=None, op0=ALU.mult), reads=['acol'], writes=['nacol'])
                prow = m.bank(4).rearrange("p (h l) -> p h l", h=4)
                for h in range(4):
                    dr = darep[h % 2]
                    s.op('pool', lambda e, dr=dr, h=h, ch=ch: e.tensor_scalar(out=dr, in0=ones_m, scalar1=da_sb[:, ch * 4 + h:ch * 4 + h + 1], scalar2=None,
                                                                           op0=ALU.mult), reads=['ones_m', 'da'], writes=[('darep', h % 2)])
                    s.op('pe', lambda e, dr=dr, h=h: e.matmul(prow[:, h, :], lhsT=dr, rhs=tri, start=True, stop=True),
                         reads=[('darep', h % 2), 'tri'], writes=[('ps', 4)])
                s.op('act', lambda e: e.activation(out=eacs, in_=prow, func=AF.Exp), reads=[('ps', 4)], writes=['eacs'])
                s.op('dve', lambda e, palast=palast: e.tensor_tensor(out=dec, in0=palast, in1=acol, op=ALU.subtract), reads=[('ps', 5), 'acol'], writes=['dec'])
                s.op('act', lambda e: e.activation(out=dec, in_=dec, func=AF.Exp), reads=['dec'], writes=['dec'])
                s.op('dve', lambda e, dsl=dsl: e.tensor_tensor(out=dtdec, in0=dec, in1=dt_sb[:, dsl], op=ALU.mult), reads=['dec', 'dt'], writes=['dtdec'])
                s.op('act', lambda e, palast=palast: e.activation(out=cdec, in_=palast, func=AF.Exp), reads=[('ps', 5)], writes=['cdec'])
                pX = m.bank(5)[:, 0:256]
                for c in range(2):
                    s.op('pe', lambda e, c=c, csl=csl: e.transpose(out=pX[:, c * 128:(c + 1) * 128], in_=xT[:, c, csl], identity=ident),
                         reads=[('xT', c), 'ident'], writes=[('ps', 5)])
                pB = m.bank(5)[:, 256:384]
                s.op('pe', lambda e, csl=csl: e.transpose(out=pB, in_=BT[:, csl], identity=ident), reads=['BT', 'ident'], writes=[('ps', 5)])
                s.op('act', lambda e: e.activation(out=Btok, in_=pB, func=AF.Copy), reads=[('ps', 5)], writes=['Btok'])
                for h in range(4):
                    hs = slice(h * 64, (h + 1) * 64)
                    dcolm = dt_sb[:, ch * 4 + h:ch * 4 + h + 1]
                    s.op('act', lambda e, hs=hs, dcolm=dcolm: e.activation(out=xdt[:, hs], in_=pX[:, hs], func=AF.Copy, scale=dcolm),
                         reads=[('ps', 5), 'dt'], writes=['xdt'])
                    s.op('act', lambda e, hs=hs, h=h: e.activation(out=xdd[:, hs], in_=pX[:, hs], func=AF.Copy, scale=dtdec[:, h:h + 1]),
                         reads=[('ps', 5), 'dtdec'], writes=['xdd'])
                py = m.bank(6)[:, 0:256].rearrange("p (c l) -> p c l", c=2)
                for h in range(4):
                    tg = targ[h % 2]
                    mt = MTb[h % 2]
                    cp = CpT[h % 2]
                    s.op('dve', lambda e, tg=tg, h=h: e.scalar_tensor_tensor(out=tg, in0=prow[:, h, :], scalar=nacol[:, h:h + 1], in1=zeros_m,
                                                                           op0=ALU.add, op1=ALU.min),
                         reads=[('ps', 4), 'nacol', 'zeros_m'], writes=[('targ', h % 2)])
                    s.op('act', lambda e, tg=tg: e.activation(out=tg, in_=tg, func=AF.Exp), reads=[('targ', h % 2)], writes=[('targ', h % 2)])
                    s.op('dve', lambda e, tg=tg, mt=mt: e.tensor_tensor(out=mt, in0=tg, in1=MG, op=ALU.mult),
                         reads=[('targ', h % 2), 'MG'], writes=[('MT', h % 2)])
                    s.op('pool', lambda e, cp=cp, h=h, csl=csl: e.tensor_tensor(out=cp, in0=CTf[:, csl], in1=eacs[:, h, :], op=ALU.mult),
                         reads=['CTf', 'eacs'], writes=[('CpT', h % 2)])
                    po = py[(h % 2) * 64:(h % 2) * 64 + 64, h // 2, :]
                    s.op('pe', lambda e, po=po, h=h, mt=mt: e.matmul(po, lhsT=xdt[:, h * 64:(h + 1) * 64], rhs=mt, start=True, stop=False),
                         reads=['xdt', ('MT', h % 2)], writes=[('ps', 6)])
                    s.op('pe', lambda e, po=po, h=h, cp=cp: e.matmul(po, lhsT=S_b[:, h * 64:(h + 1) * 64], rhs=cp, start=False, stop=True),
                         reads=['Sb', ('CpT', h % 2)], writes=[('ps', 6)])
                pS = m.bank(7)[:, 0:256]
                s.op('pe', lambda e, pS=pS: e.matmul(pS, lhsT=Btok, rhs=xdd, start=True, stop=True), reads=['Btok', 'xdd'], writes=[('ps', 7)])
                for h in range(4):
                    hs = slice(h * 64, (h + 1) * 64)
                    s.op('dve', lambda e, hs=hs, h=h, pS=pS: e.scalar_tensor_tensor(out=S_f[:, hs], in0=S_f[:, hs], scalar=cdec[:, h:h + 1], in1=pS[:, hs],
                                                                               op0=ALU.mult, op1=ALU.add), reads=['Sf', 'cdec', ('ps', 7)], writes=['Sf'])
                s.op('act', lambda e: e.activation(out=S_b, in_=S_f, func=AF.Copy), reads=['Sf'], writes=['Sb'])
                yo = yout[ti % 2]
                for c in range(2):
                    s.op('dve', lambda e, c=c, csl=csl: e.scalar_tensor_tensor(out=ytmp[:, c, :], in0=xT[:, c, csl], scalar=dcol[:, c:c + 1], in1=py[:, c, :],
                                                                              op0=ALU.mult, op1=ALU.add), reads=[('xT', c), 'dcol', ('ps', 6)], writes=[('ytmp', c)])
                    s.op('pool', lambda e, c=c, csl=csl, yo=yo: e.tensor_tensor(out=yo[:, c, csl], in0=ytmp[:, c, :], in1=sz[:, c, csl], op=ALU.mult),
                         reads=[('ytmp', c), ('sz', c)], writes=[('yout', ti % 2)])
            for c in range(2):
                s.dma('sp', y_d[c * 128:(c + 1) * 128, ti * 512:(ti + 1) * 512], yout[ti % 2][:, c, :], reads=[('yout', ti % 2)], is_output=True)
        print("SSD ops", s.n_ops)
        s.emit()
    return nc


def build_C(nbr=3):
    nc = new_nc()
    hin = nc.dram_tensor("hin", [D, TL], F32, kind="ExternalInput").ap()
    uin = nc.dram_tensor("uin", [D, TL], BF16, kind="ExternalInput").ap()
    br_d = [nc.dram_tensor(n, [D, TL], BF16, kind="ExternalInput").ap() for n in ("ys", "yg", "ya")]
    modT = nc.dram_tensor("modT", [128, 144], F32, kind="ExternalInput").ap()
    gains = nc.dram_tensor("gains", [128, 2 * NK], F32, kind="ExternalInput").ap()
    wo_d = [nc.dram_tensor(n, [D, D], F32, kind="ExternalInput").ap() for n in ("wos", "wog", "woa")]
    wgt = nc.dram_tensor("wgt", [D, 3 * D], F32, kind="ExternalInput").ap()
    wout = nc.dram_tensor("wout", [D, D], F32, kind="ExternalInput").ap()
    wg = nc.dram_tensor("wg", [D, DFF], F32, kind="ExternalInput").ap()
    wu = nc.dram_tensor("wu", [D, DFF], F32, kind="ExternalInput").ap()
    wd = nc.dram_tensor("wd", [DFF, D], F32, kind="ExternalInput").ap()
    hout = nc.dram_tensor("hout", [D, TL], F32, kind="ExternalOutput").ap()
    with ExitStack() as st:
        m = Mem(nc, st, sbuf_bytes=207 * 1024)
        s = Sched(nc)
        ones = setup_consts(s, m)
        hT = m.alloc(NK * TL, F32).rearrange("p (k t) -> p k t", k=NK)
        mod = m.alloc(144, F32)
        gn = m.alloc(2 * NK, F32)
        gs = m.alloc(NK, F32)
        gh = m.alloc(NK, F32)
        for k in range(NK):
            s.dma('sp', hT[:, k, :], hin[k * 128:(k + 1) * 128, :], writes=[('hT', k, 0), ('hT', k, 1)])
        s.dma('sp', mod, modT, writes=['mod'])
        s.dma('sp', gn, gains, writes=['gn'])
        s.op('dve', lambda e: e.scalar_tensor_tensor(out=gs, in0=mod[:, 112:128], scalar=1.0, in1=gn[:, NK:2 * NK],
                                                     op0=ALU.add, op1=ALU.mult), reads=['mod', 'gn'], writes=['u3gs'])
        s.op('dve', lambda e: e.tensor_scalar(out=gh, in0=mod[:, 128:144], scalar1=0.5, scalar2=None, op0=ALU.mult),
             reads=['mod'], writes=['ghalf'])
        m.push()
        brs = [m.alloc(NK * 512, BF16).rearrange("p (k t) -> p k t", k=NK) for _ in range(3)]
        ut = m.alloc(NK * 512, BF16).rearrange("p (k t) -> p k t", k=NK)
        mix = m.alloc(NK * 512, BF16).rearrange("p (k t) -> p k t", k=NK)
        wob = [[m.alloc(NK * 128, BF16).rearrange("p (k c) -> p k c", k=NK) for _ in range(6)] for _ in range(1)]
        wtb = [m.alloc(NK * 128, BF16).rearrange("p (k c) -> p k c", k=NK) for _ in range(2)]
        sq = [m.alloc(512, F32) for _ in range(2)]
        rstd = m.alloc(512, F32)
        tmp = [m.alloc(512, F32) for _ in range(2)]
        sig = m.alloc(512, F32)
        macc = m.alloc(512, F32)
        it = 0
        for t in range(2):
            tsl = slice(t * 512, (t + 1) * 512)
            for b in range(3):
                s.dma('sp', brs[b], br_d[b][:, tsl].rearrange("(k p) t -> p k t", p=128), writes=[('br', b)])
            s.dma('sp', ut, uin[:, tsl].rearrange("(k p) t -> p k t", p=128), writes=['ut'])
            for g in range(4):
                ps = m.bank(7)
                for kk in range(4):
                    k = g * 4 + kk
                    s.op('act', lambda e, k=k, kk=kk: e.activation(out=sq[kk % 2], in_=brs[0][:, k, :], func=AF.Square),
                         reads=[('br', 0)], writes=[('sq', kk % 2)])
                    s.op('pe', lambda e, ps=ps, kk=kk: e.matmul(ps, lhsT=ones, rhs=sq[kk % 2], start=(kk == 0), stop=(kk == 3)),
                         reads=[('sq', kk % 2), 'ones'], writes=[('ps', 7)])
                s.op('act', lambda e, ps=ps: e.activation(out=rstd, in_=ps, func=AF.Sqrt, scale=1.0 / 512, bias=EPSB[0]),
                     reads=[('ps', 7), 'epsb'], writes=['rstd'])
                s.op('dve', lambda e: e.reciprocal(out=rstd, in_=rstd), reads=['rstd'], writes=['rstd'])
                for kk in range(4):
                    k = g * 4 + kk
                    s.op('dve', lambda e, k=k, kk=kk: e.tensor_tensor(out=tmp[kk % 2], in0=brs[0][:, k, :], in1=rstd, op=ALU.mult),
                         reads=[('br', 0), 'rstd'], writes=[('tmp', kk % 2)])
                    s.op('act', lambda e, k=k, kk=kk: e.activation(out=brs[0][:, k, :], in_=tmp[kk % 2], func=AF.Copy, scale=gn[:, k:k + 1]),
                         reads=[('tmp', kk % 2), 'gn'], writes=[('br', 0)])
            for o in range(NK):
                wb6 = wob[0]
                wr = ('wo', 0)
                it += 1
                osl = slice(o * 128, (o + 1) * 128)
                for b in range(3):
                    s.dma('pool', wb6[b], wo_d[b][:, osl].rearrange("(k p) c -> p k c", p=128), writes=[wr])
                    s.dma('pool', wb6[3 + b], wgt[:, b * D + o * 128:b * D + (o + 1) * 128].rearrange("(k p) c -> p k c", p=128), writes=[wr])
                for b in range(3):
                    py = m.bank(b % 2)
                    pg = m.bank(2 + b % 2)
                    for k in range(NK):
                        s.op('pe', lambda e, py=py, b=b, k=k, wb6=wb6: e.matmul(py, lhsT=wb6[b][:, k, :], rhs=brs[b][:, k, :], start=(k == 0), stop=(k == NK - 1)),
                             reads=[wr, ('br', b)], writes=[('ps', b % 2)])
                    for k in range(NK):
                        s.op('pe', lambda e, pg=pg, b=b, k=k, wb6=wb6: e.matmul(pg, lhsT=wb6[3 + b][:, k, :], rhs=ut[:, k, :], start=(k == 0), stop=(k == NK - 1)),
                             reads=[wr, 'ut'], writes=[('ps', 2 + b % 2)])
                    s.op('act', lambda e, pg=pg: e.activation(out=sig, in_=pg, func=AF.Sigmoid), reads=[('ps', 2 + b % 2)], writes=['sig'])
                    if b == 0:
                        s.op('dve', lambda e, py=py: e.tensor_tensor(out=macc, in0=sig, in1=py, op=ALU.mult), reads=['sig', ('ps', b % 2)], writes=['macc'])
                    else:
                        s.op('dve', lambda e, py=py: e.tensor_tensor(out=sig, in0=sig, in1=py, op=ALU.mult), reads=['sig', ('ps', b % 2)], writes=['sig'])
                        if b == 1:
                            s.op('dve', lambda e: e.tensor_tensor(out=macc, in0=macc, in1=sig, op=ALU.add), reads=['sig', 'macc'], writes=['macc'])
                        else:
                            s.op('dve', lambda e, o=o: e.tensor_tensor(out=mix[:, o, :], in0=macc, in1=sig, op=ALU.add), reads=['sig', 'macc'], writes=[('mix', o)])
            for o2 in range(NK):
                wt = wtb[o2 % 2]
                s.dma('pool', wt, wout[:, o2 * 128:(o2 + 1) * 128].rearrange("(k p) c -> p k c", p=128), writes=[('wt', o2 % 2)])
                pw = m.bank(4 + o2 % 2)
                for k in range(NK):
                    s.op('pe', lambda e, pw=pw, k=k, wt=wt: e.matmul(pw, lhsT=wt[:, k, :], rhs=mix[:, k, :], start=(k == 0), stop=(k == NK - 1)),
                         reads=[('wt', o2 % 2), ('mix', k)], writes=[('ps', 4 + o2 % 2)])
                s.op('dve', lambda e, pw=pw, o2=o2, tsl=tsl: e.scalar_tensor_tensor(out=hT[:, o2, tsl], in0=pw, scalar=mod[:, 80 + o2:81 + o2], in1=hT[:, o2, tsl],
                                                                                  op0=ALU.mult, op1=ALU.add),
                     reads=[('ps', 4 + o2 % 2), 'mod', ('hT', o2, t)], writes=[('hT', o2, t)])
        s.barrier()
        m.pop()
        uT = m.alloc(NK * TL, BF16).rearrange("p (k t) -> p k t", k=NK)
        tmpb = ([m.alloc(512, F32) for _ in range(2)], m.alloc(512, F32), [m.alloc(512, F32) for _ in range(2)])
        bufs = ([m.alloc(NK * 256, BF16) for _ in range(2)], [m.alloc(NK * 256, BF16) for _ in range(2)],
                [m.alloc(2 * D, BF16) for _ in range(2)], [m.alloc(2 * TL, BF16) for _ in range(2)],
                [m.alloc(512, F32) for _ in range(2)])
        rms_modulate(s, m, hT, uT, gs, mod[:, 96:112], tmpb, 'u3gs', ones, 7)
        ffn(s, m, hT, uT, 'uT', wg, wu, wd, gh, bufs)
        for k in range(NK):
            s.dma('sp', hout[k * 128:(k + 1) * 128, :], hT[:, k, :], reads=[('hT', k, 0), ('hT', k, 1)], is_output=True)
        print("C ops", s.n_ops)
        s.emit()
    return nc


NEGM = -30000.0


def build_attn(T=8192):
    nc = new_nc()
    NB = T // 256
    NT = T // 512
    TW = T + 128
    WV = TW + 127
    uT_d = nc.dram_tensor("uT", [D, T], BF16, kind="ExternalInput").ap()
    w_d = nc.dram_tensor("w", [2, D, 384], F32, kind="ExternalInput").ap()
    gn_d = nc.dram_tensor("gn", [128, 2], F32, kind="ExternalInput").ap()
    rb_d = nc.dram_tensor("rb", [32, 2], F32, kind="ExternalInput").ap()
    oh_d = nc.dram_tensor("oh", [32, T], F32, kind="ExternalInput").ap()
    wv_d = nc.dram_tensor("wvec", [2, WV + 1], BF16, kind="Internal").ap()
    o_d = nc.dram_tensor("o", [256, T], BF16, kind="ExternalOutput").ap()
    with ExitStack() as st:
        m = Mem(nc, st, sbuf_bytes=207 * 1024)
        s = Sched(nc)
        ones = setup_consts(s, m)
        ones_m, tri, ident = make_tri_ident(s, m)
        identb = m.alloc(128, BF16)
        s.op('pool', lambda e: e.tensor_copy(out=identb, in_=ident), reads=['ident'], writes=['identb'])
        gn = m.alloc(2, F32)
        s.dma('sp', gn, gn_d, writes=['gn'])
        gq = m.alloc(1, F32)
        s.op('dve', lambda e: e.tensor_scalar(out=gq, in0=gn[:, 0:1], scalar1=128.0 ** -0.5, scalar2=None, op0=ALU.mult), reads=['gn'], writes=['gq'])
        rb = m.alloc(2, F32, parts=32)
        s.dma('sp', rb, rb_d, writes=['rb'])
        m.push()
        oh = m.alloc(T, F32, parts=32)
        s.dma('sp', oh, oh_d, writes=['oh'])
        wrow = m.alloc(WV + 1, BF16, parts=2)
        s.op('dve', lambda e: e.memset(wrow, NEGM), writes=['wrow'])
        for cch in range(T // 512):
            pb = m.bank(5 + cch % 2)
            s.op('pe', lambda e, pb=pb, cch=cch: e.matmul(pb[0:2, :], lhsT=rb, rhs=oh[:, cch * 512:(cch + 1) * 512], start=True, stop=True),
                 reads=['rb', 'oh'], writes=[('ps', 5 + cch % 2)])
            s.op('act', lambda e, pb=pb, cch=cch: e.activation(out=wrow[:, 255 + cch * 512:255 + (cch + 1) * 512], in_=pb[0:2, :], func=AF.Copy),
                 reads=[('ps', 5 + cch % 2)], writes=['wrow'])
        s.dma('sp', wv_d, wrow, reads=['wrow'], writes=['wvd'])
        s.barrier()
        m.pop()
        Tbig = m.alloc(TW, BF16)
        QT = m.alloc(T, BF16)
        KT = m.alloc(T, BF16)
        Va = m.alloc((T // 128) * 130, BF16).rearrange("p (c d) -> p c d", d=130)
        sel = m.alloc((T // 128) * 32, F32).rearrange("p (c n) -> p c n", n=32)
        negm = m.alloc(32 * 32, F32).rearrange("p (b n) -> p b n", b=32)
        zer = m.alloc(32 * 32, F32)
        s.op('pool', lambda e: e.memset(zer, 0.0), writes=['zer'])
        s.op('pool', lambda e: e.affine_select(out=negm, in_=zer.rearrange("p (b n) -> p b n", b=32), pattern=[[1, 32], [-1, 32]], compare_op=ALU.is_ge,
                                               fill=-1e30, base=-1, channel_multiplier=0), reads=['zer'], writes=['negm'])
        ones32 = ones_m[:, 0:32]
        wb = m.alloc(NK * 384, BF16).rearrange("p (k c) -> p k c", k=NK)
        ub = [m.alloc(NK * 512, BF16).rearrange("p (k t) -> p k t", k=NK) for _ in range(2)]
        xf = [m.alloc(512, F32) for _ in range(2)]
        sq = m.alloc(512, F32)
        rstd = m.alloc(512, F32)
        kn = m.alloc(512, F32)
        qn = m.alloc(512, F32)
        kmean = m.alloc(32, F32)
        s.op('pool', lambda e: e.memset(kmean, 0.0), writes=['kmean'])
        gsb = m.alloc(32, F32)
        m8 = m.alloc(8, F32)
        thr = m.alloc(1, F32)
        pT = [m.alloc(256, BF16) for _ in range(2)]
        acc = [m.alloc(130, F32) for _ in range(2)]
        rec = m.alloc(1, F32)
        of = m.alloc(128, F32)
        oT = [m.alloc(256, BF16) for _ in range(2)]
        for h in range(2):
            s.dma('pool', wb, w_d[h].rearrange("(k p) c -> p k c", p=128), writes=['wb'])
            for i in range(128):
                s.dma('sp' if i % 2 == 0 else 'act', Tbig[i:i + 1, :], wv_d[h:h + 1, 127 - i:127 - i + TW], reads=['wvd'], writes=['Tbig'])
            s.op('pool', lambda e: e.memset(Va[:, :, 128:130], 1.0), writes=['Va'])
            for ti in range(NT):
                u = ub[ti % 2]
                ur = ('u', ti % 2)
                tsl = slice(ti * 512, (ti + 1) * 512)
                s.dma('sp', u, uT_d[:, tsl].rearrange("(k p) t -> p k t", p=128), writes=[ur])
                for which in (1, 0):
                    ps = m.bank(5)
                    for k in range(NK):
                        s.op('pe', lambda e, ps=ps, k=k, u=u, which=which: e.matmul(ps, lhsT=wb[:, k, which * 128:(which + 1) * 128], rhs=u[:, k, :],
                                                                                 start=(k == 0), stop=(k == NK - 1)), reads=['wb', ur], writes=[('ps', 5)])
                    x = xf[which]
                    s.op('act', lambda e, x=x, ps=ps: e.activation(out=x, in_=ps, func=AF.Copy), reads=[('ps', 5)], writes=[('xf', which)])
                    s.op('act', lambda e, x=x: e.activation(out=sq, in_=x, func=AF.Square), reads=[('xf', which)], writes=['sq'])
                    p2 = m.bank(6)
                    s.op('pe', lambda e, p2=p2: e.matmul(p2, lhsT=ones, rhs=sq, start=True, stop=True), reads=['sq', 'ones'], writes=[('ps', 6)])
                    s.op('act', lambda e, p2=p2: e.activation(out=rstd, in_=p2, func=AF.Sqrt, scale=1.0 / 128, bias=EPSB[0]), reads=[('ps', 6), 'epsb'], writes=['rstd'])
                    s.op('dve', lambda e: e.reciprocal(out=rstd, in_=rstd), reads=['rstd'], writes=['rstd'])
                    s.op('dve', lambda e, x=x: e.tensor_tensor(out=x, in0=x, in1=rstd, op=ALU.mult), reads=[('xf', which), 'rstd'], writes=[('xf', which)])
                    if which == 1:
                        s.op('act', lambda e, x=x: e.activation(out=kn, in_=x, func=AF.Copy, scale=gn[:, 1:2]), reads=[('xf', 1), 'gn'], writes=['kn'])
                        s.op('pool', lambda e, tsl=tsl: e.tensor_copy(out=KT[:, tsl], in_=kn), reads=['kn'], writes=['KT'])
                        for bb in range(2):
                            blk = ti * 2 + bb
                            s.op('dve', lambda e, blk=blk, bb=bb: e.tensor_reduce(out=kmean[:, blk:blk + 1], in_=kn[:, bb * 256:(bb + 1) * 256], axis=AX.X, op=ALU.add),
                                 reads=['kn'], writes=['kmean'])
                    else:
                        s.op('act', lambda e, x=x: e.activation(out=qn, in_=x, func=AF.Copy, scale=gn[:, 0:1]), reads=[('xf', 0), 'gn'], writes=['qn'])
                        s.op('act', lambda e, x=x, tsl=tsl: e.activation(out=QT[:, tsl], in_=x, func=AF.Copy, scale=gq), reads=[('xf', 0), 'gq'], writes=['QT'])
                        for qq in range(4):
                            qt = ti * 4 + qq
                            b = qt // 2
                            pg = m.bank(7)[:, 0:32]
                            s.op('pe', lambda e, pg=pg, qq=qq: e.matmul(pg, lhsT=qn[:, qq * 128:(qq + 1) * 128], rhs=kmean, start=True, stop=True),
                                 reads=['qn', 'kmean'], writes=[('ps', 7)])
                            s.op('dve', lambda e, pg=pg, b=b: e.tensor_tensor(out=gsb, in0=pg, in1=negm[:, b, :], op=ALU.add), reads=[('ps', 7), 'negm'], writes=['gsb'])
                            s.op('dve', lambda e: e.max(out=m8, in_=gsb), reads=['gsb'], writes=['m8'])
                            s.op('dve', lambda e: e.tensor_scalar(out=thr, in0=m8[:, 2:3], scalar1=-1e29, scalar2=None, op0=ALU.max), reads=['m8'], writes=['thr'])
                            s.op('dve', lambda e, qt=qt: e.scalar_tensor_tensor(out=sel[:, qt, :], in0=gsb, scalar=thr, in1=ones32, op0=ALU.is_ge, op1=ALU.mult),
                                 reads=['gsb', 'thr', 'ones_m'], writes=['sel'])
                for cc in range(4):
                    ck = ti * 4 + cc
                    pv = m.bank(7)[:, 128:256]
                    for k in range(NK):
                        s.op('pe', lambda e, pv=pv, k=k, u=u, cc=cc: e.matmul(pv, lhsT=u[:, k, cc * 128:(cc + 1) * 128], rhs=wb[:, k, 256:384],
                                                                           start=(k == 0), stop=(k == NK - 1)), reads=['wb', ur], writes=[('ps', 7)])
                    s.op('act', lambda e, pv=pv, ck=ck: e.activation(out=Va[:, ck, 0:128], in_=pv, func=AF.Copy), reads=[('ps', 7)], writes=['Va'])
            it = 0
            for b in range(NB):
                qsl = slice(b * 256, (b + 1) * 256)
                for qt2 in range(2):
                    s.op('pool', lambda e, qt2=qt2: e.memset(acc[qt2], 0.0), writes=[('acc', qt2)])
                for n in range(b + 1):
                    pOb = [(2, 3)[it % 2], (5, 6)[it % 2]]
                    for kh in range(2):
                        kt = 2 * n + kh
                        D0 = b * 256 - kt * 128 + 128
                        pS = m.bank(it % 2)[:, 0:256] if kh == 0 else m.bank(it % 2)[:, 256:512]
                        sr = ('ps', it % 2)
                        s.op('pe', lambda e, pS=pS, kt=kt, qsl=qsl: e.matmul(pS, lhsT=KT[:, kt * 128:(kt + 1) * 128], rhs=QT[:, qsl], start=True, stop=False),
                             reads=['KT', 'QT'], writes=[sr])
                        s.op('pe', lambda e, pS=pS, D0=D0: e.matmul(pS, lhsT=identb, rhs=Tbig[:, D0:D0 + 256], start=False, stop=True),
                             reads=['identb', 'Tbig'], writes=[sr])
                        p = pT[kh]
                        s.op('act', lambda e, p=p, pS=pS: e.activation(out=p, in_=pS, func=AF.Exp), reads=[sr], writes=[('pT', kh)])
                        for qt2 in range(2):
                            po = m.bank(pOb[qt2])[:, 0:129]
                            s.op('pe', lambda e, po=po, p=p, qt2=qt2, kt=kt, kh=kh: e.matmul(po, lhsT=p[:, qt2 * 128:(qt2 + 1) * 128], rhs=Va[:, kt, 0:129],
                                                                                          start=(kh == 0), stop=(kh == 1)),
                                 reads=[('pT', kh), 'Va'], writes=[('ps', pOb[qt2])])
                    for qt2 in range(2):
                        po = m.bank(pOb[qt2])[:, 0:129]
                        a = acc[qt2][:, 0:129]
                        if n == b:
                            s.op('dve', lambda e, a=a, po=po: e.tensor_tensor(out=a, in0=a, in1=po, op=ALU.add), reads=[('acc', qt2), ('ps', pOb[qt2])], writes=[('acc', qt2)])
                        else:
                            s.op('dve', lambda e, a=a, po=po, qt2=qt2, b=b, n=n: e.scalar_tensor_tensor(out=a, in0=po, scalar=sel[:, b * 2 + qt2, n:n + 1], in1=a,
                                                                                                   op0=ALU.mult, op1=ALU.add),
                                 reads=[('acc', qt2), ('ps', pOb[qt2]), 'sel'], writes=[('acc', qt2)])
                    it += 1
                ot = oT[b % 2]
                for qt2 in range(2):
                    s.op('dve', lambda e, qt2=qt2: e.reciprocal(out=rec, in_=acc[qt2][:, 128:129]), reads=[('acc', qt2)], writes=['rec'])
                    s.op('act', lambda e, qt2=qt2: e.activation(out=of, in_=acc[qt2][:, 0:128], func=AF.Copy, scale=rec), reads=[('acc', qt2), 'rec'], writes=['of'])
                    pt = m.bank(4)[:, 0:128]
                    s.op('pe', lambda e, pt=pt: e.transpose(out=pt, in_=of, identity=ident), reads=['of', 'ident'], writes=[('ps', 4)])
                    s.op('act', lambda e, pt=pt, ot=ot, qt2=qt2: e.activation(out=ot[:, qt2 * 128:(qt2 + 1) * 128], in_=pt, func=AF.Copy), reads=[('ps', 4)], writes=[('oT', b % 2)])
                s.dma('sp', o_d[h * 128:(h + 1) * 128, qsl], ot, reads=[('oT', b % 2)], is_output=True)
            s.barrier()
        print("ATTN ops", s.n_ops)
        s.emit()
    return nc


def build_gdn(T=8192):
    nc = new_nc()
    NT = T // 512
    uT_d = nc.dram_tensor("uT", [D, T], BF16, kind="ExternalInput").ap()
    w_d = nc.dram_tensor("w", [2, D, 514], F32, kind="ExternalInput").ap()
    cw_d = nc.dram_tensor("cw", [2, 128, 3, 4], F32, kind="ExternalInput").ap()
    hp_d = nc.dram_tensor("hp", [2, 128, 2], F32, kind="ExternalInput").ap()
    gn_d = nc.dram_tensor("gn", [128, 1], F32, kind="ExternalInput").ap()
    o_d = nc.dram_tensor("o", [256, T], BF16, kind="ExternalOutput").ap()
    with ExitStack() as st:
        m = Mem(nc, st)
        s = Sched(nc)
        ones = setup_consts(s, m)
        ones_m, tri, ident = make_tri_ident(s, m)
        onec = ones_m[:, 0:1]
        zeros_m = m.alloc(128, F32)
        s.op('pool', lambda e: e.memset(zeros_m, 0.0), writes=['zeros_m'])
        bd = m.alloc(128, F32)
        s.op('pool', lambda e: e.memset(bd, 0.0), writes=['bd'])
        s.op('pool', lambda e: e.memset(bd[0:64, 0:64], 1.0), reads=['bd'], writes=['bd'])
        s.op('pool', lambda e: e.memset(bd[64:128, 64:128], 1.0), reads=['bd'], writes=['bd'])
        mIU = m.alloc(128, F32)
        mSU = m.alloc(128, F32)
        s.op('pool', lambda e: e.tensor_tensor(out=mIU, in0=tri, in1=bd, op=ALU.mult), reads=['tri', 'bd'], writes=['mIU'])
        s.op('pool', lambda e: e.tensor_tensor(out=mSU, in0=mIU, in1=ident, op=ALU.subtract), reads=['mIU', 'ident'], writes=['mSU'])
        gn = m.alloc(1, F32)
        s.dma('sp', gn, gn_d, writes=['gn'])
        wb = m.alloc(NK * 514, BF16).rearrange("p (k c) -> p k c", k=NK)
        cw = m.alloc(12, F32).rearrange("p (c k) -> p c k", c=3)
        hp = m.alloc(2, F32)
        negA = m.alloc(1, F32)
        ub = [m.alloc(NK * 512, BF16).rearrange("p (k t) -> p k t", k=NK) for _ in range(2)]
        pre = [m.alloc(515, F32) for _ in range(3)]
        acc = [m.alloc(512, F32) for _ in range(2)]
        qc = m.alloc(512, F32)
        kc = m.alloc(512, F32)
        vcf = m.alloc(512, F32)
        sz = m.alloc(512, F32)
        sq = m.alloc(512, F32)
        rstd = m.alloc(512, F32)
        ofull = m.alloc(512, F32)
        yout = [m.alloc(512, BF16) for _ in range(2)]
        c1 = lambda: m.alloc(1, F32)
        ab, et, spv, gcol, eb, beta, nbeta, gccol, glcol, ecol, bg, kdsc, egl0, egl1, tdiff = [c1() for _ in range(15)]
        ab = m.alloc(2, F32)
        ngccol = m.alloc(1, F32)
        grep = m.alloc(128, F32)
        egcrow = m.alloc(128, F32)
        tg = m.alloc(128, F32)
        ET = m.alloc(128, F32)
        NT0 = m.alloc(128, F32)
        qkT = m.alloc(128, F32)
        X = [m.alloc(128, F32) for _ in range(2)]
        XT = [m.alloc(128, F32) for _ in range(2)]
        PT = [m.alloc(128, F32) for _ in range(2)]
        kbg = m.alloc(128, F32)
        kdec = m.alloc(128, F32)
        vb = m.alloc(128, F32)
        u_sb = m.alloc(128, F32)
        wT = m.alloc(128, F32)
        qdT = m.alloc(128, F32)
        vc_sb = m.alloc(128, F32)
        S = m.alloc(128, F32)
        for h in range(2):
            s.dma('pool', wb, w_d[h].rearrange("(k p) c -> p k c", p=128), writes=['wb'])
            s.dma('sp', cw, cw_d[h], writes=['cw'])
            s.dma('sp', hp, hp_d[h], writes=['hp'])
            s.op('act', lambda e: e.activation(out=negA, in_=hp[:, 1:2], func=AF.Exp), reads=['hp'], writes=['negA'])
            s.op('dve', lambda e: e.tensor_scalar(out=negA, in0=negA, scalar1=-1.0, scalar2=None, op0=ALU.mult), reads=['negA'], writes=['negA'])
            s.op('pool', lambda e: e.memset(S, 0.0), writes=['S'])
            for c in range(3):
                s.op('pool', lambda e, c=c: e.memset(pre[c][:, 0:3], 0.0), writes=[('pre', c)])
            for ti in range(NT):
                u = ub[ti % 2]
                ur = ('u', ti % 2)
                s.dma('sp', u, uT_d[:, ti * 512:(ti + 1) * 512].rearrange("(k p) t -> p k t", p=128), writes=[ur])
                dst = [qc, kc, vcf]
                dname = ['qc', 'kc', 'vc']
                for g in range(4):
                    ps = m.bank(g % 2)
                    for k in range(NK):
                        s.op('pe', lambda e, ps=ps, g=g, k=k, u=u: e.matmul(ps, lhsT=wb[:, k, g * 128:(g + 1) * 128], rhs=u[:, k, :],
                                                                         start=(k == 0), stop=(k == NK - 1)), reads=['wb', ur], writes=[('ps', g % 2)])
                    if g == 3:
                        s.op('act', lambda e, ps=ps: e.activation(out=sz, in_=ps, func=AF.Silu), reads=[('ps', g % 2)], writes=['sz'])
                        continue
                    c = g
                    s.op('act', lambda e, ps=ps, c=c: e.activation(out=pre[c][:, 3:515], in_=ps, func=AF.Copy), reads=[('ps', g % 2)], writes=[('pre', c)])
                    a = acc[c % 2]
                    s.op('dve', lambda e, a=a, c=c: e.tensor_scalar(out=a, in0=pre[c][:, 0:512], scalar1=cw[:, c, 0:1], scalar2=zeros_m[:, 0:1],
                                                                    op0=ALU.mult, op1=ALU.add), reads=[('pre', c), 'cw', 'zeros_m'], writes=[('acc', c % 2)])
                    for kk in range(1, 4):
                        s.op('dve', lambda e, a=a, c=c, kk=kk: e.scalar_tensor_tensor(out=a, in0=pre[c][:, kk:kk + 512], scalar=cw[:, c, kk:kk + 1],
                                                                                      in1=a, op0=ALU.mult, op1=ALU.add),
                             reads=[('pre', c), 'cw', ('acc', c % 2)], writes=[('acc', c % 2)])
                    s.op('pool', lambda e, c=c: e.tensor_copy(out=pre[c][:, 0:3], in_=pre[c][:, 512:515]), reads=[('pre', c)], writes=[('pre', c)])
                    s.op('act', lambda e, a=a, c=c: e.activation(out=dst[c], in_=a, func=AF.Silu), reads=[('acc', c % 2)], writes=[dname[c]])
                    if c < 2:
                        s.op('act', lambda e, c=c: e.activation(out=sq, in_=dst[c], func=AF.Square), reads=[dname[c]], writes=['sq'])
                        p2 = m.bank(2)
                        s.op('pe', lambda e, p2=p2: e.matmul(p2, lhsT=ones, rhs=sq, start=True, stop=True), reads=['sq', 'ones'], writes=[('ps', 2)])
                        s.op('act', lambda e, p2=p2: e.activation(out=rstd, in_=p2, func=AF.Sqrt, scale=1.0, bias=EPSB[0]), reads=[('ps', 2), 'epsb'], writes=['rstd'])
                        s.op('dve', lambda e: e.reciprocal(out=rstd, in_=rstd), reads=['rstd'], writes=['rstd'])
                        if c == 0:
                            s.op('dve', lambda e: e.scalar_tensor_tensor(out=qc, in0=qc, scalar=128.0 ** -0.5, in1=rstd, op0=ALU.mult, op1=ALU.mult),
                                 reads=['qc', 'rstd'], writes=['qc'])
                        else:
                            s.op('dve', lambda e: e.tensor_tensor(out=kc, in0=kc, in1=rstd, op=ALU.mult), reads=['kc', 'rstd'], writes=['kc'])
                for sti in range(4):
                    cs = slice(sti * 128, (sti + 1) * 128)
                    pab = m.bank(2)[:, 0:2]
                    for k in range(NK):
                        s.op('pe', lambda e, pab=pab, k=k, u=u, cs=cs: e.matmul(pab, lhsT=u[:, k, cs], rhs=wb[:, k, 512:514], start=(k == 0), stop=(k == NK - 1)),
                             reads=['wb', ur], writes=[('ps', 2)])
                    s.op('act', lambda e, pab=pab: e.activation(out=ab, in_=pab, func=AF.Copy), reads=[('ps', 2)], writes=['ab'])
                    s.op('act', lambda e: e.activation(out=et, in_=ab[:, 0:1], func=AF.Exp, bias=hp[:, 0:1], scale=1.0), reads=['ab', 'hp'], writes=['et'])
                    s.op('act', lambda e: e.activation(out=spv, in_=et, func=AF.Ln, bias=onec, scale=1.0), reads=['et', 'ones_m'], writes=['spv'])
                    s.op('dve', lambda e: e.tensor_tensor(out=gcol, in0=spv, in1=negA, op=ALU.mult), reads=['spv', 'negA'], writes=['gcol'])
                    s.op('act', lambda e: e.activation(out=eb, in_=ab[:, 1:2], func=AF.Exp, scale=-1.0), reads=['ab'], writes=['eb'])
                    s.op('dve', lambda e: e.tensor_scalar(out=eb, in0=eb, scalar1=1.0, scalar2=None, op0=ALU.add), reads=['eb'], writes=['eb'])
                    s.op('dve', lambda e: e.reciprocal(out=beta, in_=eb), reads=['eb'], writes=['beta'])
                    s.op('pool', lambda e: e.tensor_scalar(out=grep, in0=ones_m, scalar1=gcol, scalar2=None, op0=ALU.mult), reads=['ones_m', 'gcol'], writes=['grep'])
                    prow = m.bank(3)[:, 0:128]
                    plrow = m.bank(3)[:, 128:256]
                    pcol = m.bank(2)[:, 8:9]
                    plcol = m.bank(2)[:, 16:17]
                    s.op('pe', lambda e, prow=prow: e.matmul(prow, lhsT=grep, rhs=mIU, start=True, stop=True), reads=['grep', 'mIU'], writes=[('ps', 3)])
                    s.op('pe', lambda e, plrow=plrow: e.matmul(plrow, lhsT=grep, rhs=bd, start=True, stop=True), reads=['grep', 'bd'], writes=[('ps', 3)])
                    s.op('pe', lambda e, pcol=pcol: e.matmul(pcol, lhsT=mIU, rhs=gcol, start=True, stop=True), reads=['gcol', 'mIU'], writes=[('ps', 2)])
                    s.op('pe', lambda e, plcol=plcol: e.matmul(plcol, lhsT=bd, rhs=gcol, start=True, stop=True), reads=['gcol', 'bd'], writes=[('ps', 2)])
                    s.op('act', lambda e, pcol=pcol: e.activation(out=gccol, in_=pcol, func=AF.Copy), reads=[('ps', 2)], writes=['gccol'])
                    s.op('act', lambda e, pcol=pcol: e.activation(out=ecol, in_=pcol, func=AF.Exp), reads=[('ps', 2)], writes=['ecol'])
                    s.op('dve', lambda e, plcol=plcol: e.tensor_tensor(out=tdiff, in0=plcol, in1=gccol, op=ALU.subtract), reads=[('ps', 2), 'gccol'], writes=['tdiff'])
                    s.op('act', lambda e: e.activation(out=kdsc, in_=tdiff, func=AF.Exp), reads=['tdiff'], writes=['kdsc'])
                    s.op('dve', lambda e: e.tensor_tensor(out=bg, in0=beta, in1=ecol, op=ALU.mult), reads=['beta', 'ecol'], writes=['bg'])
                    s.op('act', lambda e, plrow=plrow: e.activation(out=egl0, in_=plrow[:, 0:1], func=AF.Exp), reads=[('ps', 3)], writes=['egl0'])
                    s.op('act', lambda e, plrow=plrow: e.activation(out=egl1, in_=plrow[:, 64:65], func=AF.Exp), reads=[('ps', 3)], writes=['egl1'])
                    s.op('act', lambda e, prow=prow: e.activation(out=egcrow, in_=prow, func=AF.Exp), reads=[('ps', 3)], writes=['egcrow'])
                    s.op('dve', lambda e: e.tensor_scalar(out=ngccol, in0=gccol, scalar1=-1.0, scalar2=None, op0=ALU.mult), reads=['gccol'], writes=['ngccol'])
                    s.op('dve', lambda e, prow=prow: e.scalar_tensor_tensor(out=tg, in0=prow, scalar=ngccol, in1=zeros_m, op0=ALU.add, op1=ALU.min),
                         reads=[('ps', 3), 'ngccol', 'zeros_m'], writes=['tg'])
                    s.op('act', lambda e: e.activation(out=ET, in_=tg, func=AF.Exp), reads=['tg'], writes=['ET'])
                    pkk = m.bank(4)[:, 0:128]
                    pqk = m.bank(4)[:, 128:256]
                    s.op('pe', lambda e, pkk=pkk, cs=cs: e.matmul(pkk, lhsT=kc[:, cs], rhs=kc[:, cs], start=True, stop=True), reads=['kc'], writes=[('ps', 4)])
                    s.op('pe', lambda e, pqk=pqk, cs=cs: e.matmul(pqk, lhsT=kc[:, cs], rhs=qc[:, cs], start=True, stop=True), reads=['kc', 'qc'], writes=[('ps', 4)])
                    s.op('dve', lambda e, pkk=pkk: e.scalar_tensor_tensor(out=NT0, in0=pkk, scalar=-1.0, in1=ET, op0=ALU.mult, op1=ALU.mult),
                         reads=[('ps', 4), 'ET'], writes=['NT0'])
                    s.op('pool', lambda e: e.tensor_tensor(out=NT0, in0=NT0, in1=mSU, op=ALU.mult), reads=['NT0', 'mSU'], writes=['NT0'])
                    s.op('dve', lambda e, pqk=pqk: e.tensor_tensor(out=qkT, in0=pqk, in1=ET, op=ALU.mult), reads=[('ps', 4), 'ET'], writes=['qkT'])
                    s.op('pool', lambda e: e.tensor_tensor(out=qkT, in0=qkT, in1=mIU, op=ALU.mult), reads=['qkT', 'mIU'], writes=['qkT'])
                    pt = m.bank(5)[:, 0:128]
                    s.op('pe', lambda e, pt=pt: e.transpose(out=pt, in_=NT0, identity=ident), reads=['NT0', 'ident'], writes=[('ps', 5)])
                    s.op('act', lambda e, pt=pt: e.activation(out=X[0], in_=pt, func=AF.Copy, scale=beta), reads=[('ps', 5), 'beta'], writes=[('X', 0)])
                    pt2 = m.bank(6)[:, 0:128]
                    s.op('pe', lambda e, pt2=pt2: e.transpose(out=pt2, in_=X[0], identity=ident), reads=[('X', 0), 'ident'], writes=[('ps', 6)])
                    s.op('act', lambda e, pt2=pt2: e.activation(out=XT[0], in_=pt2, func=AF.Copy), reads=[('ps', 6)], writes=[('XT', 0)])
                    s.op('dve', lambda e: e.tensor_tensor(out=PT[0], in0=XT[0], in1=ident, op=ALU.add), reads=[('XT', 0), 'ident'], writes=[('PT', 0)])
                    cur = 0
                    for kk in range(1, 6):
                        nx = 1 - cur
                        pa = m.bank(5)[:, 0:128]
                        s.op('pe', lambda e, pa=pa, cur=cur: e.matmul(pa, lhsT=XT[cur], rhs=X[cur], start=True, stop=True),
                             reads=[('X', cur), ('XT', cur)], writes=[('ps', 5)])
                        s.op('act', lambda e, pa=pa, nx=nx: e.activation(out=X[nx], in_=pa, func=AF.Copy), reads=[('ps', 5)], writes=[('X', nx)])
                        if kk < 5:
                            pb_ = m.bank(6)[:, 0:128]
                            s.op('pe', lambda e, pb_=pb_, cur=cur: e.matmul(pb_, lhsT=X[cur], rhs=XT[cur], start=True, stop=True),
                                 reads=[('X', cur), ('XT', cur)], writes=[('ps', 6)])
                            s.op('dve', lambda e, pb_=pb_, nx=nx: e.tensor_copy(out=XT[nx], in_=pb_), reads=[('ps', 6)], writes=[('XT', nx)])
                        pc = m.bank(7)[:, 0:128]
                        s.op('pe', lambda e, pc=pc, nx=nx, cur=cur: e.matmul(pc, lhsT=X[nx], rhs=PT[cur], start=True, stop=True),
                             reads=[('X', nx), ('PT', cur)], writes=[('ps', 7)])
                        s.op('dve', lambda e, pc=pc, nx=nx, cur=cur: e.tensor_tensor(out=PT[nx], in0=pc, in1=PT[cur], op=ALU.add),
                             reads=[('ps', 7), ('PT', cur)], writes=[('PT', nx)])
                        cur = nx
                    PTf = PT[cur]
                    ptr = ('PT', cur)
                    pk = m.bank(5)[:, 128:256]
                    s.op('pe', lambda e, pk=pk, cs=cs: e.transpose(out=pk, in_=kc[:, cs], identity=ident), reads=['kc', 'ident'], writes=[('ps', 5)])
                    s.op('act', lambda e, pk=pk: e.activation(out=kbg, in_=pk, func=AF.Copy, scale=bg), reads=[('ps', 5), 'bg'], writes=['kbg'])
                    s.op('act', lambda e, pk=pk: e.activation(out=kdec, in_=pk, func=AF.Copy, scale=kdsc), reads=[('ps', 5), 'kdsc'], writes=['kdec'])
                    pv = m.bank(6)[:, 128:256]
                    s.op('pe', lambda e, pv=pv, cs=cs: e.transpose(out=pv, in_=vcf[:, cs], identity=ident), reads=['vc', 'ident'], writes=[('ps', 6)])
                    s.op('act', lambda e, pv=pv: e.activation(out=vb, in_=pv, func=AF.Copy, scale=beta), reads=[('ps', 6), 'beta'], writes=['vb'])
                    pu = m.bank(7)[:, 128:256]
                    s.op('pe', lambda e, pu=pu, PTf=PTf: e.matmul(pu, lhsT=PTf, rhs=vb, start=True, stop=True), reads=[ptr, 'vb'], writes=[('ps', 7)])
                    s.op('act', lambda e, pu=pu: e.activation(out=u_sb, in_=pu, func=AF.Copy), reads=[('ps', 7)], writes=['u_sb'])
                    pw = m.bank(7)[:, 256:384]
                    s.op('pe', lambda e, pw=pw, PTf=PTf: e.matmul(pw, lhsT=kbg, rhs=PTf, start=True, stop=True), reads=[ptr, 'kbg'], writes=[('ps', 7)])
                    s.op('dve', lambda e, pw=pw: e.tensor_copy(out=wT, in_=pw), reads=[('ps', 7)], writes=['wT'])
                    s.op('pool', lambda e, cs=cs: e.tensor_tensor(out=qdT, in0=qc[:, cs], in1=egcrow, op=ALU.mult), reads=['qc', 'egcrow'], writes=['qdT'])
                    po = m.bank(4)[:, 256:384]
                    for c in range(2):
                        rows = slice(c * 64, (c + 1) * 64)
                        pvs = m.bank(5)[rows, 256:384]
                        s.op('pe', lambda e, pvs=pvs, rows=rows: e.matmul(pvs, lhsT=wT[:, rows], rhs=S, start=True, stop=True), reads=['wT', 'S'], writes=[('ps', 5)])
                        s.op('dve', lambda e, pvs=pvs, rows=rows: e.tensor_tensor(out=vc_sb[rows, :], in0=u_sb[rows, :], in1=pvs, op=ALU.subtract),
                             reads=['u_sb', ('ps', 5)], writes=['vc_sb'])
                        s.op('pe', lambda e, po=po, rows=rows: e.matmul(po[:, rows], lhsT=S, rhs=qdT[:, rows], start=True, stop=False), reads=['S', 'qdT'], writes=[('ps', 4)])
                        s.op('pe', lambda e, po=po, rows=rows: e.matmul(po[:, rows], lhsT=vc_sb[rows, :], rhs=qkT[rows, rows], start=False, stop=True),
                             reads=['vc_sb', 'qkT'], writes=[('ps', 4)])
                        pks = m.bank(6)[:, 256:384]
                        s.op('pe', lambda e, pks=pks, rows=rows: e.matmul(pks, lhsT=kdec[rows, :], rhs=vc_sb[rows, :], start=True, stop=True),
                             reads=['kdec', 'vc_sb'], writes=[('ps', 6)])
                        eg = egl0 if c == 0 else egl1
                        s.op('dve', lambda e, pks=pks, eg=eg: e.scalar_tensor_tensor(out=S, in0=S, scalar=eg, in1=pks, op0=ALU.mult, op1=ALU.add),
                             reads=['S', 'egl0', 'egl1', ('ps', 6)], writes=['S'])
                    s.op('act', lambda e, po=po, cs=cs: e.activation(out=ofull[:, cs], in_=po, func=AF.Copy), reads=[('ps', 4)], writes=['ofull'])
                s.op('act', lambda e: e.activation(out=sq, in_=ofull, func=AF.Square), reads=['ofull'], writes=['sq'])
                p2 = m.bank(2)
                s.op('pe', lambda e, p2=p2: e.matmul(p2, lhsT=ones, rhs=sq, start=True, stop=True), reads=['sq', 'ones'], writes=[('ps', 2)])
                s.op('act', lambda e, p2=p2: e.activation(out=rstd, in_=p2, func=AF.Sqrt, scale=1.0 / 128, bias=EPSB[0]), reads=[('ps', 2), 'epsb'], writes=['rstd'])
                s.op('dve', lambda e: e.reciprocal(out=rstd, in_=rstd), reads=['rstd'], writes=['rstd'])
                s.op('dve', lambda e: e.scalar_tensor_tensor(out=ofull, in0=ofull, scalar=gn, in1=rstd, op0=ALU.mult, op1=ALU.mult), reads=['ofull', 'gn', 'rstd'], writes=['ofull'])
                yo = yout[ti % 2]
                s.op('pool', lambda e, yo=yo: e.tensor_tensor(out=yo, in0=ofull, in1=sz, op=ALU.mult), reads=['ofull', 'sz'], writes=[('yout', ti % 2)])
                s.dma('sp', o_d[h * 128:(h + 1) * 128, ti * 512:(ti + 1) * 512], yo, reads=[('yout', ti % 2)], is_output=True)
            s.barrier()
        print("GDN ops", s.n_ops)
        s.emit()
    return nc


def ssd_inputs(W, l, i, uT_full):
    g = i // 2
    w_in = W['w_in'][l]
    cols = np.concatenate([np.arange(256*i, 256*i+256), 2048 + np.arange(256*i, 256*i+256), 4096 + np.arange(128*g, 128*g+128),
                           4608 + np.arange(128*g, 128*g+128), 5120 + np.arange(4*i, 4*i+4)])
    w = np.ascontiguousarray(w_in[:, cols])
    chans = np.stack([256*i + np.arange(128), 256*i + 128 + np.arange(128), 2048 + 128*g + np.arange(128), 2560 + 128*g + np.arange(128)], axis=1)
    cwl = W['ssd_conv_w'][l]; cbl = W['ssd_conv_b'][l]
    cw = np.zeros((128, 4, 5), np.float32)
    for k in range(4):
        cw[:, :, k] = cwl[k][chans]
    cw[:, :, 4] = cbl[chans]
    hp = np.zeros((128, 3, 16), np.float32)
    hp[:, 0, :] = np.tile(W['ssd_dt_bias'][l][4*i:4*i+4], 4)[None, :]
    hp[:, 1, :] = np.tile(W['ssd_a_log'][l][4*i:4*i+4], 4)[None, :]
    dcol = np.zeros((128, 2), np.float32)
    dl = W['ssd_d'][l]
    for c in range(2):
        dcol[:64, c] = dl[4*i + 2*c]; dcol[64:, c] = dl[4*i + 2*c + 1]
    return {"uT": uT_full, "w": w, "cw": cw, "hp": hp, "dcol": dcol}
def t5_onehot(T):
    n = np.arange(T)
    nf = np.maximum(n, 1).astype(np.float32)
    large = 16 + (np.log(nf / np.float32(16)) / np.float32(math.log(4096 / 16)) * np.float32(16)).astype(np.int32)
    bkt = np.where(n < 16, n, np.minimum(large, 31))
    oh = np.zeros((32, T), np.float32)
    oh[bkt, n] = 1.0
    return oh
def attn_inputs(W, l, i, uT_full, oh):
    w_in = W['w_in'][l]
    ws = []
    for hh in range(2):
        hd = 2 * i + hh
        cols = np.concatenate([13376 + 128 * hd + np.arange(128), 15424 + 128 * hd + np.arange(128), 17472 + 128 * hd + np.arange(128)])
        ws.append(w_in[:, cols])
    gn = np.ascontiguousarray(np.stack([W['attn_q_norm'][l], W['attn_k_norm'][l]], axis=1))
    rb = np.ascontiguousarray(W['rel_bias'][:, 2 * i:2 * i + 2])
    return {"uT": uT_full, "w": np.ascontiguousarray(np.stack(ws)), "gn": gn, "rb": rb, "oh": oh}
def gdn_inputs(W, l, i, uT_full):
    w_in = W['w_in'][l]
    ws, cws, hps = [], [], []
    for hh in range(2):
        hd = 2 * i + hh
        cols = np.concatenate([5152 + 128 * hd + np.arange(128), 7200 + 128 * hd + np.arange(128), 9248 + 128 * hd + np.arange(128),
                               11296 + 128 * hd + np.arange(128), [13344 + hd], [13360 + hd]])
        ws.append(w_in[:, cols])
        chans = np.stack([128 * hd + np.arange(128), 2048 + 128 * hd + np.arange(128), 4096 + 128 * hd + np.arange(128)], axis=1)
        cwl = W['gdn_conv_w'][l]
        cw = np.zeros((128, 3, 4), np.float32)
        for k in range(4):
            cw[:, :, k] = cwl[k][chans]
        cws.append(cw)
        hp = np.zeros((128, 2), np.float32)
        hp[:, 0] = W['gdn_dt_bias'][l][hd]
        hp[:, 1] = W['gdn_a_log'][l][hd]
        hps.append(hp)
    return {"uT": uT_full, "w": np.ascontiguousarray(np.stack(ws)), "cw": np.stack(cws), "hp": np.stack(hps),
            "gn": np.ascontiguousarray(W['gdn_norm'][l].reshape(128, 1))}


_NC = {}


def _get(name, fn):
    if name not in _NC:
        _NC[name] = fn()
    return _NC[name]


def kernel(**I):
    I = {k: np.asarray(v) for k, v in I.items()}
    n = 8
    cores = list(range(n))
    x = I['x']
    c = I['c']
    cT = np.ascontiguousarray(c.reshape(16, 128).T)
    w_mod, b_mod = I['w_mod'], I['b_mod']
    ins = []
    for i in range(n):
        ins.append({"wm": np.ascontiguousarray(w_mod[:, :, i * 2304:(i + 1) * 2304]),
                    "bm": np.ascontiguousarray(np.stack([b_mod[l, i * 2304:(i + 1) * 2304].reshape(18, 128).T for l in range(4)])),
                    "cT": cT})
    res = run_bass_kernel_spmd(_get('mod', build_mod), ins, core_ids=cores)
    modT = np.concatenate([r["modp"] for r in res.results], axis=2)
    hT = [np.ascontiguousarray(x[0, i * TL:(i + 1) * TL, :].T) for i in range(n)]
    oh = t5_onehot(T)
    for l in range(4):
        gains = np.ascontiguousarray(np.concatenate([I['norm_ffn1'][l].reshape(16, 128).T, I['norm_mix'][l].reshape(16, 128).T], axis=1))
        ins = [{"hin": hT[i], "modT": modT[l], "gains": gains, "wg": I['ffn1_w_gate'][l], "wu": I['ffn1_w_up'][l],
                "wd": I['ffn1_w_down'][l]} for i in range(n)]
        res = run_bass_kernel_spmd(_get('A', build_A), ins, core_ids=cores)
        hT = [r["hout"] for r in res.results]
        uT = [r["uout"] for r in res.results]
        uT_full = np.ascontiguousarray(np.concatenate(uT, axis=1))
        ins = [ssd_inputs(I, l, i, uT_full) for i in range(n)]
        res = run_bass_kernel_spmd(_get('ssd', build_ssd), ins, core_ids=cores)
        ys_full = np.concatenate([r["y"] for r in res.results], axis=0)
        ins = [gdn_inputs(I, l, i, uT_full) for i in range(n)]
        res = run_bass_kernel_spmd(_get('gdn', build_gdn), ins, core_ids=cores)
        yg_full = np.concatenate([r["o"] for r in res.results], axis=0)
        ins = [attn_inputs(I, l, i, uT_full, oh) for i in range(n)]
        res = run_bass_kernel_spmd(_get('attn', build_attn), ins, core_ids=cores)
        ya_full = np.concatenate([r["o"] for r in res.results], axis=0)
        gains = np.ascontiguousarray(np.concatenate([I['ssd_norm'][l].reshape(16, 128).T, I['norm_ffn2'][l].reshape(16, 128).T], axis=1))
        wgt = np.ascontiguousarray(I['w_in'][l][:, 19520:25664])
        ins = []
        for i in range(n):
            tsl = slice(i * TL, (i + 1) * TL)
            ins.append({"hin": hT[i], "uin": uT[i], "ys": np.ascontiguousarray(ys_full[:, tsl]), "yg": np.ascontiguousarray(yg_full[:, tsl]),
                        "ya": np.ascontiguousarray(ya_full[:, tsl]),
                        "modT": modT[l], "gains": gains, "wos": I['w_o_ssd'][l], "wog": I['w_o_gdn'][l], "woa": I['w_o_attn'][l],
                        "wgt": wgt, "wout": I['w_out'][l], "wg": I['ffn2_w_gate'][l], "wu": I['ffn2_w_up'][l], "wd": I['ffn2_w_down'][l]})
        res = run_bass_kernel_spmd(_get('C', build_C), ins, core_ids=cores)
        hT = [r["hout"] for r in res.results]
    out = np.concatenate([h.T for h in hT], axis=0)[None]
    return np.ascontiguousarray(out.astype(np.float32))
```

```python
import math
import threading
import numpy as np
from contextlib import ExitStack
import concourse.bass as bass
import concourse.mybir as mybir
from concourse.bass_utils import run_bass_kernel_spmd
import ml_dtypes

F32 = mybir.dt.float32
BF16 = mybir.dt.bfloat16
I32 = mybir.dt.int32
AF = mybir.ActivationFunctionType
ALU = mybir.AluOpType
AX = mybir.AxisListType
NPBF16 = ml_dtypes.bfloat16

SAME_ENGINE_SYNC = True
EPOCH_MAX = 30000


class Sched:
    ENGS = ('pe', 'act', 'dve', 'pool', 'sp')

    def __init__(self, nc, ndma=8):
        self.nc = nc
        self.streams = {e: [] for e in self.ENGS}
        self.cnt = {}
        self.known = {e: {} for e in self.ENGS}
        self.lastw = {}
        self.readers = {}
        self.epoch = {e: 0 for e in self.ENGS}
        self.dma_rr = {e: 0 for e in self.ENGS}
        self.dma_epoch = {}
        self.ndma = ndma
        self.keys = []
        self.keyset = set()
        self.out_tokens = []
        self.n_ops = 0

    def _key(self, k):
        if k not in self.keyset:
            self.keyset.add(k)
            self.keys.append(k)
        return k

    def _deps(self, eng, reads, writes, extra=()):
        need = {}

        def add(t):
            if t is None:
                return
            k, v = t
            if not SAME_ENGINE_SYNC or eng == 'pe':
                if k[0] == 'e' and k[1] == eng:
                    return
            if need.get(k, 0) < v:
                need[k] = v
        for r in reads:
            add(self.lastw.get(r))
        for w in writes:
            add(self.lastw.get(w))
            rd = self.readers.get(w)
            if rd:
                for k, v in rd.items():
                    add((k, v))
        for t in extra:
            add(t)
        waits = []
        kn = self.known[eng]
        for k, v in need.items():
            if kn.get(k, 0) >= v:
                continue
            kn[k] = v
            waits.append((k, v))
        return waits

    def _commit(self, tok, reads, writes):
        k, v = tok
        for r in reads:
            d = self.readers.setdefault(r, {})
            if d.get(k, 0) < v:
                d[k] = v
        for w in writes:
            self.lastw[w] = tok
            self.readers[w] = {}

    def op(self, eng, fn, reads=(), writes=()):
        psr = [r for r in reads if isinstance(r, tuple) and r and r[0] == 'ps']
        if psr:
            writes = list(writes) + [r for r in psr if r not in writes]
        waits = self._deps(eng, reads, writes)
        key = self._key(('e', eng, self.epoch[eng]))
        val = self.cnt.get(key, 0) + 1
        self.cnt[key] = val
        if val >= EPOCH_MAX:
            self.epoch[eng] += 1
        tok = (key, val)
        self.streams[eng].append((waits, fn, key, 1))
        self._commit(tok, reads, writes)
        self.n_ops += 1
        return tok

    def dma(self, q, out, in_, reads=(), writes=(), is_output=False, **kw):
        j = self.dma_rr[q] % self.ndma
        self.dma_rr[q] += 1
        ep = self.dma_epoch.get((q, j), 0)
        key = ('d', q, j, ep)
        prev = self.cnt.get(key, 0)
        extra = []
        if prev > 0:
            extra.append((key, prev))
        elif ep > 0:
            pk = ('d', q, j, ep - 1)
            extra.append((pk, self.cnt[pk]))
        waits = self._deps(q, reads, writes, extra)
        self._key(key)
        val = prev + 16
        self.cnt[key] = val
        if val >= EPOCH_MAX:
            self.dma_epoch[(q, j)] = ep + 1
        tok = (key, val)

        def fn(eng, out=out, in_=in_, kw=kw):
            return eng.dma_start(out=out, in_=in_, **kw)
        self.streams[q].append((waits, fn, key, 16))
        self._commit(tok, reads, writes)
        if is_output:
            self.out_tokens.append(tok)
        self.n_ops += 1
        return tok

    def coll(self, kind, ins, outs, reads=(), writes=(), groups=None):
        q = 'pool'
        j = self.dma_rr[q] % self.ndma
        self.dma_rr[q] += 1
        ep = self.dma_epoch.get((q, j), 0)
        key = ('d', q, j, ep)
        prev = self.cnt.get(key, 0)
        extra = [(key, prev)] if prev > 0 else []
        waits = self._deps(q, reads, writes, extra)
        self._key(key)
        val = prev + 16
        self.cnt[key] = val
        if val >= EPOCH_MAX:
            self.dma_epoch[(q, j)] = ep + 1
        tok = (key, val)
        groups = groups or [list(range(8))]

        def fn(eng):
            return eng.collective_compute(kind, ALU.bypass, replica_groups=groups, ins=ins, outs=outs)
        self.streams[q].append((waits, fn, key, 16))
        self._commit(tok, reads, writes)
        self.n_ops += 1
        return tok

    def barrier(self):
        toks = [(k, v) for k, v in self.cnt.items()]
        for e in self.ENGS:
            waits = []
            kn = self.known[e]
            for k, v in toks:
                if k[0] == 'e' and k[1] == e:
                    continue
                if kn.get(k, 0) >= v:
                    continue
                kn[k] = v
                waits.append((k, v))
            if waits:
                self.streams[e].append((waits, None, None, 0))
        self.lastw = {}
        self.readers = {}

    def emit(self):
        nc = self.nc
        self.barrier()
        with ExitStack() as st:
            sems = {}
            for i, k in enumerate(self.keys):
                sems[k] = st.enter_context(nc.semaphore("s%d" % i))
            with nc.Block() as block:
                def mk(name):
                    def f(eng):
                        for (waits, fn, key, inc) in self.streams[name]:
                            for (k, v) in waits:
                                eng.wait_ge(sems[k], v)
                            if fn is not None:
                                ins = fn(eng)
                                ins.then_inc(sems[key], inc)
                    return f
                block.tensor(mk('pe'))
                block.scalar(mk('act'))
                block.vector(mk('dve'))
                block.gpsimd(mk('pool'))
                block.sync(mk('sp'))


class Mem:
    def __init__(self, nc, st, sbuf_bytes=200 * 1024):
        self.nc = nc
        self.big = st.enter_context(nc.sbuf_tensor("big", [128, sbuf_bytes // 4], F32))
        self.ps = st.enter_context(nc.psum_tensor("psall", [128, 8 * 512], F32))
        self.off = 0
        self.cap = sbuf_bytes
        self.marks = []

    def push(self):
        self.marks.append(self.off)

    def pop(self):
        self.off = self.marks.pop()

    def alloc(self, free_elems, dtype, parts=128):
        esz = 2 if dtype == BF16 else 4
        nbytes = (free_elems * esz + 63) // 64 * 64
        o = self.off
        self.off += nbytes
        assert self.off <= self.cap, "SBUF overflow %d" % self.off
        v = self.big[0:parts, o // 4:(o + nbytes) // 4]
        if dtype != F32:
            v = v.bitcast(dtype)
        return v[:, 0:free_elems]

    def bank(self, b, dtype=F32, parts=128):
        v = self.ps[0:parts, b * 512:(b + 1) * 512]
        if dtype != F32:
            v = v.bitcast(dtype)
        return v


D = 2048
DFF = 5632
TL = 1024
NK = 16
EPS = 1e-6


def new_nc():
    return bass.Bass("TRN2", target_bir_lowering=False)


def build_mod():
    nc = new_nc()
    wm = nc.dram_tensor("wm", [4, D, 2304], F32, kind="ExternalInput").ap()
    bm = nc.dram_tensor("bm", [4, 128, 18], F32, kind="ExternalInput").ap()
    cT = nc.dram_tensor("cT", [128, NK], F32, kind="ExternalInput").ap()
    out = nc.dram_tensor("modp", [4, 128, 18], F32, kind="ExternalOutput").ap()
    with ExitStack() as st:
        m = Mem(nc, st)
        s = Sched(nc)
        c_sb = m.alloc(NK, F32)
        s.dma('sp', c_sb, cT, writes=['c'])
        b_sb = m.alloc(4 * 18, F32)
        for l in range(4):
            s.dma('sp', b_sb[:, l * 18:(l + 1) * 18], bm[l], writes=[('b', l)])
        o_sb = m.alloc(4 * 18, F32)
        wt = [m.alloc(NK * 384, F32) for _ in range(2)]
        it = 0
        for l in range(4):
            for cb in range(6):
                w = wt[it % 2]
                w3 = w.rearrange("p (k c) -> p k c", k=NK)
                src = wm[l][:, cb * 384:(cb + 1) * 384].rearrange("(k p) c -> p k c", p=128)
                s.dma('sp' if it % 2 == 0 else 'act', w3, src, writes=[('w', it % 2)])
                for cc in range(3):
                    col = cb * 3 + cc
                    ps = m.bank(l % 2)[:, col:col + 1]
                    for k in range(NK):
                        s.op('pe', lambda e, ps=ps, w3=w3, k=k, cc=cc: e.matmul(
                            ps, lhsT=w3[:, k, cc * 128:(cc + 1) * 128], rhs=c_sb[:, k:k + 1],
                            start=(k == 0), stop=(k == NK - 1)),
                            reads=[('w', it % 2), 'c'], writes=[('ps', l % 2)])
                it += 1
            s.op('dve', lambda e, l=l: e.tensor_tensor(out=o_sb[:, l * 18:(l + 1) * 18], in0=m.bank(l % 2)[:, 0:18],
                                                       in1=b_sb[:, l * 18:(l + 1) * 18], op=ALU.add),
                 reads=[('ps', l % 2), ('b', l)], writes=[('o', l)])
            s.dma('sp', out[l], o_sb[:, l * 18:(l + 1) * 18], reads=[('o', l)], is_output=True)
        s.emit()
    return nc


def rms_modulate(s, m, hT, uT, gs, sh, tmpb, tag, ones_f32, psb):
    sq, rstd, tmp = tmpb
    for t in range(TL // 512):
        tsl = slice(t * 512, (t + 1) * 512)
        ps = m.bank(psb)
        for k in range(NK):
            sqb = sq[k % 2]
            s.op('act', lambda e, sqb=sqb, k=k, tsl=tsl: e.activation(out=sqb, in_=hT[:, k, tsl], func=AF.Square),
                 reads=[('hT', k, t)], writes=[('sq', k % 2)])
            s.op('pe', lambda e, ps=ps, sqb=sqb, k=k: e.matmul(ps, lhsT=ones_f32, rhs=sqb, start=(k == 0), stop=(k == NK - 1)),
                 reads=[('sq', k % 2), 'ones'], writes=[('ps', psb)])
        s.op('act', lambda e, ps=ps: e.activation(out=rstd, in_=ps, func=AF.Sqrt, scale=1.0 / D, bias=EPSB[0]),
             reads=[('ps', psb), 'epsb'], writes=['rstd'])
        s.op('dve', lambda e: e.reciprocal(out=rstd, in_=rstd), reads=['rstd'], writes=['rstd'])
        for k in range(NK):
            tb = tmp[k % 2]
            s.op('dve', lambda e, tb=tb, k=k, tsl=tsl: e.tensor_tensor(out=tb, in0=hT[:, k, tsl], in1=rstd, op=ALU.mult),
                 reads=[('hT', k, t), 'rstd'], writes=[('tmp', k % 2)])
            s.op('act', lambda e, tb=tb, k=k, tsl=tsl: e.activation(out=uT[:, k, tsl], in_=tb, func=AF.Identity,
                                                                    scale=gs[:, k:k + 1], bias=sh[:, k:k + 1]),
                 reads=[('tmp', k % 2), tag], writes=[('uT', t)])


EPSB = [None]


def ffn(s, m, hT, uT, utag, wg, wu, wd, ghalf, bufs):
    wgb, wub, wdb, hh, sg = bufs
    NP = DFF // 256
    for j in range(NP):
        b = j % 2
        wg3 = wgb[b].rearrange("p (k c) -> p k c", k=NK)
        wu3 = wub[b].rearrange("p (k c) -> p k c", k=NK)
        wd3 = wdb[b].rearrange("p (j o) -> p j o", j=2)
        hh3 = hh[b].rearrange("p (j t) -> p j t", j=2)
        s.dma('pool', wg3, wg[:, j * 256:(j + 1) * 256].rearrange("(k p) c -> p k c", p=128), writes=[('wg', b)])
        s.dma('pool', wu3, wu[:, j * 256:(j + 1) * 256].rearrange("(k p) c -> p k c", p=128), writes=[('wu', b)])
        s.dma('pool', wd3, wd[j * 256:(j + 1) * 256, :].rearrange("(j p) o -> p j o", p=128), writes=[('wd', b)])
        for t in range(TL // 512):
            tsl = slice(t * 512, (t + 1) * 512)
            for jj in range(2):
                bg = (t * 2 + jj) % 2
                pg = m.bank(bg)
                pu = m.bank(2 + bg)
                for k in range(NK):
                    s.op('pe', lambda e, pg=pg, wg3=wg3, k=k, jj=jj, tsl=tsl: e.matmul(
                        pg, lhsT=wg3[:, k, jj * 128:(jj + 1) * 128], rhs=uT[:, k, tsl], start=(k == 0), stop=(k == NK - 1)),
                        reads=[('wg', b), (utag, t)], writes=[('ps', bg)])
                for k in range(NK):
                    s.op('pe', lambda e, pu=pu, wu3=wu3, k=k, jj=jj, tsl=tsl: e.matmul(
                        pu, lhsT=wu3[:, k, jj * 128:(jj + 1) * 128], rhs=uT[:, k, tsl], start=(k == 0), stop=(k == NK - 1)),
                        reads=[('wu', b), (utag, t)], writes=[('ps', 2 + bg)])
                sgb = sg[bg]
                s.op('act', lambda e, sgb=sgb, pg=pg: e.activation(out=sgb, in_=pg, func=AF.Silu),
                     reads=[('ps', bg)], writes=[('sg', bg)])
                s.op('dve', lambda e, sgb=sgb, pu=pu, hh3=hh3, jj=jj, tsl=tsl: e.tensor_tensor(
                    out=hh3[:, jj, tsl], in0=sgb, in1=pu, op=ALU.mult),
                    reads=[('sg', bg), ('ps', 2 + bg)], writes=[('hh', b, jj, t)])
        i = 0
        for o in range(NK):
            for t in range(TL // 512):
                tsl = slice(t * 512, (t + 1) * 512)
                pb = 4 + (i % 4)
                i += 1
                pd = m.bank(pb)
                for jj in range(2):
                    s.op('pe', lambda e, pd=pd, wd3=wd3, jj=jj, o=o, hh3=hh3, tsl=tsl: e.matmul(
                        pd, lhsT=wd3[:, jj, o * 128:(o + 1) * 128], rhs=hh3[:, jj, tsl], start=(jj == 0), stop=(jj == 1)),
                        reads=[('wd', b), ('hh', b, jj, t)], writes=[('ps', pb)])
                s.op('dve', lambda e, pd=pd, o=o, tsl=tsl: e.scalar_tensor_tensor(
                    out=hT[:, o, tsl], in0=pd, scalar=ghalf[:, o:o + 1], in1=hT[:, o, tsl], op0=ALU.mult, op1=ALU.add),
                    reads=[('ps', pb), ('hT', o, t), 'ghalf'], writes=[('hT', o, t)])


def setup_consts(s, m):
    ones = m.alloc(128, F32)
    s.op('dve', lambda e: e.memset(ones, 1.0), writes=['ones'])
    epsb = m.alloc(1, F32)
    s.op('dve', lambda e: e.memset(epsb, EPS), writes=['epsb'])
    EPSB[0] = epsb
    return ones


def build_A():
    nc = new_nc()
    hin = nc.dram_tensor("hin", [D, TL], F32, kind="ExternalInput").ap()
    modT = nc.dram_tensor("modT", [128, 144], F32, kind="ExternalInput").ap()
    gains = nc.dram_tensor("gains", [128, 2 * NK], F32, kind="ExternalInput").ap()
    wg = nc.dram_tensor("wg", [D, DFF], F32, kind="ExternalInput").ap()
    wu = nc.dram_tensor("wu", [D, DFF], F32, kind="ExternalInput").ap()
    wd = nc.dram_tensor("wd", [DFF, D], F32, kind="ExternalInput").ap()
    hout = nc.dram_tensor("hout", [D, TL], F32, kind="ExternalOutput").ap()
    uout = nc.dram_tensor("uout", [D, TL], BF16, kind="ExternalOutput").ap()
    with ExitStack() as st:
        m = Mem(nc, st)
        s = Sched(nc)
        ones = setup_consts(s, m)
        hT = m.alloc(NK * TL, F32).rearrange("p (k t) -> p k t", k=NK)
        uT = m.alloc(NK * TL, BF16).rearrange("p (k t) -> p k t", k=NK)
        mod = m.alloc(144, F32)
        gn = m.alloc(2 * NK, F32)
        gs = m.alloc(2 * NK, F32)
        gh = m.alloc(NK, F32)
        tmpb = ([m.alloc(512, F32) for _ in range(2)], m.alloc(512, F32), [m.alloc(512, F32) for _ in range(2)])
        bufs = ([m.alloc(NK * 256, BF16) for _ in range(2)], [m.alloc(NK * 256, BF16) for _ in range(2)],
                [m.alloc(2 * D, BF16) for _ in range(2)], [m.alloc(2 * TL, BF16) for _ in range(2)],
                [m.alloc(512, F32) for _ in range(2)])
        for k in range(NK):
            s.dma('sp', hT[:, k, :], hin[k * 128:(k + 1) * 128, :], writes=[('hT', k, 0), ('hT', k, 1)])
        s.dma('sp', mod, modT, writes=['mod'])
        s.dma('sp', gn, gains, writes=['gn'])
        s.op('dve', lambda e: e.scalar_tensor_tensor(out=gs[:, 0:NK], in0=mod[:, 16:32], scalar=1.0, in1=gn[:, 0:NK],
                                                     op0=ALU.add, op1=ALU.mult), reads=['mod', 'gn'], writes=['u1gs'])
        s.op('dve', lambda e: e.scalar_tensor_tensor(out=gs[:, NK:2 * NK], in0=mod[:, 64:80], scalar=1.0, in1=gn[:, NK:2 * NK],
                                                     op0=ALU.add, op1=ALU.mult), reads=['mod', 'gn'], writes=['u2gs'])
        s.op('dve', lambda e: e.tensor_scalar(out=gh, in0=mod[:, 32:48], scalar1=0.5, scalar2=None, op0=ALU.mult),
             reads=['mod'], writes=['ghalf'])
        rms_modulate(s, m, hT, uT, gs[:, 0:NK], mod[:, 0:16], tmpb, 'u1gs', ones, 7)
        ffn(s, m, hT, uT, 'uT', wg, wu, wd, gh, bufs)
        rms_modulate(s, m, hT, uT, gs[:, NK:2 * NK], mod[:, 48:64], tmpb, 'u2gs', ones, 7)
        for k in range(NK):
            s.dma('sp', hout[k * 128:(k + 1) * 128, :], hT[:, k, :], reads=[('hT', k, 0), ('hT', k, 1)], is_output=True)
            s.dma('sp', uout[k * 128:(k + 1) * 128, :], uT[:, k, :], reads=[('uT', 0), ('uT', 1)], is_output=True)
        print("A ops", s.n_ops)
        s.emit()
    return nc


T = 8192


def make_tri_ident(s, m):
    ones_m = m.alloc(128, F32)
    tri = m.alloc(128, F32)
    ident = m.alloc(128, F32)
    s.op('pool', lambda e: e.memset(ones_m, 1.0), writes=['ones_m'])
    s.op('pool', lambda e: e.affine_select(out=tri, in_=ones_m, pattern=[[1, 128]], compare_op=ALU.is_ge, fill=0.0,
                                           base=0, channel_multiplier=-1), reads=['ones_m'], writes=['tri'])
    s.op('pool', lambda e: e.affine_select(out=ident, in_=ones_m, pattern=[[1, 128]], compare_op=ALU.is_equal, fill=0.0,
                                           base=0, channel_multiplier=-1), reads=['ones_m'], writes=['ident'])
    return ones_m, tri, ident


def build_ssd(T=8192):
    nc = new_nc()
    uT_d = nc.dram_tensor("uT", [D, T], BF16, kind="ExternalInput").ap()
    w_d = nc.dram_tensor("w", [D, 772], F32, kind="ExternalInput").ap()
    cw_d = nc.dram_tensor("cw", [128, 4, 5], F32, kind="ExternalInput").ap()
    hp_d = nc.dram_tensor("hp", [128, 3, 16], F32, kind="ExternalInput").ap()
    dc_d = nc.dram_tensor("dcol", [128, 2], F32, kind="ExternalInput").ap()
    y_d = nc.dram_tensor("y", [256, T], BF16, kind="ExternalOutput").ap()
    with ExitStack() as st:
        m = Mem(nc, st)
        s = Sched(nc)
        ones_m, tri, ident = make_tri_ident(s, m)
        onec = ones_m[:, 0:1]
        zeros_m = m.alloc(128, F32)
        s.op('pool', lambda e: e.memset(zeros_m, 0.0), writes=['zeros_m'])
        wb = m.alloc(NK * 772, BF16).rearrange("p (k c) -> p k c", k=NK)
        s.dma('pool', wb, w_d.rearrange("(k p) c -> p k c", p=128), writes=['wb'])
        cw = m.alloc(20, F32).rearrange("p (c k) -> p c k", c=4)
        s.dma('sp', cw, cw_d, writes=['cw'])
        hp = m.alloc(48, F32).rearrange("p (a b) -> p a b", a=3)
        s.dma('sp', hp, hp_d, writes=['hp'])
        dcol = m.alloc(2, F32)
        s.dma('sp', dcol, dc_d, writes=['dcol'])
        a16 = m.alloc(16, F32)
        s.op('act', lambda e: e.activation(out=a16, in_=hp[:, 1, :], func=AF.Exp), reads=['hp'], writes=['a16'])
        s.op('dve', lambda e: e.tensor_scalar(out=a16, in0=a16, scalar1=-1.0, scalar2=None, op0=ALU.mult), reads=['a16'], writes=['a16'])
        ub = [m.alloc(NK * 512, BF16).rearrange("p (k t) -> p k t", k=NK) for _ in range(2)]
        pre = [m.alloc(515, F32) for _ in range(4)]
        for c in range(4):
            s.op('pool', lambda e, c=c: e.memset(pre[c][:, 0:3], 0.0), writes=[('pre', c)])
        acc = [m.alloc(512, F32) for _ in range(2)]
        xT = m.alloc(2 * 512, F32).rearrange("p (c t) -> p c t", c=2)
        BT = m.alloc(512, F32)
        BTb = m.alloc(512, BF16)
        CTf = m.alloc(512, F32)
        CTb = m.alloc(512, BF16)
        sz = m.alloc(2 * 512, F32).rearrange("p (c t) -> p c t", c=2)
        dt_sb = m.alloc(16, F32)
        da_sb = m.alloc(16, F32)
        et = m.alloc(16, F32)
        S_f = m.alloc(256, F32)
        S_b = m.alloc(256, BF16)
        s.op('pool', lambda e: e.memset(S_f, 0.0), writes=['Sf'])
        s.op('pool', lambda e: e.memset(S_b, 0.0), writes=['Sb'])
        darep = [m.alloc(128, F32) for _ in range(2)]
        MG = m.alloc(128, F32)
        targ = [m.alloc(128, F32) for _ in range(2)]
        MTb = [m.alloc(128, BF16) for _ in range(2)]
        eacs = m.alloc(512, F32).rearrange("p (h l) -> p h l", h=4)
        CpT = [m.alloc(128, BF16) for _ in range(2)]
        acol = m.alloc(4, F32)
        nacol = m.alloc(4, F32)
        dec = m.alloc(4, F32)
        dtdec = m.alloc(4, F32)
        cdec = m.alloc(4, F32)
        xdt = m.alloc(256, BF16)
        xdd = m.alloc(256, BF16)
        Btok = m.alloc(128, BF16)
        ytmp = m.alloc(256, F32).rearrange("p (c l) -> p c l", c=2)
        yout = [m.alloc(2 * 512, BF16).rearrange("p (c t) -> p c t", c=2) for _ in range(2)]
        Stmp = m.alloc(256, F32)
        NT = T // 512
        for ti in range(NT):
            u = ub[ti % 2]
            ur = ('u', ti % 2)
            s.dma('sp', u, uT_d[:, ti * 512:(ti + 1) * 512].rearrange("(k p) t -> p k t", p=128), writes=[ur])
            for g in range(6):
                pb = g % 2
                ps = m.bank(pb)
                for k in range(NK):
                    s.op('pe', lambda e, ps=ps, g=g, k=k, u=u: e.matmul(ps, lhsT=wb[:, k, g * 128:(g + 1) * 128], rhs=u[:, k, :],
                                                                     start=(k == 0), stop=(k == NK - 1)),
                         reads=['wb', ur], writes=[('ps', pb)])
                if g < 2:
                    s.op('act', lambda e, ps=ps, g=g: e.activation(out=sz[:, g, :], in_=ps, func=AF.Silu),
                         reads=[('ps', pb)], writes=[('sz', g)])
                else:
                    c = g - 2
                    s.op('act', lambda e, ps=ps, c=c: e.activation(out=pre[c][:, 3:515], in_=ps, func=AF.Copy),
                         reads=[('ps', pb)], writes=[('pre', c)])
                    a = acc[c % 2]
                    s.op('dve', lambda e, a=a, c=c: e.tensor_scalar(out=a, in0=pre[c][:, 0:512], scalar1=cw[:, c, 0:1], scalar2=cw[:, c, 4:5],
                                                                    op0=ALU.mult, op1=ALU.add), reads=[('pre', c), 'cw'], writes=[('acc', c % 2)])
                    for kk in range(1, 4):
                        s.op('dve', lambda e, a=a, c=c, kk=kk: e.scalar_tensor_tensor(out=a, in0=pre[c][:, kk:kk + 512], scalar=cw[:, c, kk:kk + 1],
                                                                                      in1=a, op0=ALU.mult, op1=ALU.add),
                             reads=[('pre', c), 'cw', ('acc', c % 2)], writes=[('acc', c % 2)])
                    s.op('pool', lambda e, c=c: e.tensor_copy(out=pre[c][:, 0:3], in_=pre[c][:, 512:515]), reads=[('pre', c)], writes=[('pre', c)])
                    if c < 2:
                        s.op('act', lambda e, a=a, c=c: e.activation(out=xT[:, c, :], in_=a, func=AF.Silu), reads=[('acc', c % 2)], writes=[('xT', c)])
                    elif c == 2:
                        s.op('act', lambda e, a=a: e.activation(out=BT, in_=a, func=AF.Silu), reads=[('acc', 0)], writes=['BT'])
                        s.op('pool', lambda e: e.tensor_copy(out=BTb, in_=BT), reads=['BT'], writes=['BTb'])
                    else:
                        s.op('act', lambda e, a=a: e.activation(out=CTf, in_=a, func=AF.Silu), reads=[('acc', 1)], writes=['CTf'])
                        s.op('pool', lambda e: e.tensor_copy(out=CTb, in_=CTf), reads=['CTf'], writes=['CTb'])
            pdt = m.bank(2)
            for ch in range(4):
                for k in range(NK):
                    s.op('pe', lambda e, ch=ch, k=k, u=u: e.matmul(pdt[:, ch * 4:(ch + 1) * 4], lhsT=u[:, k, ch * 128:(ch + 1) * 128],
                                                               rhs=wb[:, k, 768:772], start=(k == 0), stop=(k == NK - 1)),
                         reads=['wb', ur], writes=[('ps', 2)])
            s.op('dve', lambda e: e.tensor_tensor(out=et, in0=pdt[:, 0:16], in1=hp[:, 0, :], op=ALU.add), reads=[('ps', 2), 'hp'], writes=['et'])
            s.op('act', lambda e: e.activation(out=et, in_=et, func=AF.Exp), reads=['et'], writes=['et'])
            s.op('act', lambda e: e.activation(out=dt_sb, in_=et, func=AF.Ln, bias=onec, scale=1.0), reads=['et', 'ones_m'], writes=['dt'])
            s.op('dve', lambda e: e.tensor_tensor(out=da_sb, in0=dt_sb, in1=a16, op=ALU.mult), reads=['dt', 'a16'], writes=['da'])
            for ch in range(4):
                csl = slice(ch * 128, (ch + 1) * 128)
                dsl = slice(ch * 4, (ch + 1) * 4)
                pG = m.bank(3)[:, 0:128]
                s.op('pe', lambda e, pG=pG, csl=csl: e.matmul(pG, lhsT=BTb[:, csl], rhs=CTb[:, csl], start=True, stop=True),
                     reads=['BTb', 'CTb'], writes=[('ps', 3)])
                s.op('dve', lambda e, pG=pG: e.tensor_tensor(out=MG, in0=pG, in1=tri, op=ALU.mult), reads=[('ps', 3), 'tri'], writes=['MG'])
                pcol = m.bank(5)[:, 384:388]
                s.op('pe', lambda e, pcol=pcol, dsl=dsl: e.matmul(pcol, lhsT=tri, rhs=da_sb[:, dsl], start=True, stop=True),
                     reads=['tri', 'da'], writes=[('ps', 5)])
                palast = m.bank(5)[:, 388:392]
                s.op('pe', lambda e, palast=palast, dsl=dsl: e.matmul(palast, lhsT=ones_m, rhs=da_sb[:, dsl], start=True, stop=True),
                     reads=['ones_m', 'da'], writes=[('ps', 5)])
                s.op('dve', lambda e, pcol=pcol: e.tensor_copy(out=acol, in_=pcol), reads=[('ps', 5)], writes=['acol'])
                s.op('dve', lambda e: e.tensor_scalar(out=nacol, in0=acol, scalar1=-1.0, scalar2=None, op0=ALU.mult), reads=['acol'], writes=['nacol'])
                prow = m.bank(4).rearrange("p (h l) -> p h l", h=4)
                for h in range(4):
                    dr = darep[h % 2]
                    s.op('pool', lambda e, dr=dr, h=h, ch=ch: e.tensor_scalar(out=dr, in0=ones_m, scalar1=da_sb[:, ch * 4 + h:ch * 4 + h + 1], scalar2=None,
                                                                           op0=ALU.mult), reads=['ones_m', 'da'], writes=[('darep', h % 2)])
                    s.op('pe', lambda e, dr=dr, h=h: e.matmul(prow[:, h, :], lhsT=dr, rhs=tri, start=True, stop=True),
                         reads=[('darep', h % 2), 'tri'], writes=[('ps', 4)])
                s.op('act', lambda e: e.activation(out=eacs, in_=prow, func=AF.Exp), reads=[('ps', 4)], writes=['eacs'])
                s.op('dve', lambda e, palast=palast: e.tensor_tensor(out=dec, in0=palast, in1=acol, op=ALU.subtract), reads=[('ps', 5), 'acol'], writes=['dec'])
                s.op('act', lambda e: e.activation(out=dec, in_=dec, func=AF.Exp), reads=['dec'], writes=['dec'])
                s.op('dve', lambda e, dsl=dsl: e.tensor_tensor(out=dtdec, in0=dec, in1=dt_sb[:, dsl], op=ALU.mult), reads=['dec', 'dt'], writes=['dtdec'])
                s.op('act', lambda e, palast=palast: e.activation(out=cdec, in_=palast, func=AF.Exp), reads=[('ps', 5)], writes=['cdec'])
                pX = m.bank(5)[:, 0:256]
                for c in range(2):
                    s.op('pe', lambda e, c=c, csl=csl: e.transpose(out=pX[:, c * 128:(c + 1) * 128], in_=xT[:, c, csl], identity=ident),
                         reads=[('xT', c), 'ident'], writes=[('ps', 5)])
                pB = m.bank(5)[:, 256:384]
                s.op('pe', lambda e, csl=csl: e.transpose(out=pB, in_=BT[:, csl], identity=ident), reads=['BT', 'ident'], writes=[('ps', 5)])
                s.op('act', lambda e: e.activation(out=Btok, in_=pB, func=AF.Copy), reads=[('ps', 5)], writes=['Btok'])
                for h in range(4):
                    hs = slice(h * 64, (h + 1) * 64)
                    dcolm = dt_sb[:, ch * 4 + h:ch * 4 + h + 1]
                    s.op('act', lambda e, hs=hs, dcolm=dcolm: e.activation(out=xdt[:, hs], in_=pX[:, hs], func=AF.Copy, scale=dcolm),
                         reads=[('ps', 5), 'dt'], writes=['xdt'])
                    s.op('act', lambda e, hs=hs, h=h: e.activation(out=xdd[:, hs], in_=pX[:, hs], func=AF.Copy, scale=dtdec[:, h:h + 1]),
                         reads=[('ps', 5), 'dtdec'], writes=['xdd'])
                py = m.bank(6)[:, 0:256].rearrange("p (c l) -> p c l", c=2)
                for h in range(4):
                    tg = targ[h % 2]
                    mt = MTb[h % 2]
                    cp = CpT[h % 2]
                    s.op('dve', lambda e, tg=tg, h=h: e.scalar_tensor_tensor(out=tg, in0=prow[:, h, :], scalar=nacol[:, h:h + 1], in1=zeros_m,
                                                                           op0=ALU.add, op1=ALU.min),
                         reads=[('ps', 4), 'nacol', 'zeros_m'], writes=[('targ', h % 2)])
                    s.op('act', lambda e, tg=tg: e.activation(out=tg, in_=tg, func=AF.Exp), reads=[('targ', h % 2)], writes=[('targ', h % 2)])
                    s.op('dve', lambda e, tg=tg, mt=mt: e.tensor_tensor(out=mt, in0=tg, in1=MG, op=ALU.mult),
                         reads=[('targ', h % 2), 'MG'], writes=[('MT', h % 2)])
                    s.op('pool', lambda e, cp=cp, h=h, csl=csl: e.tensor_tensor(out=cp, in0=CTf[:, csl], in1=eacs[:, h, :], op=ALU.mult),
                         reads=['CTf', 'eacs'], writes=[('CpT', h % 2)])
                    po = py[(h % 2) * 64:(h % 2) * 64 + 64, h // 2, :]
                    s.op('pe', lambda e, po=po, h=h, mt=mt: e.matmul(po, lhsT=xdt[:, h * 64:(h + 1) * 64], rhs=mt, start=True, stop=False),
                         reads=['xdt', ('MT', h % 2)], writes=[('ps', 6)])
                    s.op('pe', lambda e, po=po, h=h, cp=cp: e.matmul(po, lhsT=S_b[:, h * 64:(h + 1) * 64], rhs=cp, start=False, stop=True),
                         reads=['Sb', ('CpT', h % 2)], writes=[('ps', 6)])
                pS = m.bank(7)[:, 0:256]
                s.op('pe', lambda e, pS=pS: e.matmul(pS, lhsT=Btok, rhs=xdd, start=True, stop=True), reads=['Btok', 'xdd'], writes=[('ps', 7)])
                for h in range(4):
                    hs = slice(h * 64, (h + 1) * 64)
                    s.op('dve', lambda e, hs=hs, h=h, pS=pS: e.scalar_tensor_tensor(out=S_f[:, hs], in0=S_f[:, hs], scalar=cdec[:, h:h + 1], in1=pS[:, hs],
                                                                               op0=ALU.mult, op1=ALU.add), reads=['Sf', 'cdec', ('ps', 7)], writes=['Sf'])
                s.op('act', lambda e: e.activation(out=S_b, in_=S_f, func=AF.Copy), reads=['Sf'], writes=['Sb'])
                yo = yout[ti % 2]
                for c in range(2):
                    s.op('dve', lambda e, c=c, csl=csl: e.scalar_tensor_tensor(out=ytmp[:, c, :], in0=xT[:, c, csl], scalar=dcol[:, c:c + 1], in1=py[:, c, :],
                                                                              op0=ALU.mult, op1=ALU.add), reads=[('xT', c), 'dcol', ('ps', 6)], writes=[('ytmp', c)])
                    s.op('pool', lambda e, c=c, csl=csl, yo=yo: e.tensor_tensor(out=yo[:, c, csl], in0=ytmp[:, c, :], in1=sz[:, c, csl], op=ALU.mult),
                         reads=[('ytmp', c), ('sz', c)], writes=[('yout', ti % 2)])
            for c in range(2):
                s.dma('sp', y_d[c * 128:(c + 1) * 128, ti * 512:(ti + 1) * 512], yout[ti % 2][:, c, :], reads=[('yout', ti % 2)], is_output=True)
        print("SSD ops", s.n_ops)
        s.emit()
    return nc


def build_C(nbr=3):
    nc = new_nc()
    hin = nc.dram_tensor("hin", [D, TL], F32, kind="ExternalInput").ap()
    uin = nc.dram_tensor("uin", [D, TL], BF16, kind="ExternalInput").ap()
    br_d = [nc.dram_tensor(n, [D, TL], BF16, kind="ExternalInput").ap() for n in ("ys", "yg", "ya")]
    modT = nc.dram_tensor("modT", [128, 144], F32, kind="ExternalInput").ap()
    gains = nc.dram_tensor("gains", [128, 2 * NK], F32, kind="ExternalInput").ap()
    wo_d = [nc.dram_tensor(n, [D, D], F32, kind="ExternalInput").ap() for n in ("wos", "wog", "woa")]
    wgt = nc.dram_tensor("wgt", [D, 3 * D], F32, kind="ExternalInput").ap()
    wout = nc.dram_tensor("wout", [D, D], F32, kind="ExternalInput").ap()
    wg = nc.dram_tensor("wg", [D, DFF], F32, kind="ExternalInput").ap()
    wu = nc.dram_tensor("wu", [D, DFF], F32, kind="ExternalInput").ap()
    wd = nc.dram_tensor("wd", [DFF, D], F32, kind="ExternalInput").ap()
    hout = nc.dram_tensor("hout", [D, TL], F32, kind="ExternalOutput").ap()
    with ExitStack() as st:
        m = Mem(nc, st, sbuf_bytes=207 * 1024)
        s = Sched(nc)
        ones = setup_consts(s, m)
        hT = m.alloc(NK * TL, F32).rearrange("p (k t) -> p k t", k=NK)
        mod = m.alloc(144, F32)
        gn = m.alloc(2 * NK, F32)
        gs = m.alloc(NK, F32)
        gh = m.alloc(NK, F32)
        for k in range(NK):
            s.dma('sp', hT[:, k, :], hin[k * 128:(k + 1) * 128, :], writes=[('hT', k, 0), ('hT', k, 1)])
        s.dma('sp', mod, modT, writes=['mod'])
        s.dma('sp', gn, gains, writes=['gn'])
        s.op('dve', lambda e: e.scalar_tensor_tensor(out=gs, in0=mod[:, 112:128], scalar=1.0, in1=gn[:, NK:2 * NK],
                                                     op0=ALU.add, op1=ALU.mult), reads=['mod', 'gn'], writes=['u3gs'])
        s.op('dve', lambda e: e.tensor_scalar(out=gh, in0=mod[:, 128:144], scalar1=0.5, scalar2=None, op0=ALU.mult),
             reads=['mod'], writes=['ghalf'])
        m.push()
        brs = [m.alloc(NK * 512, BF16).rearrange("p (k t) -> p k t", k=NK) for _ in range(3)]
        ut = m.alloc(NK * 512, BF16).rearrange("p (k t) -> p k t", k=NK)
        mix = m.alloc(NK * 512, BF16).rearrange("p (k t) -> p k t", k=NK)
        wob = [[m.alloc(NK * 128, BF16).rearrange("p (k c) -> p k c", k=NK) for _ in range(6)] for _ in range(1)]
        wtb = [m.alloc(NK * 128, BF16).rearrange("p (k c) -> p k c", k=NK) for _ in range(2)]
        sq = [m.alloc(512, F32) for _ in range(2)]
        rstd = m.alloc(512, F32)
        tmp = [m.alloc(512, F32) for _ in range(2)]
        sig = m.alloc(512, F32)
        macc = m.alloc(512, F32)
        it = 0
        for t in range(2):
            tsl = slice(t * 512, (t + 1) * 512)
            for b in range(3):
                s.dma('sp', brs[b], br_d[b][:, tsl].rearrange("(k p) t -> p k t", p=128), writes=[('br', b)])
            s.dma('sp', ut, uin[:, tsl].rearrange("(k p) t -> p k t", p=128), writes=['ut'])
            for g in range(4):
                ps = m.bank(7)
                for kk in range(4):
                    k = g * 4 + kk
                    s.op('act', lambda e, k=k, kk=kk: e.activation(out=sq[kk % 2], in_=brs[0][:, k, :], func=AF.Square),
                         reads=[('br', 0)], writes=[('sq', kk % 2)])
                    s.op('pe', lambda e, ps=ps, kk=kk: e.matmul(ps, lhsT=ones, rhs=sq[kk % 2], start=(kk == 0), stop=(kk == 3)),
                         reads=[('sq', kk % 2), 'ones'], writes=[('ps', 7)])
                s.op('act', lambda e, ps=ps: e.activation(out=rstd, in_=ps, func=AF.Sqrt, scale=1.0 / 512, bias=EPSB[0]),
                     reads=[('ps', 7), 'epsb'], writes=['rstd'])
                s.op('dve', lambda e: e.reciprocal(out=rstd, in_=rstd), reads=['rstd'], writes=['rstd'])
                for kk in range(4):
                    k = g * 4 + kk
                    s.op('dve', lambda e, k=k, kk=kk: e.tensor_tensor(out=tmp[kk % 2], in0=brs[0][:, k, :], in1=rstd, op=ALU.mult),
                         reads=[('br', 0), 'rstd'], writes=[('tmp', kk % 2)])
                    s.op('act', lambda e, k=k, kk=kk: e.activation(out=brs[0][:, k, :], in_=tmp[kk % 2], func=AF.Copy, scale=gn[:, k:k + 1]),
                         reads=[('tmp', kk % 2), 'gn'], writes=[('br', 0)])
            for o in range(NK):
                wb6 = wob[0]
                wr = ('wo', 0)
                it += 1
                osl = slice(o * 128, (o + 1) * 128)
                for b in range(3):
                    s.dma('pool', wb6[b], wo_d[b][:, osl].rearrange("(k p) c -> p k c", p=128), writes=[wr])
                    s.dma('pool', wb6[3 + b], wgt[:, b * D + o * 128:b * D + (o + 1) * 128].rearrange("(k p) c -> p k c", p=128), writes=[wr])
                for b in range(3):
                    py = m.bank(b % 2)
                    pg = m.bank(2 + b % 2)
                    for k in range(NK):
                        s.op('pe', lambda e, py=py, b=b, k=k, wb6=wb6: e.matmul(py, lhsT=wb6[b][:, k, :], rhs=brs[b][:, k, :], start=(k == 0), stop=(k == NK - 1)),
                             reads=[wr, ('br', b)], writes=[('ps', b % 2)])
                    for k in range(NK):
                        s.op('pe', lambda e, pg=pg, b=b, k=k, wb6=wb6: e.matmul(pg, lhsT=wb6[3 + b][:, k, :], rhs=ut[:, k, :], start=(k == 0), stop=(k == NK - 1)),
                             reads=[wr, 'ut'], writes=[('ps', 2 + b % 2)])
                    s.op('act', lambda e, pg=pg: e.activation(out=sig, in_=pg, func=AF.Sigmoid), reads=[('ps', 2 + b % 2)], writes=['sig'])
                    if b == 0:
                        s.op('dve', lambda e, py=py: e.tensor_tensor(out=macc, in0=sig, in1=py, op=ALU.mult), reads=['sig', ('ps', b % 2)], writes=['macc'])
                    else:
                        s.op('dve', lambda e, py=py: e.tensor_tensor(out=sig, in0=sig, in1=py, op=ALU.mult), reads=['sig', ('ps', b % 2)], writes=['sig'])
                        if b == 1:
                            s.op('dve', lambda e: e.tensor_tensor(out=macc, in0=macc, in1=sig, op=ALU.add), reads=['sig', 'macc'], writes=['macc'])
                        else:
                            s.op('dve', lambda e, o=o: e.tensor_tensor(out=mix[:, o, :], in0=macc, in1=sig, op=ALU.add), reads=['sig', 'macc'], writes=[('mix', o)])
            for o2 in range(NK):
                wt = wtb[o2 % 2]
                s.dma('pool', wt, wout[:, o2 * 128:(o2 + 1) * 128].rearrange("(k p) c -> p k c", p=128), writes=[('wt', o2 % 2)])
                pw = m.bank(4 + o2 % 2)
                for k in range(NK):
                    s.op('pe', lambda e, pw=pw, k=k, wt=wt: e.matmul(pw, lhsT=wt[:, k, :], rhs=mix[:, k, :], start=(k == 0), stop=(k == NK - 1)),
                         reads=[('wt', o2 % 2), ('mix', k)], writes=[('ps', 4 + o2 % 2)])
                s.op('dve', lambda e, pw=pw, o2=o2, tsl=tsl: e.scalar_tensor_tensor(out=hT[:, o2, tsl], in0=pw, scalar=mod[:, 80 + o2:81 + o2], in1=hT[:, o2, tsl],
                                                                                  op0=ALU.mult, op1=ALU.add),
                     reads=[('ps', 4 + o2 % 2), 'mod', ('hT', o2, t)], writes=[('hT', o2, t)])
        s.barrier()
        m.pop()
        uT = m.alloc(NK * TL, BF16).rearrange("p (k t) -> p k t", k=NK)
        tmpb = ([m.alloc(512, F32) for _ in range(2)], m.alloc(512, F32), [m.alloc(512, F32) for _ in range(2)])
        bufs = ([m.alloc(NK * 256, BF16) for _ in range(2)], [m.alloc(NK * 256, BF16) for _ in range(2)],
                [m.alloc(2 * D, BF16) for _ in range(2)], [m.alloc(2 * TL, BF16) for _ in range(2)],
                [m.alloc(512, F32) for _ in range(2)])
        rms_modulate(s, m, hT, uT, gs, mod[:, 96:112], tmpb, 'u3gs', ones, 7)
        ffn(s, m, hT, uT, 'uT', wg, wu, wd, gh, bufs)
        for k in range(NK):
            s.dma('sp', hout[k * 128:(k + 1) * 128, :], hT[:, k, :], reads=[('hT', k, 0), ('hT', k, 1)], is_output=True)
        print("C ops", s.n_ops)
        s.emit()
    return nc


NEGM = -30000.0


def build_attn2(T=8192):
    nc = new_nc()
    NB = T // 256
    NT = T // 512
    TW = T + 128
    WV = TW + 127
    uT_d = nc.dram_tensor("uT", [D, T], BF16, kind="ExternalInput").ap()
    w_d = nc.dram_tensor("w", [2, D, 384], F32, kind="ExternalInput").ap()
    gn_d = nc.dram_tensor("gn", [128, 2], F32, kind="ExternalInput").ap()
    rb_d = nc.dram_tensor("rb", [32, 2], F32, kind="ExternalInput").ap()
    oh_d = nc.dram_tensor("oh", [32, T], F32, kind="ExternalInput").ap()
    wv_d = nc.dram_tensor("wvec", [2, WV + 1], BF16, kind="Internal").ap()
    o_d = nc.dram_tensor("o", [256, T], BF16, kind="ExternalOutput").ap()
    with ExitStack() as st:
        m = Mem(nc, st, sbuf_bytes=207 * 1024)
        s = Sched(nc)
        ones = setup_consts(s, m)
        ones_m, tri, ident = make_tri_ident(s, m)
        identb = m.alloc(128, BF16)
        s.op('pool', lambda e: e.tensor_copy(out=identb, in_=ident), reads=['ident'], writes=['identb'])
        gn = m.alloc(2, F32)
        s.dma('sp', gn, gn_d, writes=['gn'])
        gq = m.alloc(1, F32)
        s.op('dve', lambda e: e.tensor_scalar(out=gq, in0=gn[:, 0:1], scalar1=128.0 ** -0.5, scalar2=None, op0=ALU.mult), reads=['gn'], writes=['gq'])
        rb = m.alloc(2, F32, parts=32)
        s.dma('sp', rb, rb_d, writes=['rb'])
        m.push()
        oh = m.alloc(T, F32, parts=32)
        s.dma('sp', oh, oh_d, writes=['oh'])
        wrow = m.alloc(WV + 1, BF16, parts=2)
        s.op('dve', lambda e: e.memset(wrow, NEGM), writes=['wrow'])
        for cch in range(T // 512):
            pb = m.bank(5 + cch % 2)
            s.op('pe', lambda e, pb=pb, cch=cch: e.matmul(pb[0:2, :], lhsT=rb, rhs=oh[:, cch * 512:(cch + 1) * 512], start=True, stop=True),
                 reads=['rb', 'oh'], writes=[('ps', 5 + cch % 2)])
            s.op('act', lambda e, pb=pb, cch=cch: e.activation(out=wrow[:, 255 + cch * 512:255 + (cch + 1) * 512], in_=pb[0:2, :], func=AF.Copy),
                 reads=[('ps', 5 + cch % 2)], writes=['wrow'])
        s.dma('sp', wv_d, wrow, reads=['wrow'], writes=['wvd'])
        s.barrier()
        m.pop()
        Tbig = m.alloc(TW, BF16)
        QT = m.alloc(T, BF16)
        KT = m.alloc(T, BF16)
        Va = m.alloc((T // 128) * 130, BF16).rearrange("p (c d) -> p c d", d=130)
        sel = m.alloc((T // 128) * 32, F32).rearrange("p (c n) -> p c n", n=32)
        negm = m.alloc(32 * 32, F32).rearrange("p (b n) -> p b n", b=32)
        zer = m.alloc(32 * 32, F32)
        s.op('pool', lambda e: e.memset(zer, 0.0), writes=['zer'])
        s.op('pool', lambda e: e.affine_select(out=negm, in_=zer.rearrange("p (b n) -> p b n", b=32), pattern=[[1, 32], [-1, 32]], compare_op=ALU.is_ge,
                                               fill=-1e30, base=-1, channel_multiplier=0), reads=['zer'], writes=['negm'])
        ones32 = ones_m[:, 0:32]
        wb = m.alloc(NK * 384, BF16).rearrange("p (k c) -> p k c", k=NK)
        ub = [m.alloc(NK * 512, BF16).rearrange("p (k t) -> p k t", k=NK) for _ in range(2)]
        xf = [m.alloc(512, F32) for _ in range(2)]
        sq = m.alloc(512, F32)
        rstd = m.alloc(512, F32)
        kn = m.alloc(512, F32)
        qn = m.alloc(512, F32)
        kmean = m.alloc(32, F32)
        s.op('pool', lambda e: e.memset(kmean, 0.0), writes=['kmean'])
        gsb = m.alloc(32, F32)
        m8 = m.alloc(8, F32)
        thr = m.alloc(1, F32)
        LB = [dict(pT=[m.alloc(256, BF16) for _ in range(2)], acc=[m.alloc(130, F32) for _ in range(2)], rec=m.alloc(1, F32),
                   of=m.alloc(128, F32), oT=[m.alloc(256, BF16) for _ in range(2)]) for _ in range(2)]
        for h in range(2):
            s.dma('pool', wb, w_d[h].rearrange("(k p) c -> p k c", p=128), writes=['wb'])
            for i in range(128):
                s.dma('sp' if i % 2 == 0 else 'act', Tbig[i:i + 1, :], wv_d[h:h + 1, 127 - i:127 - i + TW], reads=['wvd'], writes=['Tbig'])
            s.op('pool', lambda e: e.memset(Va[:, :, 128:130], 1.0), writes=['Va'])
            for ti in range(NT):
                u = ub[ti % 2]
                ur = ('u', ti % 2)
                tsl = slice(ti * 512, (ti + 1) * 512)
                s.dma('sp', u, uT_d[:, tsl].rearrange("(k p) t -> p k t", p=128), writes=[ur])
                for which in (1, 0):
                    ps = m.bank(5)
                    for k in range(NK):
                        s.op('pe', lambda e, ps=ps, k=k, u=u, which=which: e.matmul(ps, lhsT=wb[:, k, which * 128:(which + 1) * 128], rhs=u[:, k, :],
                                                                                 start=(k == 0), stop=(k == NK - 1)), reads=['wb', ur], writes=[('ps', 5)])
                    x = xf[which]
                    s.op('act', lambda e, x=x, ps=ps: e.activation(out=x, in_=ps, func=AF.Copy), reads=[('ps', 5)], writes=[('xf', which)])
                    s.op('act', lambda e, x=x: e.activation(out=sq, in_=x, func=AF.Square), reads=[('xf', which)], writes=['sq'])
                    p2 = m.bank(6)
                    s.op('pe', lambda e, p2=p2: e.matmul(p2, lhsT=ones, rhs=sq, start=True, stop=True), reads=['sq', 'ones'], writes=[('ps', 6)])
                    s.op('act', lambda e, p2=p2: e.activation(out=rstd, in_=p2, func=AF.Sqrt, scale=1.0 / 128, bias=EPSB[0]), reads=[('ps', 6), 'epsb'], writes=['rstd'])
                    s.op('dve', lambda e: e.reciprocal(out=rstd, in_=rstd), reads=['rstd'], writes=['rstd'])
                    s.op('dve', lambda e, x=x: e.tensor_tensor(out=x, in0=x, in1=rstd, op=ALU.mult), reads=[('xf', which), 'rstd'], writes=[('xf', which)])
                    if which == 1:
                        s.op('act', lambda e, x=x: e.activation(out=kn, in_=x, func=AF.Copy, scale=gn[:, 1:2]), reads=[('xf', 1), 'gn'], writes=['kn'])
                        s.op('pool', lambda e, tsl=tsl: e.tensor_copy(out=KT[:, tsl], in_=kn), reads=['kn'], writes=['KT'])
                        for bb in range(2):
                            blk = ti * 2 + bb
                            s.op('dve', lambda e, blk=blk, bb=bb: e.tensor_reduce(out=kmean[:, blk:blk + 1], in_=kn[:, bb * 256:(bb + 1) * 256], axis=AX.X, op=ALU.add),
                                 reads=['kn'], writes=['kmean'])
                    else:
                        s.op('act', lambda e, x=x: e.activation(out=qn, in_=x, func=AF.Copy, scale=gn[:, 0:1]), reads=[('xf', 0), 'gn'], writes=['qn'])
                        s.op('act', lambda e, x=x, tsl=tsl: e.activation(out=QT[:, tsl], in_=x, func=AF.Copy, scale=gq), reads=[('xf', 0), 'gq'], writes=['QT'])
                        for qq in range(4):
                            qt = ti * 4 + qq
                            b = qt // 2
                            pg = m.bank(7)[:, 0:32]
                            s.op('pe', lambda e, pg=pg, qq=qq: e.matmul(pg, lhsT=qn[:, qq * 128:(qq + 1) * 128], rhs=kmean, start=True, stop=True),
                                 reads=['qn', 'kmean'], writes=[('ps', 7)])
                            s.op('dve', lambda e, pg=pg, b=b: e.tensor_tensor(out=gsb, in0=pg, in1=negm[:, b, :], op=ALU.add), reads=[('ps', 7), 'negm'], writes=['gsb'])
                            s.op('dve', lambda e: e.max(out=m8, in_=gsb), reads=['gsb'], writes=['m8'])
                            s.op('dve', lambda e: e.tensor_scalar(out=thr, in0=m8[:, 2:3], scalar1=-1e29, scalar2=None, op0=ALU.max), reads=['m8'], writes=['thr'])
                            s.op('dve', lambda e, qt=qt: e.scalar_tensor_tensor(out=sel[:, qt, :], in0=gsb, scalar=thr, in1=ones32, op0=ALU.is_ge, op1=ALU.mult),
                                 reads=['gsb', 'thr', 'ones_m'], writes=['sel'])
                for cc in range(4):
                    ck = ti * 4 + cc
                    pv = m.bank(7)[:, 128:256]
                    for k in range(NK):
                        s.op('pe', lambda e, pv=pv, k=k, u=u, cc=cc: e.matmul(pv, lhsT=u[:, k, cc * 128:(cc + 1) * 128], rhs=wb[:, k, 256:384],
                                                                           start=(k == 0), stop=(k == NK - 1)), reads=['wb', ur], writes=[('ps', 7)])
                    s.op('act', lambda e, pv=pv, ck=ck: e.activation(out=Va[:, ck, 0:128], in_=pv, func=AF.Copy), reads=[('ps', 7)], writes=['Va'])
            def lane_body(s, lane, h=h):
                pT, acc, rec, of, oT = LB[lane]['pT'], LB[lane]['acc'], LB[lane]['rec'], LB[lane]['of'], LB[lane]['oT']
                it = 0
                for b in range(lane, NB, 2):
                    qsl = slice(b * 256, (b + 1) * 256)
                    for qt2 in range(2):
                        s.op('pool', lambda e, qt2=qt2: e.memset(acc[qt2], 0.0), writes=[('acc', qt2)])
                    for n in range(b + 1):
                        pOb = [4 * lane + 1, 4 * lane + 2]
                        for kh in range(2):
                            kt = 2 * n + kh
                            D0 = b * 256 - kt * 128 + 128
                            pS = m.bank(4 * lane)[:, 0:256] if kh == 0 else m.bank(4 * lane)[:, 256:512]
                            sr = ('ps', 4 * lane)
                            s.op('pe', lambda e, pS=pS, kt=kt, qsl=qsl: e.matmul(pS, lhsT=KT[:, kt * 128:(kt + 1) * 128], rhs=QT[:, qsl], start=True, stop=False),
                                 reads=['KT', 'QT'], writes=[sr])
                            s.op('pe', lambda e, pS=pS, D0=D0: e.matmul(pS, lhsT=identb, rhs=Tbig[:, D0:D0 + 256], start=False, stop=True),
                                 reads=['identb', 'Tbig'], writes=[sr])
                            p = pT[kh]
                            s.op('act', lambda e, p=p, pS=pS: e.activation(out=p, in_=pS, func=AF.Exp), reads=[sr], writes=[('pT', kh)])
                            for qt2 in range(2):
                                po = m.bank(pOb[qt2])[:, 0:129]
                                s.op('pe', lambda e, po=po, p=p, qt2=qt2, kt=kt, kh=kh: e.matmul(po, lhsT=p[:, qt2 * 128:(qt2 + 1) * 128], rhs=Va[:, kt, 0:129],
                                                                                              start=(kh == 0), stop=(kh == 1)),
                                     reads=[('pT', kh), 'Va'], writes=[('ps', pOb[qt2])])
                        for qt2 in range(2):
                            po = m.bank(pOb[qt2])[:, 0:129]
                            a = acc[qt2][:, 0:129]
                            if n == b:
                                s.op('dve', lambda e, a=a, po=po: e.tensor_tensor(out=a, in0=a, in1=po, op=ALU.add), reads=[('acc', qt2), ('ps', pOb[qt2])], writes=[('acc', qt2)])
                            else:
                                s.op('dve', lambda e, a=a, po=po, qt2=qt2, b=b, n=n: e.scalar_tensor_tensor(out=a, in0=po, scalar=sel[:, b * 2 + qt2, n:n + 1], in1=a,
                                                                                                       op0=ALU.mult, op1=ALU.add),
                                     reads=[('acc', qt2), ('ps', pOb[qt2]), 'sel'], writes=[('acc', qt2)])
                        it += 1
                    ot = oT[(b // 2) % 2]
                    for qt2 in range(2):
                        s.op('dve', lambda e, qt2=qt2: e.reciprocal(out=rec, in_=acc[qt2][:, 128:129]), reads=[('acc', qt2)], writes=['rec'])
                        s.op('act', lambda e, qt2=qt2: e.activation(out=of, in_=acc[qt2][:, 0:128], func=AF.Copy, scale=rec), reads=[('acc', qt2), 'rec'], writes=['of'])
                        pt = m.bank(4 * lane + 3)[:, 0:128]
                        s.op('pe', lambda e, pt=pt: e.transpose(out=pt, in_=of, identity=ident), reads=['of', 'ident'], writes=[('ps', 4 * lane + 3)])
                        s.op('act', lambda e, pt=pt, ot=ot, qt2=qt2: e.activation(out=ot[:, qt2 * 128:(qt2 + 1) * 128], in_=pt, func=AF.Copy), reads=[('ps', 4 * lane + 3)], writes=[('oT', (b // 2) % 2)])
                    s.dma('sp', o_d[h * 128:(h + 1) * 128, qsl], ot, reads=[('oT', (b // 2) % 2)], is_output=True)
            L = Lanes(s, 2, shared=['KT', 'QT', 'Va', 'Tbig', 'sel', 'identb', 'ident', 'ones_m'])
            L.run([lambda P, ln=ln: lane_body(P, ln) for ln in range(2)])
            s.barrier()
        print("ATTN2 ops", s.n_ops)
        s.emit()
    return nc


class Lanes:
    SHARED = {'ones', 'ones_m', 'tri', 'ident', 'zeros_m', 'bd', 'mIU', 'mSU', 'gn', 'epsb'}

    def __init__(self, s, n, shared=None):
        self.s = s
        self.n = n
        if shared is not None:
            self.SHARED = set(shared)
        self.turn = 0
        self.done = [False] * n
        self.cv = threading.Condition()
        self.err = []

    def _ren(self, lane, keys):
        out = []
        for k in keys:
            if (isinstance(k, tuple) and k and k[0] == 'ps') or (not isinstance(k, tuple) and k in self.SHARED):
                out.append(k)
            else:
                out.append((k, 'lane', lane))
        return out

    def _advance(self, lane):
        for d in range(1, self.n + 1):
            nx = (lane + d) % self.n
            if not self.done[nx]:
                self.turn = nx
                return

    def proxy(self, lane):
        L = self

        class P:
            def op(self_, eng, fn, reads=(), writes=()):
                with L.cv:
                    while L.turn != lane:
                        L.cv.wait()
                    r = L.s.op(eng, fn, L._ren(lane, reads), L._ren(lane, writes))
                    L._advance(lane)
                    L.cv.notify_all()
                return r

            def dma(self_, q, out, in_, reads=(), writes=(), **kw):
                with L.cv:
                    while L.turn != lane:
                        L.cv.wait()
                    r = L.s.dma(q, out, in_, L._ren(lane, reads), L._ren(lane, writes), **kw)
                    L._advance(lane)
                    L.cv.notify_all()
                return r
        return P()

    def run(self, fns):
        def wrap(i, f):
            try:
                f(self.proxy(i))
            except BaseException as ex:
                self.err.append(ex)
            finally:
                with self.cv:
                    self.done[i] = True
                    if self.turn == i:
                        self._advance(i)
                    self.cv.notify_all()
        ths = [threading.Thread(target=wrap, args=(i, f)) for i, f in enumerate(fns)]
        for t in ths:
            t.start()
        for t in ths:
            t.join()
        if self.err:
            raise self.err[0]


def build_gdn2(T=8192):
    nc = new_nc()
    NT = T // 512
    uT_d = nc.dram_tensor("uT", [D, T], BF16, kind="ExternalInput").ap()
    w_d = nc.dram_tensor("w", [2, D, 514], F32, kind="ExternalInput").ap()
    cw_d = nc.dram_tensor("cw", [2, 128, 3, 4], F32, kind="ExternalInput").ap()
    hp_d = nc.dram_tensor("hp", [2, 128, 2], F32, kind="ExternalInput").ap()
    gn_d = nc.dram_tensor("gn", [128, 1], F32, kind="ExternalInput").ap()
    o_d = nc.dram_tensor("o", [256, T], BF16, kind="ExternalOutput").ap()
    with ExitStack() as st:
        m = Mem(nc, st)
        s = Sched(nc)
        ones = setup_consts(s, m)
        ones_m, tri, ident = make_tri_ident(s, m)
        onec = ones_m[:, 0:1]
        zeros_m = m.alloc(128, F32)
        s.op('pool', lambda e: e.memset(zeros_m, 0.0), writes=['zeros_m'])
        bd = m.alloc(128, F32)
        s.op('pool', lambda e: e.memset(bd, 0.0), writes=['bd'])
        s.op('pool', lambda e: e.memset(bd[0:64, 0:64], 1.0), reads=['bd'], writes=['bd'])
        s.op('pool', lambda e: e.memset(bd[64:128, 64:128], 1.0), reads=['bd'], writes=['bd'])
        mIU = m.alloc(128, F32)
        mSU = m.alloc(128, F32)
        s.op('pool', lambda e: e.tensor_tensor(out=mIU, in0=tri, in1=bd, op=ALU.mult), reads=['tri', 'bd'], writes=['mIU'])
        s.op('pool', lambda e: e.tensor_tensor(out=mSU, in0=mIU, in1=ident, op=ALU.subtract), reads=['mIU', 'ident'], writes=['mSU'])
        gn = m.alloc(1, F32)
        s.dma('sp', gn, gn_d, writes=['gn'])
        def alloc_head():
            wb = m.alloc(NK * 514, BF16).rearrange("p (k c) -> p k c", k=NK)
            cw = m.alloc(12, F32).rearrange("p (c k) -> p c k", c=3)
            hp = m.alloc(2, F32)
            negA = m.alloc(1, F32)
            ub = [m.alloc(NK * 512, BF16).rearrange("p (k t) -> p k t", k=NK) for _ in range(2)]
            pre = [m.alloc(515, F32) for _ in range(3)]
            acc = [m.alloc(512, F32) for _ in range(2)]
            qc = m.alloc(512, F32)
            kc = m.alloc(512, F32)
            vcf = m.alloc(512, F32)
            sz = m.alloc(512, F32)
            sq = m.alloc(512, F32)
            rstd = m.alloc(512, F32)
            ofull = m.alloc(512, F32)
            yout = [m.alloc(512, BF16) for _ in range(2)]
            c1 = lambda: m.alloc(1, F32)
            ab, et, spv, gcol, eb, beta, nbeta, gccol, glcol, ecol, bg, kdsc, egl0, egl1, tdiff = [c1() for _ in range(15)]
            ab = m.alloc(2, F32)
            ngccol = m.alloc(1, F32)
            grep = m.alloc(128, F32)
            egcrow = m.alloc(128, F32)
            tg = m.alloc(128, F32)
            ET = m.alloc(128, F32)
            NT0 = m.alloc(128, F32)
            qkT = m.alloc(128, F32)
            X = [m.alloc(128, F32) for _ in range(2)]
            XT = [m.alloc(128, F32) for _ in range(2)]
            PT = [m.alloc(128, F32) for _ in range(2)]
            kbg = m.alloc(128, F32)
            kdec = m.alloc(128, F32)
            vb = m.alloc(128, F32)
            u_sb = m.alloc(128, F32)
            wT = m.alloc(128, F32)
            qdT = m.alloc(128, F32)
            vc_sb = m.alloc(128, F32)
            S = m.alloc(128, F32)
            return dict(locals())
        def head_body(h, s, B, A_, B_, C_, D_):
            wb = B['wb']
            cw = B['cw']
            hp = B['hp']
            negA = B['negA']
            ub = B['ub']
            pre = B['pre']
            acc = B['acc']
            qc = B['qc']
            kc = B['kc']
            vcf = B['vcf']
            sz = B['sz']
            sq = B['sq']
            rstd = B['rstd']
            ofull = B['ofull']
            yout = B['yout']
            ab = B['ab']
            et = B['et']
            spv = B['spv']
            gcol = B['gcol']
            eb = B['eb']
            beta = B['beta']
            nbeta = B['nbeta']
            gccol = B['gccol']
            glcol = B['glcol']
            ecol = B['ecol']
            bg = B['bg']
            kdsc = B['kdsc']
            egl0 = B['egl0']
            egl1 = B['egl1']
            tdiff = B['tdiff']
            ngccol = B['ngccol']
            grep = B['grep']
            egcrow = B['egcrow']
            tg = B['tg']
            ET = B['ET']
            NT0 = B['NT0']
            qkT = B['qkT']
            X = B['X']
            XT = B['XT']
            PT = B['PT']
            kbg = B['kbg']
            kdec = B['kdec']
            vb = B['vb']
            u_sb = B['u_sb']
            wT = B['wT']
            qdT = B['qdT']
            vc_sb = B['vc_sb']
            S = B['S']
            s.dma('pool', wb, w_d[h].rearrange("(k p) c -> p k c", p=128), writes=['wb'])
            s.dma('sp', cw, cw_d[h], writes=['cw'])
            s.dma('sp', hp, hp_d[h], writes=['hp'])
            s.op('act', lambda e: e.activation(out=negA, in_=hp[:, 1:2], func=AF.Exp), reads=['hp'], writes=['negA'])
            s.op('dve', lambda e: e.tensor_scalar(out=negA, in0=negA, scalar1=-1.0, scalar2=None, op0=ALU.mult), reads=['negA'], writes=['negA'])
            s.op('pool', lambda e: e.memset(S, 0.0), writes=['S'])
            for c in range(3):
                s.op('pool', lambda e, c=c: e.memset(pre[c][:, 0:3], 0.0), writes=[('pre', c)])
            for ti in range(NT):
                u = ub[ti % 2]
                ur = ('u', ti % 2)
                s.dma('sp', u, uT_d[:, ti * 512:(ti + 1) * 512].rearrange("(k p) t -> p k t", p=128), writes=[ur])
                dst = [qc, kc, vcf]
                dname = ['qc', 'kc', 'vc']
                for g in range(4):
                    ps = m.bank(A_)
                    for k in range(NK):
                        s.op('pe', lambda e, ps=ps, g=g, k=k, u=u: e.matmul(ps, lhsT=wb[:, k, g * 128:(g + 1) * 128], rhs=u[:, k, :],
                                                                         start=(k == 0), stop=(k == NK - 1)), reads=['wb', ur], writes=[('ps', A_)])
                    if g == 3:
                        s.op('act', lambda e, ps=ps: e.activation(out=sz, in_=ps, func=AF.Silu), reads=[('ps', A_)], writes=['sz'])
                        continue
                    c = g
                    s.op('act', lambda e, ps=ps, c=c: e.activation(out=pre[c][:, 3:515], in_=ps, func=AF.Copy), reads=[('ps', A_)], writes=[('pre', c)])
                    a = acc[c % 2]
                    s.op('dve', lambda e, a=a, c=c: e.tensor_scalar(out=a, in0=pre[c][:, 0:512], scalar1=cw[:, c, 0:1], scalar2=zeros_m[:, 0:1],
                                                                    op0=ALU.mult, op1=ALU.add), reads=[('pre', c), 'cw', 'zeros_m'], writes=[('acc', c % 2)])
                    for kk in range(1, 4):
                        s.op('dve', lambda e, a=a, c=c, kk=kk: e.scalar_tensor_tensor(out=a, in0=pre[c][:, kk:kk + 512], scalar=cw[:, c, kk:kk + 1],
                                                                                      in1=a, op0=ALU.mult, op1=ALU.add),
                             reads=[('pre', c), 'cw', ('acc', c % 2)], writes=[('acc', c % 2)])
                    s.op('pool', lambda e, c=c: e.tensor_copy(out=pre[c][:, 0:3], in_=pre[c][:, 512:515]), reads=[('pre', c)], writes=[('pre', c)])
                    s.op('act', lambda e, a=a, c=c: e.activation(out=dst[c], in_=a, func=AF.Silu), reads=[('acc', c % 2)], writes=[dname[c]])
                    if c < 2:
                        s.op('act', lambda e, c=c: e.activation(out=sq, in_=dst[c], func=AF.Square), reads=[dname[c]], writes=['sq'])
                        p2 = m.bank(B_)
                        s.op('pe', lambda e, p2=p2: e.matmul(p2, lhsT=ones, rhs=sq, start=True, stop=True), reads=['sq', 'ones'], writes=[('ps', B_)])
                        s.op('act', lambda e, p2=p2: e.activation(out=rstd, in_=p2, func=AF.Sqrt, scale=1.0, bias=EPSB[0]), reads=[('ps', B_), 'epsb'], writes=['rstd'])
                        s.op('dve', lambda e: e.reciprocal(out=rstd, in_=rstd), reads=['rstd'], writes=['rstd'])
                        if c == 0:
                            s.op('dve', lambda e: e.scalar_tensor_tensor(out=qc, in0=qc, scalar=128.0 ** -0.5, in1=rstd, op0=ALU.mult, op1=ALU.mult),
                                 reads=['qc', 'rstd'], writes=['qc'])
                        else:
                            s.op('dve', lambda e: e.tensor_tensor(out=kc, in0=kc, in1=rstd, op=ALU.mult), reads=['kc', 'rstd'], writes=['kc'])
                for sti in range(4):
                    cs = slice(sti * 128, (sti + 1) * 128)
                    pab = m.bank(B_)[:, 0:2]
                    for k in range(NK):
                        s.op('pe', lambda e, pab=pab, k=k, u=u, cs=cs: e.matmul(pab, lhsT=u[:, k, cs], rhs=wb[:, k, 512:514], start=(k == 0), stop=(k == NK - 1)),
                             reads=['wb', ur], writes=[('ps', B_)])
                    s.op('act', lambda e, pab=pab: e.activation(out=ab, in_=pab, func=AF.Copy), reads=[('ps', B_)], writes=['ab'])
                    s.op('act', lambda e: e.activation(out=et, in_=ab[:, 0:1], func=AF.Exp, bias=hp[:, 0:1], scale=1.0), reads=['ab', 'hp'], writes=['et'])
                    s.op('act', lambda e: e.activation(out=spv, in_=et, func=AF.Ln, bias=onec, scale=1.0), reads=['et', 'ones_m'], writes=['spv'])
                    s.op('dve', lambda e: e.tensor_tensor(out=gcol, in0=spv, in1=negA, op=ALU.mult), reads=['spv', 'negA'], writes=['gcol'])
                    s.op('act', lambda e: e.activation(out=eb, in_=ab[:, 1:2], func=AF.Exp, scale=-1.0), reads=['ab'], writes=['eb'])
                    s.op('dve', lambda e: e.tensor_scalar(out=eb, in0=eb, scalar1=1.0, scalar2=None, op0=ALU.add), reads=['eb'], writes=['eb'])
                    s.op('dve', lambda e: e.reciprocal(out=beta, in_=eb), reads=['eb'], writes=['beta'])
                    s.op('pool', lambda e: e.tensor_scalar(out=grep, in0=ones_m, scalar1=gcol, scalar2=None, op0=ALU.mult), reads=['ones_m', 'gcol'], writes=['grep'])
                    prow = m.bank(B_)[:, 128:256]
                    plrow = m.bank(B_)[:, 256:384]
                    pcol = m.bank(B_)[:, 8:9]
                    plcol = m.bank(B_)[:, 16:17]
                    s.op('pe', lambda e, prow=prow: e.matmul(prow, lhsT=grep, rhs=mIU, start=True, stop=True), reads=['grep', 'mIU'], writes=[('ps', B_)])
                    s.op('pe', lambda e, plrow=plrow: e.matmul(plrow, lhsT=grep, rhs=bd, start=True, stop=True), reads=['grep', 'bd'], writes=[('ps', B_)])
                    s.op('pe', lambda e, pcol=pcol: e.matmul(pcol, lhsT=mIU, rhs=gcol, start=True, stop=True), reads=['gcol', 'mIU'], writes=[('ps', B_)])
                    s.op('pe', lambda e, plcol=plcol: e.matmul(plcol, lhsT=bd, rhs=gcol, start=True, stop=True), reads=['gcol', 'bd'], writes=[('ps', B_)])
                    s.op('act', lambda e, pcol=pcol: e.activation(out=gccol, in_=pcol, func=AF.Copy), reads=[('ps', B_)], writes=['gccol'])
                    s.op('act', lambda e, pcol=pcol: e.activation(out=ecol, in_=pcol, func=AF.Exp), reads=[('ps', B_)], writes=['ecol'])
                    s.op('dve', lambda e, plcol=plcol: e.tensor_tensor(out=tdiff, in0=plcol, in1=gccol, op=ALU.subtract), reads=[('ps', B_), 'gccol'], writes=['tdiff'])
                    s.op('act', lambda e: e.activation(out=kdsc, in_=tdiff, func=AF.Exp), reads=['tdiff'], writes=['kdsc'])
                    s.op('dve', lambda e: e.tensor_tensor(out=bg, in0=beta, in1=ecol, op=ALU.mult), reads=['beta', 'ecol'], writes=['bg'])
                    s.op('act', lambda e, plrow=plrow: e.activation(out=egl0, in_=plrow[:, 0:1], func=AF.Exp), reads=[('ps', B_)], writes=['egl0'])
                    s.op('act', lambda e, plrow=plrow: e.activation(out=egl1, in_=plrow[:, 64:65], func=AF.Exp), reads=[('ps', B_)], writes=['egl1'])
                    s.op('act', lambda e, prow=prow: e.activation(out=egcrow, in_=prow, func=AF.Exp), reads=[('ps', B_)], writes=['egcrow'])
                    s.op('dve', lambda e: e.tensor_scalar(out=ngccol, in0=gccol, scalar1=-1.0, scalar2=None, op0=ALU.mult), reads=['gccol'], writes=['ngccol'])
                    s.op('dve', lambda e, prow=prow: e.scalar_tensor_tensor(out=tg, in0=prow, scalar=ngccol, in1=zeros_m, op0=ALU.add, op1=ALU.min),
                         reads=[('ps', B_), 'ngccol', 'zeros_m'], writes=['tg'])
                    s.op('act', lambda e: e.activation(out=ET, in_=tg, func=AF.Exp), reads=['tg'], writes=['ET'])
                    pkk = m.bank(C_)[:, 0:128]
                    pqk = m.bank(C_)[:, 128:256]
                    s.op('pe', lambda e, pkk=pkk, cs=cs: e.matmul(pkk, lhsT=kc[:, cs], rhs=kc[:, cs], start=True, stop=True), reads=['kc'], writes=[('ps', C_)])
                    s.op('pe', lambda e, pqk=pqk, cs=cs: e.matmul(pqk, lhsT=kc[:, cs], rhs=qc[:, cs], start=True, stop=True), reads=['kc', 'qc'], writes=[('ps', C_)])
                    s.op('dve', lambda e, pkk=pkk: e.scalar_tensor_tensor(out=NT0, in0=pkk, scalar=-1.0, in1=ET, op0=ALU.mult, op1=ALU.mult),
                         reads=[('ps', C_), 'ET'], writes=['NT0'])
                    s.op('pool', lambda e: e.tensor_tensor(out=NT0, in0=NT0, in1=mSU, op=ALU.mult), reads=['NT0', 'mSU'], writes=['NT0'])
                    s.op('dve', lambda e, pqk=pqk: e.tensor_tensor(out=qkT, in0=pqk, in1=ET, op=ALU.mult), reads=[('ps', C_), 'ET'], writes=['qkT'])
                    s.op('pool', lambda e: e.tensor_tensor(out=qkT, in0=qkT, in1=mIU, op=ALU.mult), reads=['qkT', 'mIU'], writes=['qkT'])
                    pt = m.bank(C_)[:, 384:512]
                    s.op('pe', lambda e, pt=pt: e.transpose(out=pt, in_=NT0, identity=ident), reads=['NT0', 'ident'], writes=[('ps', A_), ('ps', C_), ('ps', D_)])
                    s.op('act', lambda e, pt=pt: e.activation(out=X[0], in_=pt, func=AF.Copy, scale=beta), reads=[('ps', A_), ('ps', C_), ('ps', D_), 'beta'], writes=[('X', 0)])
                    pt2 = m.bank(D_)[:, 0:128]
                    s.op('pe', lambda e, pt2=pt2: e.transpose(out=pt2, in_=X[0], identity=ident), reads=[('X', 0), 'ident'], writes=[('ps', A_), ('ps', D_)])
                    s.op('act', lambda e, pt2=pt2: e.activation(out=XT[0], in_=pt2, func=AF.Copy), reads=[('ps', A_), ('ps', D_)], writes=[('XT', 0)])
                    s.op('dve', lambda e: e.tensor_tensor(out=PT[0], in0=XT[0], in1=ident, op=ALU.add), reads=[('XT', 0), 'ident'], writes=[('PT', 0)])
                    cur = 0
                    for kk in range(1, 6):
                        nx = 1 - cur
                        pa = m.bank(C_)[:, 384:512]
                        s.op('pe', lambda e, pa=pa, cur=cur: e.matmul(pa, lhsT=XT[cur], rhs=X[cur], start=True, stop=True),
                             reads=[('X', cur), ('XT', cur)], writes=[('ps', A_), ('ps', C_), ('ps', D_)])
                        s.op('act', lambda e, pa=pa, nx=nx: e.activation(out=X[nx], in_=pa, func=AF.Copy), reads=[('ps', A_), ('ps', C_), ('ps', D_)], writes=[('X', nx)])
                        if kk < 5:
                            pb_ = m.bank(D_)[:, 0:128]
                            s.op('pe', lambda e, pb_=pb_, cur=cur: e.matmul(pb_, lhsT=X[cur], rhs=XT[cur], start=True, stop=True),
                                 reads=[('X', cur), ('XT', cur)], writes=[('ps', A_), ('ps', D_)])
                            s.op('dve', lambda e, pb_=pb_, nx=nx: e.tensor_copy(out=XT[nx], in_=pb_), reads=[('ps', A_), ('ps', D_)], writes=[('XT', nx)])
                        pc = m.bank(D_)[:, 128:256]
                        s.op('pe', lambda e, pc=pc, nx=nx, cur=cur: e.matmul(pc, lhsT=X[nx], rhs=PT[cur], start=True, stop=True),
                             reads=[('X', nx), ('PT', cur)], writes=[('ps', A_), ('ps', D_)])
                        s.op('dve', lambda e, pc=pc, nx=nx, cur=cur: e.tensor_tensor(out=PT[nx], in0=pc, in1=PT[cur], op=ALU.add),
                             reads=[('ps', A_), ('ps', D_), ('PT', cur)], writes=[('PT', nx)])
                        cur = nx
                    PTf = PT[cur]
                    ptr = ('PT', cur)
                    pk = m.bank(A_)[:, 0:128]
                    s.op('pe', lambda e, pk=pk, cs=cs: e.transpose(out=pk, in_=kc[:, cs], identity=ident), reads=['kc', 'ident'], writes=[('ps', A_), ('ps', C_), ('ps', D_)])
                    s.op('act', lambda e, pk=pk: e.activation(out=kbg, in_=pk, func=AF.Copy, scale=bg), reads=[('ps', A_), ('ps', C_), ('ps', D_), 'bg'], writes=['kbg'])
                    s.op('act', lambda e, pk=pk: e.activation(out=kdec, in_=pk, func=AF.Copy, scale=kdsc), reads=[('ps', A_), ('ps', C_), ('ps', D_), 'kdsc'], writes=['kdec'])
                    pv = m.bank(A_)[:, 128:256]
                    s.op('pe', lambda e, pv=pv, cs=cs: e.transpose(out=pv, in_=vcf[:, cs], identity=ident), reads=['vc', 'ident'], writes=[('ps', A_), ('ps', D_)])
                    s.op('act', lambda e, pv=pv: e.activation(out=vb, in_=pv, func=AF.Copy, scale=beta), reads=[('ps', A_), ('ps', D_), 'beta'], writes=['vb'])
                    pu = m.bank(A_)[:, 256:384]
                    s.op('pe', lambda e, pu=pu, PTf=PTf: e.matmul(pu, lhsT=PTf, rhs=vb, start=True, stop=True), reads=[ptr, 'vb'], writes=[('ps', A_), ('ps', D_)])
                    s.op('act', lambda e, pu=pu: e.activation(out=u_sb, in_=pu, func=AF.Copy), reads=[('ps', A_), ('ps', D_)], writes=['u_sb'])
                    pw = m.bank(A_)[:, 384:512]
                    s.op('pe', lambda e, pw=pw, PTf=PTf: e.matmul(pw, lhsT=kbg, rhs=PTf, start=True, stop=True), reads=[ptr, 'kbg'], writes=[('ps', A_), ('ps', D_)])
                    s.op('dve', lambda e, pw=pw: e.tensor_copy(out=wT, in_=pw), reads=[('ps', A_), ('ps', D_)], writes=['wT'])
                    s.op('pool', lambda e, cs=cs: e.tensor_tensor(out=qdT, in0=qc[:, cs], in1=egcrow, op=ALU.mult), reads=['qc', 'egcrow'], writes=['qdT'])
                    po = m.bank(C_)[:, 256:384]
                    for c in range(2):
                        rows = slice(c * 64, (c + 1) * 64)
                        pvs = m.bank(D_)[rows, 256:384]
                        s.op('pe', lambda e, pvs=pvs, rows=rows: e.matmul(pvs, lhsT=wT[:, rows], rhs=S, start=True, stop=True), reads=['wT', 'S'], writes=[('ps', A_), ('ps', C_), ('ps', D_)])
                        s.op('dve', lambda e, pvs=pvs, rows=rows: e.tensor_tensor(out=vc_sb[rows, :], in0=u_sb[rows, :], in1=pvs, op=ALU.subtract),
                             reads=['u_sb', ('ps', A_), ('ps', C_), ('ps', D_)], writes=['vc_sb'])
                        s.op('pe', lambda e, po=po, rows=rows: e.matmul(po[:, rows], lhsT=S, rhs=qdT[:, rows], start=True, stop=False), reads=['S', 'qdT'], writes=[('ps', C_)])
                        s.op('pe', lambda e, po=po, rows=rows: e.matmul(po[:, rows], lhsT=vc_sb[rows, :], rhs=qkT[rows, rows], start=False, stop=True),
                             reads=['vc_sb', 'qkT'], writes=[('ps', C_)])
                        pks = m.bank(D_)[:, 384:512]
                        s.op('pe', lambda e, pks=pks, rows=rows: e.matmul(pks, lhsT=kdec[rows, :], rhs=vc_sb[rows, :], start=True, stop=True),
                             reads=['kdec', 'vc_sb'], writes=[('ps', A_), ('ps', D_)])
                        eg = egl0 if c == 0 else egl1
                        s.op('dve', lambda e, pks=pks, eg=eg: e.scalar_tensor_tensor(out=S, in0=S, scalar=eg, in1=pks, op0=ALU.mult, op1=ALU.add),
                             reads=['S', 'egl0', 'egl1', ('ps', A_), ('ps', D_)], writes=['S'])
                    s.op('act', lambda e, po=po, cs=cs: e.activation(out=ofull[:, cs], in_=po, func=AF.Copy), reads=[('ps', C_)], writes=['ofull'])
                s.op('act', lambda e: e.activation(out=sq, in_=ofull, func=AF.Square), reads=['ofull'], writes=['sq'])
                p2 = m.bank(B_)
                s.op('pe', lambda e, p2=p2: e.matmul(p2, lhsT=ones, rhs=sq, start=True, stop=True), reads=['sq', 'ones'], writes=[('ps', B_)])
                s.op('act', lambda e, p2=p2: e.activation(out=rstd, in_=p2, func=AF.Sqrt, scale=1.0 / 128, bias=EPSB[0]), reads=[('ps', B_), 'epsb'], writes=['rstd'])
                s.op('dve', lambda e: e.reciprocal(out=rstd, in_=rstd), reads=['rstd'], writes=['rstd'])
                s.op('dve', lambda e: e.scalar_tensor_tensor(out=ofull, in0=ofull, scalar=gn, in1=rstd, op0=ALU.mult, op1=ALU.mult), reads=['ofull', 'gn', 'rstd'], writes=['ofull'])
                yo = yout[ti % 2]
                s.op('pool', lambda e, yo=yo: e.tensor_tensor(out=yo, in0=ofull, in1=sz, op=ALU.mult), reads=['ofull', 'sz'], writes=[('yout', ti % 2)])
                s.dma('sp', o_d[h * 128:(h + 1) * 128, ti * 512:(ti + 1) * 512], yo, reads=[('yout', ti % 2)], is_output=True)
        HB = [alloc_head() for _ in range(2)]
        L = Lanes(s, 2)
        L.run([lambda P, h=h: head_body(h, P, HB[h], 4 * h + 0, 4 * h + 1, 4 * h + 2, 4 * h + 3) for h in range(2)])
        print("GDN2 ops", s.n_ops)
        s.emit()
    return nc


def ssd_inputs(W, l, i, uT_full):
    g = i // 2
    w_in = W['w_in'][l]
    cols = np.concatenate([np.arange(256*i, 256*i+256), 2048 + np.arange(256*i, 256*i+256), 4096 + np.arange(128*g, 128*g+128),
                           4608 + np.arange(128*g, 128*g+128), 5120 + np.arange(4*i, 4*i+4)])
    w = np.ascontiguousarray(w_in[:, cols])
    chans = np.stack([256*i + np.arange(128), 256*i + 128 + np.arange(128), 2048 + 128*g + np.arange(128), 2560 + 128*g + np.arange(128)], axis=1)
    cwl = W['ssd_conv_w'][l]; cbl = W['ssd_conv_b'][l]
    cw = np.zeros((128, 4, 5), np.float32)
    for k in range(4):
        cw[:, :, k] = cwl[k][chans]
    cw[:, :, 4] = cbl[chans]
    hp = np.zeros((128, 3, 16), np.float32)
    hp[:, 0, :] = np.tile(W['ssd_dt_bias'][l][4*i:4*i+4], 4)[None, :]
    hp[:, 1, :] = np.tile(W['ssd_a_log'][l][4*i:4*i+4], 4)[None, :]
    dcol = np.zeros((128, 2), np.float32)
    dl = W['ssd_d'][l]
    for c in range(2):
        dcol[:64, c] = dl[4*i + 2*c]; dcol[64:, c] = dl[4*i + 2*c + 1]
    return {"uT": uT_full, "w": w, "cw": cw, "hp": hp, "dcol": dcol}
def t5_onehot(T):
    n = np.arange(T)
    nf = np.maximum(n, 1).astype(np.float32)
    large = 16 + (np.log(nf / np.float32(16)) / np.float32(math.log(4096 / 16)) * np.float32(16)).astype(np.int32)
    bkt = np.where(n < 16, n, np.minimum(large, 31))
    oh = np.zeros((32, T), np.float32)
    oh[bkt, n] = 1.0
    return oh
def attn_inputs(W, l, i, uT_full, oh):
    w_in = W['w_in'][l]
    ws = []
    for hh in range(2):
        hd = 2 * i + hh
        cols = np.concatenate([13376 + 128 * hd + np.arange(128), 15424 + 128 * hd + np.arange(128), 17472 + 128 * hd + np.arange(128)])
        ws.append(w_in[:, cols])
    gn = np.ascontiguousarray(np.stack([W['attn_q_norm'][l], W['attn_k_norm'][l]], axis=1))
    rb = np.ascontiguousarray(W['rel_bias'][:, 2 * i:2 * i + 2])
    return {"uT": uT_full, "w": np.ascontiguousarray(np.stack(ws)), "gn": gn, "rb": rb, "oh": oh}
def gdn_inputs(W, l, i, uT_full):
    w_in = W['w_in'][l]
    ws, cws, hps = [], [], []
    for hh in range(2):
        hd = 2 * i + hh
        cols = np.concatenate([5152 + 128 * hd + np.arange(128), 7200 + 128 * hd + np.arange(128), 9248 + 128 * hd + np.arange(128),
                               11296 + 128 * hd + np.arange(128), [13344 + hd], [13360 + hd]])
        ws.append(w_in[:, cols])
        chans = np.stack([128 * hd + np.arange(128), 2048 + 128 * hd + np.arange(128), 4096 + 128 * hd + np.arange(128)], axis=1)
        cwl = W['gdn_conv_w'][l]
        cw = np.zeros((128, 3, 4), np.float32)
        for k in range(4):
            cw[:, :, k] = cwl[k][chans]
        cws.append(cw)
        hp = np.zeros((128, 2), np.float32)
        hp[:, 0] = W['gdn_dt_bias'][l][hd]
        hp[:, 1] = W['gdn_a_log'][l][hd]
        hps.append(hp)
    return {"uT": uT_full, "w": np.ascontiguousarray(np.stack(ws)), "cw": np.stack(cws), "hp": np.stack(hps),
            "gn": np.ascontiguousarray(W['gdn_norm'][l].reshape(128, 1))}


_NC = {}


def _get(name, fn):
    if name not in _NC:
        _NC[name] = fn()
    return _NC[name]


def kernel(**I):
    I = {k: np.asarray(v) for k, v in I.items()}
    n = 8
    cores = list(range(n))
    x = I['x']
    c = I['c']
    cT = np.ascontiguousarray(c.reshape(16, 128).T)
    w_mod, b_mod = I['w_mod'], I['b_mod']
    ins = []
    for i in range(n):
        ins.append({"wm": np.ascontiguousarray(w_mod[:, :, i * 2304:(i + 1) * 2304]),
                    "bm": np.ascontiguousarray(np.stack([b_mod[l, i * 2304:(i + 1) * 2304].reshape(18, 128).T for l in range(4)])),
                    "cT": cT})
    res = run_bass_kernel_spmd(_get('mod', build_mod), ins, core_ids=cores)
    modT = np.concatenate([r["modp"] for r in res.results], axis=2)
    hT = [np.ascontiguousarray(x[0, i * TL:(i + 1) * TL, :].T) for i in range(n)]
    oh = t5_onehot(T)
    for l in range(4):
        gains = np.ascontiguousarray(np.concatenate([I['norm_ffn1'][l].reshape(16, 128).T, I['norm_mix'][l].reshape(16, 128).T], axis=1))
        ins = [{"hin": hT[i], "modT": modT[l], "gains": gains, "wg": I['ffn1_w_gate'][l], "wu": I['ffn1_w_up'][l],
                "wd": I['ffn1_w_down'][l]} for i in range(n)]
        res = run_bass_kernel_spmd(_get('A', build_A), ins, core_ids=cores)
        hT = [r["hout"] for r in res.results]
        uT = [r["uout"] for r in res.results]
        uT_full = np.ascontiguousarray(np.concatenate(uT, axis=1))
        ins = [ssd_inputs(I, l, i, uT_full) for i in range(n)]
        res = run_bass_kernel_spmd(_get('ssd', build_ssd), ins, core_ids=cores)
        ys_full = np.concatenate([r["y"] for r in res.results], axis=0)
        ins = [gdn_inputs(I, l, i, uT_full) for i in range(n)]
        res = run_bass_kernel_spmd(_get('gdn', build_gdn2), ins, core_ids=cores)
        yg_full = np.concatenate([r["o"] for r in res.results], axis=0)
        ins = [attn_inputs(I, l, i, uT_full, oh) for i in range(n)]
        res = run_bass_kernel_spmd(_get('attn', build_attn2), ins, core_ids=cores)
        ya_full = np.concatenate([r["o"] for r in res.results], axis=0)
        gains = np.ascontiguousarray(np.concatenate([I['ssd_norm'][l].reshape(16, 128).T, I['norm_ffn2'][l].reshape(16, 128).T], axis=1))
        wgt = np.ascontiguousarray(I['w_in'][l][:, 19520:25664])
        ins = []
        for i in range(n):
            tsl = slice(i * TL, (i + 1) * TL)
            ins.append({"hin": hT[i], "uin": uT[i], "ys": np.ascontiguousarray(ys_full[:, tsl]), "yg": np.ascontiguousarray(yg_full[:, tsl]),
                        "ya": np.ascontiguousarray(ya_full[:, tsl]),
                        "modT": modT[l], "gains": gains, "wos": I['w_o_ssd'][l], "wog": I['w_o_gdn'][l], "woa": I['w_o_attn'][l],
                        "wgt": wgt, "wout": I['w_out'][l], "wg": I['ffn2_w_gate'][l], "wu": I['ffn2_w_up'][l], "wd": I['ffn2_w_down'][l]})
        res = run_bass_kernel_spmd(_get('C', build_C), ins, core_ids=cores)
        hT = [r["hout"] for r in res.results]
    out = np.concatenate([h.T for h in hT], axis=0)[None]
    return np.ascontiguousarray(out.astype(np.float32))
```

```python
import math
import threading
import numpy as np
from contextlib import ExitStack
import concourse.bass as bass
import concourse.mybir as mybir
from concourse.bass_utils import run_bass_kernel_spmd
import ml_dtypes

F32 = mybir.dt.float32
BF16 = mybir.dt.bfloat16
I32 = mybir.dt.int32
AF = mybir.ActivationFunctionType
ALU = mybir.AluOpType
AX = mybir.AxisListType
NPBF16 = ml_dtypes.bfloat16

SAME_ENGINE_SYNC = True
EPOCH_MAX = 30000


class Sched:
    ENGS = ('pe', 'act', 'dve', 'pool', 'sp')

    def __init__(self, nc, ndma=8):
        self.nc = nc
        self.streams = {e: [] for e in self.ENGS}
        self.cnt = {}
        self.known = {e: {} for e in self.ENGS}
        self.lastw = {}
        self.readers = {}
        self.epoch = {e: 0 for e in self.ENGS}
        self.dma_rr = {e: 0 for e in self.ENGS}
        self.dma_epoch = {}
        self.ndma = ndma
        self.keys = []
        self.keyset = set()
        self.out_tokens = []
        self.n_ops = 0

    def _key(self, k):
        if k not in self.keyset:
            self.keyset.add(k)
            self.keys.append(k)
        return k

    def _deps(self, eng, reads, writes, extra=()):
        need = {}

        def add(t):
            if t is None:
                return
            k, v = t
            if not SAME_ENGINE_SYNC or eng == 'pe':
                if k[0] == 'e' and k[1] == eng:
                    return
            if need.get(k, 0) < v:
                need[k] = v
        for r in reads:
            add(self.lastw.get(r))
        for w in writes:
            add(self.lastw.get(w))
            rd = self.readers.get(w)
            if rd:
                for k, v in rd.items():
                    add((k, v))
        for t in extra:
            add(t)
        waits = []
        kn = self.known[eng]
        for k, v in need.items():
            if kn.get(k, 0) >= v:
                continue
            kn[k] = v
            waits.append((k, v))
        return waits

    def _commit(self, tok, reads, writes):
        k, v = tok
        for r in reads:
            d = self.readers.setdefault(r, {})
            if d.get(k, 0) < v:
                d[k] = v
        for w in writes:
            self.lastw[w] = tok
            self.readers[w] = {}

    def op(self, eng, fn, reads=(), writes=()):
        psr = [r for r in reads if isinstance(r, tuple) and r and r[0] == 'ps']
        if psr:
            writes = list(writes) + [r for r in psr if r not in writes]
        waits = self._deps(eng, reads, writes)
        key = self._key(('e', eng, self.epoch[eng]))
        val = self.cnt.get(key, 0) + 1
        self.cnt[key] = val
        if val >= EPOCH_MAX:
            self.epoch[eng] += 1
        tok = (key, val)
        self.streams[eng].append((waits, fn, key, 1))
        self._commit(tok, reads, writes)
        self.n_ops += 1
        return tok

    def dma(self, q, out, in_, reads=(), writes=(), is_output=False, **kw):
        j = self.dma_rr[q] % self.ndma
        self.dma_rr[q] += 1
        ep = self.dma_epoch.get((q, j), 0)
        key = ('d', q, j, ep)
        prev = self.cnt.get(key, 0)
        extra = []
        if prev > 0:
            extra.append((key, prev))
        elif ep > 0:
            pk = ('d', q, j, ep - 1)
            extra.append((pk, self.cnt[pk]))
        waits = self._deps(q, reads, writes, extra)
        self._key(key)
        val = prev + 16
        self.cnt[key] = val
        if val >= EPOCH_MAX:
            self.dma_epoch[(q, j)] = ep + 1
        tok = (key, val)

        def fn(eng, out=out, in_=in_, kw=kw):
            return eng.dma_start(out=out, in_=in_, **kw)
        self.streams[q].append((waits, fn, key, 16))
        self._commit(tok, reads, writes)
        if is_output:
            self.out_tokens.append(tok)
        self.n_ops += 1
        return tok

    def coll(self, kind, ins, outs, reads=(), writes=(), groups=None):
        q = 'pool'
        j = self.dma_rr[q] % self.ndma
        self.dma_rr[q] += 1
        ep = self.dma_epoch.get((q, j), 0)
        key = ('d', q, j, ep)
        prev = self.cnt.get(key, 0)
        extra = [(key, prev)] if prev > 0 else []
        waits = self._deps(q, reads, writes, extra)
        self._key(key)
        val = prev + 16
        self.cnt[key] = val
        if val >= EPOCH_MAX:
            self.dma_epoch[(q, j)] = ep + 1
        tok = (key, val)
        groups = groups or [list(range(8))]

        def fn(eng):
            return eng.collective_compute(kind, ALU.bypass, replica_groups=groups, ins=ins, outs=outs)
        self.streams[q].append((waits, fn, key, 16))
        self._commit(tok, reads, writes)
        self.n_ops += 1
        return tok

    def barrier(self):
        toks = [(k, v) for k, v in self.cnt.items()]
        for e in self.ENGS:
            waits = []
            kn = self.known[e]
            for k, v in toks:
                if k[0] == 'e' and k[1] == e:
                    continue
                if kn.get(k, 0) >= v:
                    continue
                kn[k] = v
                waits.append((k, v))
            if waits:
                self.streams[e].append((waits, None, None, 0))
        self.lastw = {}
        self.readers = {}

    def emit(self):
        nc = self.nc
        self.barrier()
        with ExitStack() as st:
            sems = {}
            for i, k in enumerate(self.keys):
                sems[k] = st.enter_context(nc.semaphore("s%d" % i))
            with nc.Block() as block:
                def mk(name):
                    def f(eng):
                        for (waits, fn, key, inc) in self.streams[name]:
                            for (k, v) in waits:
                                eng.wait_ge(sems[k], v)
                            if fn is not None:
                                ins = fn(eng)
                                ins.then_inc(sems[key], inc)
                    return f
                block.tensor(mk('pe'))
                block.scalar(mk('act'))
                block.vector(mk('dve'))
                block.gpsimd(mk('pool'))
                block.sync(mk('sp'))


class Mem:
    def __init__(self, nc, st, sbuf_bytes=200 * 1024):
        self.nc = nc
        self.big = st.enter_context(nc.sbuf_tensor("big", [128, sbuf_bytes // 4], F32))
        self.ps = st.enter_context(nc.psum_tensor("psall", [128, 8 * 512], F32))
        self.off = 0
        self.cap = sbuf_bytes
        self.marks = []

    def push(self):
        self.marks.append(self.off)

    def pop(self):
        self.off = self.marks.pop()

    def alloc(self, free_elems, dtype, parts=128):
        esz = 2 if dtype == BF16 else 4
        nbytes = (free_elems * esz + 63) // 64 * 64
        o = self.off
        self.off += nbytes
        assert self.off <= self.cap, "SBUF overflow %d" % self.off
        v = self.big[0:parts, o // 4:(o + nbytes) // 4]
        if dtype != F32:
            v = v.bitcast(dtype)
        return v[:, 0:free_elems]

    def bank(self, b, dtype=F32, parts=128):
        v = self.ps[0:parts, b * 512:(b + 1) * 512]
        if dtype != F32:
            v = v.bitcast(dtype)
        return v


D = 2048
DFF = 5632
TL = 1024
NK = 16
EPS = 1e-6


def new_nc():
    return bass.Bass("TRN2", target_bir_lowering=False)


def build_mod():
    nc = new_nc()
    wm = nc.dram_tensor("wm", [4, D, 2304], F32, kind="ExternalInput").ap()
    bm = nc.dram_tensor("bm", [4, 128, 18], F32, kind="ExternalInput").ap()
    cT = nc.dram_tensor("cT", [128, NK], F32, kind="ExternalInput").ap()
    out = nc.dram_tensor("modp", [4, 128, 18], F32, kind="ExternalOutput").ap()
    with ExitStack() as st:
        m = Mem(nc, st)
        s = Sched(nc)
        c_sb = m.alloc(NK, F32)
        s.dma('sp', c_sb, cT, writes=['c'])
        b_sb = m.alloc(4 * 18, F32)
        for l in range(4):
            s.dma('sp', b_sb[:, l * 18:(l + 1) * 18], bm[l], writes=[('b', l)])
        o_sb = m.alloc(4 * 18, F32)
        wt = [m.alloc(NK * 384, F32) for _ in range(2)]
        it = 0
        for l in range(4):
            for cb in range(6):
                w = wt[it % 2]
                w3 = w.rearrange("p (k c) -> p k c", k=NK)
                src = wm[l][:, cb * 384:(cb + 1) * 384].rearrange("(k p) c -> p k c", p=128)
                s.dma('sp' if it % 2 == 0 else 'act', w3, src, writes=[('w', it % 2)])
                for cc in range(3):
                    col = cb * 3 + cc
                    ps = m.bank(l % 2)[:, col:col + 1]
                    for k in range(NK):
                        s.op('pe', lambda e, ps=ps, w3=w3, k=k, cc=cc: e.matmul(
                            ps, lhsT=w3[:, k, cc * 128:(cc + 1) * 128], rhs=c_sb[:, k:k + 1],
                            start=(k == 0), stop=(k == NK - 1)),
                            reads=[('w', it % 2), 'c'], writes=[('ps', l % 2)])
                it += 1
            s.op('dve', lambda e, l=l: e.tensor_tensor(out=o_sb[:, l * 18:(l + 1) * 18], in0=m.bank(l % 2)[:, 0:18],
                                                       in1=b_sb[:, l * 18:(l + 1) * 18], op=ALU.add),
                 reads=[('ps', l % 2), ('b', l)], writes=[('o', l)])
            s.dma('sp', out[l], o_sb[:, l * 18:(l + 1) * 18], reads=[('o', l)], is_output=True)
        s.emit()
    return nc


def rms_modulate(s, m, hT, uT, gs, sh, tmpb, tag, ones_f32, psb):
    sq, rstd, tmp = tmpb
    for t in range(TL // 512):
        tsl = slice(t * 512, (t + 1) * 512)
        ps = m.bank(psb)
        for k in range(NK):
            sqb = sq[k % 2]
            s.op('act', lambda e, sqb=sqb, k=k, tsl=tsl: e.activation(out=sqb, in_=hT[:, k, tsl], func=AF.Square),
                 reads=[('hT', k, t)], writes=[('sq', k % 2)])
            s.op('pe', lambda e, ps=ps, sqb=sqb, k=k: e.matmul(ps, lhsT=ones_f32, rhs=sqb, start=(k == 0), stop=(k == NK - 1)),
                 reads=[('sq', k % 2), 'ones'], writes=[('ps', psb)])
        s.op('act', lambda e, ps=ps: e.activation(out=rstd, in_=ps, func=AF.Sqrt, scale=1.0 / D, bias=EPSB[0]),
             reads=[('ps', psb), 'epsb'], writes=['rstd'])
        s.op('dve', lambda e: e.reciprocal(out=rstd, in_=rstd), reads=['rstd'], writes=['rstd'])
        for k in range(NK):
            tb = tmp[k % 2]
            s.op('dve', lambda e, tb=tb, k=k, tsl=tsl: e.tensor_tensor(out=tb, in0=hT[:, k, tsl], in1=rstd, op=ALU.mult),
                 reads=[('hT', k, t), 'rstd'], writes=[('tmp', k % 2)])
            s.op('act', lambda e, tb=tb, k=k, tsl=tsl: e.activation(out=uT[:, k, tsl], in_=tb, func=AF.Identity,
                                                                    scale=gs[:, k:k + 1], bias=sh[:, k:k + 1]),
                 reads=[('tmp', k % 2), tag], writes=[('uT', t)])


EPSB = [None]


def ffn(s, m, hT, uT, utag, wg, wu, wd, ghalf, bufs):
    wgb, wub, wdb, hh, sg = bufs
    NP = DFF // 256
    for j in range(NP):
        b = j % 2
        wg3 = wgb[b].rearrange("p (k c) -> p k c", k=NK)
        wu3 = wub[b].rearrange("p (k c) -> p k c", k=NK)
        wd3 = wdb[b].rearrange("p (j o) -> p j o", j=2)
        hh3 = hh[b].rearrange("p (j t) -> p j t", j=2)
        s.dma('pool', wg3, wg[:, j * 256:(j + 1) * 256].rearrange("(k p) c -> p k c", p=128), writes=[('wg', b)])
        s.dma('pool', wu3, wu[:, j * 256:(j + 1) * 256].rearrange("(k p) c -> p k c", p=128), writes=[('wu', b)])
        s.dma('pool', wd3, wd[j * 256:(j + 1) * 256, :].rearrange("(j p) o -> p j o", p=128), writes=[('wd', b)])
        for t in range(TL // 512):
            tsl = slice(t * 512, (t + 1) * 512)
            for jj in range(2):
                bg = (t * 2 + jj) % 2
                pg = m.bank(bg)
                pu = m.bank(2 + bg)
                for k in range(NK):
                    s.op('pe', lambda e, pg=pg, wg3=wg3, k=k, jj=jj, tsl=tsl: e.matmul(
                        pg, lhsT=wg3[:, k, jj * 128:(jj + 1) * 128], rhs=uT[:, k, tsl], start=(k == 0), stop=(k == NK - 1)),
                        reads=[('wg', b), (utag, t)], writes=[('ps', bg)])
                for k in range(NK):
                    s.op('pe', lambda e, pu=pu, wu3=wu3, k=k, jj=jj, tsl=tsl: e.matmul(
                        pu, lhsT=wu3[:, k, jj * 128:(jj + 1) * 128], rhs=uT[:, k, tsl], start=(k == 0), stop=(k == NK - 1)),
                        reads=[('wu', b), (utag, t)], writes=[('ps', 2 + bg)])
                sgb = sg[bg]
                s.op('act', lambda e, sgb=sgb, pg=pg: e.activation(out=sgb, in_=pg, func=AF.Silu),
                     reads=[('ps', bg)], writes=[('sg', bg)])
                s.op('dve', lambda e, sgb=sgb, pu=pu, hh3=hh3, jj=jj, tsl=tsl: e.tensor_tensor(
                    out=hh3[:, jj, tsl], in0=sgb, in1=pu, op=ALU.mult),
                    reads=[('sg', bg), ('ps', 2 + bg)], writes=[('hh', b, jj, t)])
        i = 0
        for o in range(NK):
            for t in range(TL // 512):
                tsl = slice(t * 512, (t + 1) * 512)
                pb = 4 + (i % 4)
                i += 1
                pd = m.bank(pb)
                for jj in range(2):
                    s.op('pe', lambda e, pd=pd, wd3=wd3, jj=jj, o=o, hh3=hh3, tsl=tsl: e.matmul(
                        pd, lhsT=wd3[:, jj, o * 128:(o + 1) * 128], rhs=hh3[:, jj, tsl], start=(jj == 0), stop=(jj == 1)),
                        reads=[('wd', b), ('hh', b, jj, t)], writes=[('ps', pb)])
                s.op('dve', lambda e, pd=pd, o=o, tsl=tsl: e.scalar_tensor_tensor(
                    out=hT[:, o, tsl], in0=pd, scalar=ghalf[:, o:o + 1], in1=hT[:, o, tsl], op0=ALU.mult, op1=ALU.add),
                    reads=[('ps', pb), ('hT', o, t), 'ghalf'], writes=[('hT', o, t)])


def setup_consts(s, m):
    ones = m.alloc(128, F32)
    s.op('dve', lambda e: e.memset(ones, 1.0), writes=['ones'])
    epsb = m.alloc(1, F32)
    s.op('dve', lambda e: e.memset(epsb, EPS), writes=['epsb'])
    EPSB[0] = epsb
    return ones


def build_A():
    nc = new_nc()
    hin = nc.dram_tensor("hin", [D, TL], F32, kind="ExternalInput").ap()
    modT = nc.dram_tensor("modT", [128, 144], F32, kind="ExternalInput").ap()
    gains = nc.dram_tensor("gains", [128, 2 * NK], F32, kind="ExternalInput").ap()
    wg = nc.dram_tensor("wg", [D, DFF], F32, kind="ExternalInput").ap()
    wu = nc.dram_tensor("wu", [D, DFF], F32, kind="ExternalInput").ap()
    wd = nc.dram_tensor("wd", [DFF, D], F32, kind="ExternalInput").ap()
    hout = nc.dram_tensor("hout", [D, TL], F32, kind="ExternalOutput").ap()
    uout = nc.dram_tensor("uout", [D, TL], BF16, kind="ExternalOutput").ap()
    with ExitStack() as st:
        m = Mem(nc, st)
        s = Sched(nc)
        ones = setup_consts(s, m)
        hT = m.alloc(NK * TL, F32).rearrange("p (k t) -> p k t", k=NK)
        uT = m.alloc(NK * TL, BF16).rearrange("p (k t) -> p k t", k=NK)
        mod = m.alloc(144, F32)
        gn = m.alloc(2 * NK, F32)
        gs = m.alloc(2 * NK, F32)
        gh = m.alloc(NK, F32)
        tmpb = ([m.alloc(512, F32) for _ in range(2)], m.alloc(512, F32), [m.alloc(512, F32) for _ in range(2)])
        bufs = ([m.alloc(NK * 256, BF16) for _ in range(2)], [m.alloc(NK * 256, BF16) for _ in range(2)],
                [m.alloc(2 * D, BF16) for _ in range(2)], [m.alloc(2 * TL, BF16) for _ in range(2)],
                [m.alloc(512, F32) for _ in range(2)])
        for k in range(NK):
            s.dma('sp', hT[:, k, :], hin[k * 128:(k + 1) * 128, :], writes=[('hT', k, 0), ('hT', k, 1)])
        s.dma('sp', mod, modT, writes=['mod'])
        s.dma('sp', gn, gains, writes=['gn'])
        s.op('dve', lambda e: e.scalar_tensor_tensor(out=gs[:, 0:NK], in0=mod[:, 16:32], scalar=1.0, in1=gn[:, 0:NK],
                                                     op0=ALU.add, op1=ALU.mult), reads=['mod', 'gn'], writes=['u1gs'])
        s.op('dve', lambda e: e.scalar_tensor_tensor(out=gs[:, NK:2 * NK], in0=mod[:, 64:80], scalar=1.0, in1=gn[:, NK:2 * NK],
                                                     op0=ALU.add, op1=ALU.mult), reads=['mod', 'gn'], writes=['u2gs'])
        s.op('dve', lambda e: e.tensor_scalar(out=gh, in0=mod[:, 32:48], scalar1=0.5, scalar2=None, op0=ALU.mult),
             reads=['mod'], writes=['ghalf'])
        rms_modulate(s, m, hT, uT, gs[:, 0:NK], mod[:, 0:16], tmpb, 'u1gs', ones, 7)
        ffn(s, m, hT, uT, 'uT', wg, wu, wd, gh, bufs)
        rms_modulate(s, m, hT, uT, gs[:, NK:2 * NK], mod[:, 48:64], tmpb, 'u2gs', ones, 7)
        for k in range(NK):
            s.dma('sp', hout[k * 128:(k + 1) * 128, :], hT[:, k, :], reads=[('hT', k, 0), ('hT', k, 1)], is_output=True)
            s.dma('sp', uout[k * 128:(k + 1) * 128, :], uT[:, k, :], reads=[('uT', 0), ('uT', 1)], is_output=True)
        print("A ops", s.n_ops)
        s.emit()
    return nc


T = 8192


def make_tri_ident(s, m):
    ones_m = m.alloc(128, F32)
    tri = m.alloc(128, F32)
    ident = m.alloc(128, F32)
    s.op('pool', lambda e: e.memset(ones_m, 1.0), writes=['ones_m'])
    s.op('pool', lambda e: e.affine_select(out=tri, in_=ones_m, pattern=[[1, 128]], compare_op=ALU.is_ge, fill=0.0,
                                           base=0, channel_multiplier=-1), reads=['ones_m'], writes=['tri'])
    s.op('pool', lambda e: e.affine_select(out=ident, in_=ones_m, pattern=[[1, 128]], compare_op=ALU.is_equal, fill=0.0,
                                           base=0, channel_multiplier=-1), reads=['ones_m'], writes=['ident'])
    return ones_m, tri, ident


def build_ssd(T=8192):
    nc = new_nc()
    uT_d = nc.dram_tensor("uT", [D, T], BF16, kind="ExternalInput").ap()
    w_d = nc.dram_tensor("w", [D, 772], F32, kind="ExternalInput").ap()
    cw_d = nc.dram_tensor("cw", [128, 4, 5], F32, kind="ExternalInput").ap()
    hp_d = nc.dram_tensor("hp", [128, 3, 16], F32, kind="ExternalInput").ap()
    dc_d = nc.dram_tensor("dcol", [128, 2], F32, kind="ExternalInput").ap()
    y_d = nc.dram_tensor("y", [256, T], BF16, kind="ExternalOutput").ap()
    with ExitStack() as st:
        m = Mem(nc, st)
        s = Sched(nc)
        ones_m, tri, ident = make_tri_ident(s, m)
        onec = ones_m[:, 0:1]
        zeros_m = m.alloc(128, F32)
        s.op('pool', lambda e: e.memset(zeros_m, 0.0), writes=['zeros_m'])
        wb = m.alloc(NK * 772, BF16).rearrange("p (k c) -> p k c", k=NK)
        s.dma('pool', wb, w_d.rearrange("(k p) c -> p k c", p=128), writes=['wb'])
        cw = m.alloc(20, F32).rearrange("p (c k) -> p c k", c=4)
        s.dma('sp', cw, cw_d, writes=['cw'])
        hp = m.alloc(48, F32).rearrange("p (a b) -> p a b", a=3)
        s.dma('sp', hp, hp_d, writes=['hp'])
        dcol = m.alloc(2, F32)
        s.dma('sp', dcol, dc_d, writes=['dcol'])
        a16 = m.alloc(16, F32)
        s.op('act', lambda e: e.activation(out=a16, in_=hp[:, 1, :], func=AF.Exp), reads=['hp'], writes=['a16'])
        s.op('dve', lambda e: e.tensor_scalar(out=a16, in0=a16, scalar1=-1.0, scalar2=None, op0=ALU.mult), reads=['a16'], writes=['a16'])
        ub = [m.alloc(NK * 512, BF16).rearrange("p (k t) -> p k t", k=NK) for _ in range(2)]
        pre = [m.alloc(515, F32) for _ in range(4)]
        for c in range(4):
            s.op('pool', lambda e, c=c: e.memset(pre[c][:, 0:3], 0.0), writes=[('pre', c)])
        acc = [m.alloc(512, F32) for _ in range(2)]
        xT = m.alloc(2 * 512, F32).rearrange("p (c t) -> p c t", c=2)
        BT = m.alloc(512, F32)
        BTb = m.alloc(512, BF16)
        CTf = m.alloc(512, F32)
        CTb = m.alloc(512, BF16)
        sz = m.alloc(2 * 512, F32).rearrange("p (c t) -> p c t", c=2)
        dt_sb = m.alloc(16, F32)
        da_sb = m.alloc(16, F32)
        et = m.alloc(16, F32)
        S_f = m.alloc(256, F32)
        S_b = m.alloc(256, BF16)
        s.op('pool', lambda e: e.memset(S_f, 0.0), writes=['Sf'])
        s.op('pool', lambda e: e.memset(S_b, 0.0), writes=['Sb'])
        darep = [m.alloc(128, F32) for _ in range(2)]
        MG = m.alloc(128, F32)
        targ = [m.alloc(128, F32) for _ in range(2)]
        MTb = [m.alloc(128, BF16) for _ in range(2)]
        eacs = m.alloc(512, F32).rearrange("p (h l) -> p h l", h=4)
        CpT = [m.alloc(128, BF16) for _ in range(2)]
        acol = m.alloc(4, F32)
        nacol = m.alloc(4, F32)
        dec = m.alloc(4, F32)
        dtdec = m.alloc(4, F32)
        cdec = m.alloc(4, F32)
        xdt = m.alloc(256, BF16)
        xdd = m.alloc(256, BF16)
        Btok = m.alloc(128, BF16)
        ytmp = m.alloc(256, F32).rearrange("p (c l) -> p c l", c=2)
        yout = [m.alloc(2 * 512, BF16).rearrange("p (c t) -> p c t", c=2) for _ in range(2)]
        Stmp = m.alloc(256, F32)
        NT = T // 512
        for ti in range(NT):
            u = ub[ti % 2]
            ur = ('u', ti % 2)
            s.dma('sp', u, uT_d[:, ti * 512:(ti + 1) * 512].rearrange("(k p) t -> p k t", p=128), writes=[ur])
            for g in range(6):
                pb = g % 2
                ps = m.bank(pb)
                for k in range(NK):
                    s.op('pe', lambda e, ps=ps, g=g, k=k, u=u: e.matmul(ps, lhsT=wb[:, k, g * 128:(g + 1) * 128], rhs=u[:, k, :],
                                                                     start=(k == 0), stop=(k == NK - 1)),
                         reads=['wb', ur], writes=[('ps', pb)])
                if g < 2:
                    s.op('act', lambda e, ps=ps, g=g: e.activation(out=sz[:, g, :], in_=ps, func=AF.Silu),
                         reads=[('ps', pb)], writes=[('sz', g)])
                else:
                    c = g - 2
                    s.op('act', lambda e, ps=ps, c=c: e.activation(out=pre[c][:, 3:515], in_=ps, func=AF.Copy),
                         reads=[('ps', pb)], writes=[('pre', c)])
                    a = acc[c % 2]
                    s.op('dve', lambda e, a=a, c=c: e.tensor_scalar(out=a, in0=pre[c][:, 0:512], scalar1=cw[:, c, 0:1], scalar2=cw[:, c, 4:5],
                                                                    op0=ALU.mult, op1=ALU.add), reads=[('pre', c), 'cw'], writes=[('acc', c % 2)])
                    for kk in range(1, 4):
                        s.op('dve', lambda e, a=a, c=c, kk=kk: e.scalar_tensor_tensor(out=a, in0=pre[c][:, kk:kk + 512], scalar=cw[:, c, kk:kk + 1],
                                                                                      in1=a, op0=ALU.mult, op1=ALU.add),
                             reads=[('pre', c), 'cw', ('acc', c % 2)], writes=[('acc', c % 2)])
                    s.op('pool', lambda e, c=c: e.tensor_copy(out=pre[c][:, 0:3], in_=pre[c][:, 512:515]), reads=[('pre', c)], writes=[('pre', c)])
                    if c < 2:
                        s.op('act', lambda e, a=a, c=c: e.activation(out=xT[:, c, :], in_=a, func=AF.Silu), reads=[('acc', c % 2)], writes=[('xT', c)])
                    elif c == 2:
                        s.op('act', lambda e, a=a: e.activation(out=BT, in_=a, func=AF.Silu), reads=[('acc', 0)], writes=['BT'])
                        s.op('pool', lambda e: e.tensor_copy(out=BTb, in_=BT), reads=['BT'], writes=['BTb'])
                    else:
                        s.op('act', lambda e, a=a: e.activation(out=CTf, in_=a, func=AF.Silu), reads=[('acc', 1)], writes=['CTf'])
                        s.op('pool', lambda e: e.tensor_copy(out=CTb, in_=CTf), reads=['CTf'], writes=['CTb'])
            pdt = m.bank(2)
            for ch in range(4):
                for k in range(NK):
                    s.op('pe', lambda e, ch=ch, k=k, u=u: e.matmul(pdt[:, ch * 4:(ch + 1) * 4], lhsT=u[:, k, ch * 128:(ch + 1) * 128],
                                                               rhs=wb[:, k, 768:772], start=(k == 0), stop=(k == NK - 1)),
                         reads=['wb', ur], writes=[('ps', 2)])
            s.op('dve', lambda e: e.tensor_tensor(out=et, in0=pdt[:, 0:16], in1=hp[:, 0, :], op=ALU.add), reads=[('ps', 2), 'hp'], writes=['et'])
            s.op('act', lambda e: e.activation(out=et, in_=et, func=AF.Exp), reads=['et'], writes=['et'])
            s.op('act', lambda e: e.activation(out=dt_sb, in_=et, func=AF.Ln, bias=onec, scale=1.0), reads=['et', 'ones_m'], writes=['dt'])
            s.op('dve', lambda e: e.tensor_tensor(out=da_sb, in0=dt_sb, in1=a16, op=ALU.mult), reads=['dt', 'a16'], writes=['da'])
            for ch in range(4):
                csl = slice(ch * 128, (ch + 1) * 128)
                dsl = slice(ch * 4, (ch + 1) * 4)
                pG = m.bank(3)[:, 0:128]
                s.op('pe', lambda e, pG=pG, csl=csl: e.matmul(pG, lhsT=BTb[:, csl], rhs=CTb[:, csl], start=True, stop=True),
                     reads=['BTb', 'CTb'], writes=[('ps', 3)])
                s.op('dve', lambda e, pG=pG: e.tensor_tensor(out=MG, in0=pG, in1=tri, op=ALU.mult), reads=[('ps', 3), 'tri'], writes=['MG'])
                pcol = m.bank(5)[:, 384:388]
                s.op('pe', lambda e, pcol=pcol, dsl=dsl: e.matmul(pcol, lhsT=tri, rhs=da_sb[:, dsl], start=True, stop=True),
                     reads=['tri', 'da'], writes=[('ps', 5)])
                palast = m.bank(5)[:, 388:392]
                s.op('pe', lambda e, palast=palast, dsl=dsl: e.matmul(palast, lhsT=ones_m, rhs=da_sb[:, dsl], start=True, stop=True),
                     reads=['ones_m', 'da'], writes=[('ps', 5)])
                s.op('dve', lambda e, pcol=pcol: e.tensor_copy(out=acol, in_=pcol), reads=[('ps', 5)], writes=['acol'])
                s.op('dve', lambda e: e.tensor_scalar(out=nacol, in0=acol, scalar1=-1.0, scalar2=None, op0=ALU.mult), reads=['acol'], writes=['nacol'])
                prow = m.bank(4).rearrange("p (h l) -> p h l", h=4)
                for h in range(4):
                    dr = darep[h % 2]
                    s.op('pool', lambda e, dr=dr, h=h, ch=ch: e.tensor_scalar(out=dr, in0=ones_m, scalar1=da_sb[:, ch * 4 + h:ch * 4 + h + 1], scalar2=None,
                                                                           op0=ALU.mult), reads=['ones_m', 'da'], writes=[('darep', h % 2)])
                    s.op('pe', lambda e, dr=dr, h=h: e.matmul(prow[:, h, :], lhsT=dr, rhs=tri, start=True, stop=True),
                         reads=[('darep', h % 2), 'tri'], writes=[('ps', 4)])
                s.op('act', lambda e: e.activation(out=eacs, in_=prow, func=AF.Exp), reads=[('ps', 4)], writes=['eacs'])
                s.op('dve', lambda e, palast=palast: e.tensor_tensor(out=dec, in0=palast, in1=acol, op=ALU.subtract), reads=[('ps', 5), 'acol'], writes=['dec'])
                s.op('act', lambda e: e.activation(out=dec, in_=dec, func=AF.Exp), reads=['dec'], writes=['dec'])
                s.op('dve', lambda e, dsl=dsl: e.tensor_tensor(out=dtdec, in0=dec, in1=dt_sb[:, dsl], op=ALU.mult), reads=['dec', 'dt'], writes=['dtdec'])
                s.op('act', lambda e, palast=palast: e.activation(out=cdec, in_=palast, func=AF.Exp), reads=[('ps', 5)], writes=['cdec'])
                pX = m.bank(5)[:, 0:256]
                for c in range(2):
                    s.op('pe', lambda e, c=c, csl=csl: e.transpose(out=pX[:, c * 128:(c + 1) * 128], in_=xT[:, c, csl], identity=ident),
                         reads=[('xT', c), 'ident'], writes=[('ps', 5)])
                pB = m.bank(5)[:, 256:384]
                s.op('pe', lambda e, csl=csl: e.transpose(out=pB, in_=BT[:, csl], identity=ident), reads=['BT', 'ident'], writes=[('ps', 5)])
                s.op('act', lambda e: e.activation(out=Btok, in_=pB, func=AF.Copy), reads=[('ps', 5)], writes=['Btok'])
                for h in range(4):
                    hs = slice(h * 64, (h + 1) * 64)
                    dcolm = dt_sb[:, ch * 4 + h:ch * 4 + h + 1]
                    s.op('act', lambda e, hs=hs, dcolm=dcolm: e.activation(out=xdt[:, hs], in_=pX[:, hs], func=AF.Copy, scale=dcolm),
                         reads=[('ps', 5), 'dt'], writes=['xdt'])
                    s.op('act', lambda e, hs=hs, h=h: e.activation(out=xdd[:, hs], in_=pX[:, hs], func=AF.Copy, scale=dtdec[:, h:h + 1]),
                         reads=[('ps', 5), 'dtdec'], writes=['xdd'])
                py = m.bank(6)[:, 0:256].rearrange("p (c l) -> p c l", c=2)
                for h in range(4):
                    tg = targ[h % 2]
                    mt = MTb[h % 2]
                    cp = CpT[h % 2]
                    s.op('dve', lambda e, tg=tg, h=h: e.scalar_tensor_tensor(out=tg, in0=prow[:, h, :], scalar=nacol[:, h:h + 1], in1=zeros_m,
                                                                           op0=ALU.add, op1=ALU.min),
                         reads=[('ps', 4), 'nacol', 'zeros_m'], writes=[('targ', h % 2)])
                    s.op('act', lambda e, tg=tg: e.activation(out=tg, in_=tg, func=AF.Exp), reads=[('targ', h % 2)], writes=[('targ', h % 2)])
                    s.op('dve', lambda e, tg=tg, mt=mt: e.tensor_tensor(out=mt, in0=tg, in1=MG, op=ALU.mult),
                         reads=[('targ', h % 2), 'MG'], writes=[('MT', h % 2)])
                    s.op('pool', lambda e, cp=cp, h=h, csl=csl: e.tensor_tensor(out=cp, in0=CTf[:, csl], in1=eacs[:, h, :], op=ALU.mult),
                         reads=['CTf', 'eacs'], writes=[('CpT', h % 2)])
                    po = py[(h % 2) * 64:(h % 2) * 64 + 64, h // 2, :]
                    s.op('pe', lambda e, po=po, h=h, mt=mt: e.matmul(po, lhsT=xdt[:, h * 64:(h + 1) * 64], rhs=mt, start=True, stop=False),
                         reads=['xdt', ('MT', h % 2)], writes=[('ps', 6)])
                    s.op('pe', lambda e, po=po, h=h, cp=cp: e.matmul(po, lhsT=S_b[:, h * 64:(h + 1) * 64], rhs=cp, start=False, stop=True),
                         reads=['Sb', ('CpT', h % 2)], writes=[('ps', 6)])
                pS = m.bank(7)[:, 0:256]
                s.op('pe', lambda e, pS=pS: e.matmul(pS, lhsT=Btok, rhs=xdd, start=True, stop=True), reads=['Btok', 'xdd'], writes=[('ps', 7)])
                for h in range(4):
                    hs = slice(h * 64, (h + 1) * 64)
                    s.op('dve', lambda e, hs=hs, h=h, pS=pS: e.scalar_tensor_tensor(out=S_f[:, hs], in0=S_f[:, hs], scalar=cdec[:, h:h + 1], in1=pS[:, hs],
                                                                               op0=ALU.mult, op1=ALU.add), reads=['Sf', 'cdec', ('ps', 7)], writes=['Sf'])
                s.op('act', lambda e: e.activation(out=S_b, in_=S_f, func=AF.Copy), reads=['Sf'], writes=['Sb'])
                yo = yout[ti % 2]
                for c in range(2):
                    s.op('dve', lambda e, c=c, csl=csl: e.scalar_tensor_tensor(out=ytmp[:, c, :], in0=xT[:, c, csl], scalar=dcol[:, c:c + 1], in1=py[:, c, :],
                                                                              op0=ALU.mult, op1=ALU.add), reads=[('xT', c), 'dcol', ('ps', 6)], writes=[('ytmp', c)])
                    s.op('pool', lambda e, c=c, csl=csl, yo=yo: e.tensor_tensor(out=yo[:, c, csl], in0=ytmp[:, c, :], in1=sz[:, c, csl], op=ALU.mult),
                         reads=[('ytmp', c), ('sz', c)], writes=[('yout', ti % 2)])
            for c in range(2):
                s.dma('sp', y_d[c * 128:(c + 1) * 128, ti * 512:(ti + 1) * 512], yout[ti % 2][:, c, :], reads=[('yout', ti % 2)], is_output=True)
        print("SSD ops", s.n_ops)
        s.emit()
    return nc


def build_C(nbr=3):
    nc = new_nc()
    hin = nc.dram_tensor("hin", [D, TL], F32, kind="ExternalInput").ap()
    uin = nc.dram_tensor("uin", [D, TL], BF16, kind="ExternalInput").ap()
    br_d = [nc.dram_tensor(n, [D, TL], BF16, kind="ExternalInput").ap() for n in ("ys", "yg", "ya")]
    modT = nc.dram_tensor("modT", [128, 144], F32, kind="ExternalInput").ap()
    gains = nc.dram_tensor("gains", [128, 2 * NK], F32, kind="ExternalInput").ap()
    wo_d = [nc.dram_tensor(n, [D, D], F32, kind="ExternalInput").ap() for n in ("wos", "wog", "woa")]
    wgt = nc.dram_tensor("wgt", [D, 3 * D], F32, kind="ExternalInput").ap()
    wout = nc.dram_tensor("wout", [D, D], F32, kind="ExternalInput").ap()
    wg = nc.dram_tensor("wg", [D, DFF], F32, kind="ExternalInput").ap()
    wu = nc.dram_tensor("wu", [D, DFF], F32, kind="ExternalInput").ap()
    wd = nc.dram_tensor("wd", [DFF, D], F32, kind="ExternalInput").ap()
    hout = nc.dram_tensor("hout", [D, TL], F32, kind="ExternalOutput").ap()
    with ExitStack() as st:
        m = Mem(nc, st, sbuf_bytes=207 * 1024)
        s = Sched(nc)
        ones = setup_consts(s, m)
        hT = m.alloc(NK * TL, F32).rearrange("p (k t) -> p k t", k=NK)
        mod = m.alloc(144, F32)
        gn = m.alloc(2 * NK, F32)
        gs = m.alloc(NK, F32)
        gh = m.alloc(NK, F32)
        for k in range(NK):
            s.dma('sp', hT[:, k, :], hin[k * 128:(k + 1) * 128, :], writes=[('hT', k, 0), ('hT', k, 1)])
        s.dma('sp', mod, modT, writes=['mod'])
        s.dma('sp', gn, gains, writes=['gn'])
        s.op('dve', lambda e: e.scalar_tensor_tensor(out=gs, in0=mod[:, 112:128], scalar=1.0, in1=gn[:, NK:2 * NK],
                                                     op0=ALU.add, op1=ALU.mult), reads=['mod', 'gn'], writes=['u3gs'])
        s.op('dve', lambda e: e.tensor_scalar(out=gh, in0=mod[:, 128:144], scalar1=0.5, scalar2=None, op0=ALU.mult),
             reads=['mod'], writes=['ghalf'])
        m.push()
        brs = [m.alloc(NK * 512, BF16).rearrange("p (k t) -> p k t", k=NK) for _ in range(3)]
        ut = m.alloc(NK * 512, BF16).rearrange("p (k t) -> p k t", k=NK)
        mix = m.alloc(NK * 512, BF16).rearrange("p (k t) -> p k t", k=NK)
        wob = [[m.alloc(NK * 128, BF16).rearrange("p (k c) -> p k c", k=NK) for _ in range(6)] for _ in range(1)]
        wtb = [m.alloc(NK * 128, BF16).rearrange("p (k c) -> p k c", k=NK) for _ in range(2)]
        sq = [m.alloc(512, F32) for _ in range(2)]
        rstd = m.alloc(512, F32)
        tmp = [m.alloc(512, F32) for _ in range(2)]
        sig = m.alloc(512, F32)
        macc = m.alloc(512, F32)
        it = 0
        for t in range(2):
            tsl = slice(t * 512, (t + 1) * 512)
            for b in range(3):
                s.dma('sp', brs[b], br_d[b][:, tsl].rearrange("(k p) t -> p k t", p=128), writes=[('br', b)])
            s.dma('sp', ut, uin[:, tsl].rearrange("(k p) t -> p k t", p=128), writes=['ut'])
            for g in range(4):
                ps = m.bank(7)
                for kk in range(4):
                    k = g * 4 + kk
                    s.op('act', lambda e, k=k, kk=kk: e.activation(out=sq[kk % 2], in_=brs[0][:, k, :], func=AF.Square),
                         reads=[('br', 0)], writes=[('sq', kk % 2)])
                    s.op('pe', lambda e, ps=ps, kk=kk: e.matmul(ps, lhsT=ones, rhs=sq[kk % 2], start=(kk == 0), stop=(kk == 3)),
                         reads=[('sq', kk % 2), 'ones'], writes=[('ps', 7)])
                s.op('act', lambda e, ps=ps: e.activation(out=rstd, in_=ps, func=AF.Sqrt, scale=1.0 / 512, bias=EPSB[0]),
                     reads=[('ps', 7), 'epsb'], writes=['rstd'])
                s.op('dve', lambda e: e.reciprocal(out=rstd, in_=rstd), reads=['rstd'], writes=['rstd'])
                for kk in range(4):
                    k = g * 4 + kk
                    s.op('dve', lambda e, k=k, kk=kk: e.tensor_tensor(out=tmp[kk % 2], in0=brs[0][:, k, :], in1=rstd, op=ALU.mult),
                         reads=[('br', 0), 'rstd'], writes=[('tmp', kk % 2)])
                    s.op('act', lambda e, k=k, kk=kk: e.activation(out=brs[0][:, k, :], in_=tmp[kk % 2], func=AF.Copy, scale=gn[:, k:k + 1]),
                         reads=[('tmp', kk % 2), 'gn'], writes=[('br', 0)])
            for o in range(NK):
                wb6 = wob[0]
                wr = ('wo', 0)
                it += 1
                osl = slice(o * 128, (o + 1) * 128)
                for b in range(3):
                    s.dma('pool', wb6[b], wo_d[b][:, osl].rearrange("(k p) c -> p k c", p=128), writes=[('wo', b)])
                    s.dma('pool', wb6[3 + b], wgt[:, b * D + o * 128:b * D + (o + 1) * 128].rearrange("(k p) c -> p k c", p=128), writes=[('wgt', b)])
                for b in range(3):
                    py = m.bank(b % 2)
                    pg = m.bank(2 + b % 2)
                    for k in range(NK):
                        s.op('pe', lambda e, py=py, b=b, k=k, wb6=wb6: e.matmul(py, lhsT=wb6[b][:, k, :], rhs=brs[b][:, k, :], start=(k == 0), stop=(k == NK - 1)),
                             reads=[('wo', b), ('br', b)], writes=[('ps', b % 2)])
                    for k in range(NK):
                        s.op('pe', lambda e, pg=pg, b=b, k=k, wb6=wb6: e.matmul(pg, lhsT=wb6[3 + b][:, k, :], rhs=ut[:, k, :], start=(k == 0), stop=(k == NK - 1)),
                             reads=[('wgt', b), 'ut'], writes=[('ps', 2 + b % 2)])
                    s.op('act', lambda e, pg=pg: e.activation(out=sig, in_=pg, func=AF.Sigmoid), reads=[('ps', 2 + b % 2)], writes=['sig'])
                    if b == 0:
                        s.op('dve', lambda e, py=py: e.tensor_tensor(out=macc, in0=sig, in1=py, op=ALU.mult), reads=['sig', ('ps', b % 2)], writes=['macc'])
                    else:
                        s.op('dve', lambda e, py=py: e.tensor_tensor(out=sig, in0=sig, in1=py, op=ALU.mult), reads=['sig', ('ps', b % 2)], writes=['sig'])
                        if b == 1:
                            s.op('dve', lambda e: e.tensor_tensor(out=macc, in0=macc, in1=sig, op=ALU.add), reads=['sig', 'macc'], writes=['macc'])
                        else:
                            s.op('dve', lambda e, o=o: e.tensor_tensor(out=mix[:, o, :], in0=macc, in1=sig, op=ALU.add), reads=['sig', 'macc'], writes=[('mix', o)])
            for o2 in range(NK):
                wt = wtb[o2 % 2]
                s.dma('pool', wt, wout[:, o2 * 128:(o2 + 1) * 128].rearrange("(k p) c -> p k c", p=128), writes=[('wt', o2 % 2)])
                pw = m.bank(4 + o2 % 2)
                for k in range(NK):
                    s.op('pe', lambda e, pw=pw, k=k, wt=wt: e.matmul(pw, lhsT=wt[:, k, :], rhs=mix[:, k, :], start=(k == 0), stop=(k == NK - 1)),
                         reads=[('wt', o2 % 2), ('mix', k)], writes=[('ps', 4 + o2 % 2)])
                s.op('dve', lambda e, pw=pw, o2=o2, tsl=tsl: e.scalar_tensor_tensor(out=hT[:, o2, tsl], in0=pw, scalar=mod[:, 80 + o2:81 + o2], in1=hT[:, o2, tsl],
                                                                                  op0=ALU.mult, op1=ALU.add),
                     reads=[('ps', 4 + o2 % 2), 'mod', ('hT', o2, t)], writes=[('hT', o2, t)])
        s.barrier()
        m.pop()
        uT = m.alloc(NK * TL, BF16).rearrange("p (k t) -> p k t", k=NK)
        tmpb = ([m.alloc(512, F32) for _ in range(2)], m.alloc(512, F32), [m.alloc(512, F32) for _ in range(2)])
        bufs = ([m.alloc(NK * 256, BF16) for _ in range(2)], [m.alloc(NK * 256, BF16) for _ in range(2)],
                [m.alloc(2 * D, BF16) for _ in range(2)], [m.alloc(2 * TL, BF16) for _ in range(2)],
                [m.alloc(512, F32) for _ in range(2)])
        rms_modulate(s, m, hT, uT, gs, mod[:, 96:112], tmpb, 'u3gs', ones, 7)
        ffn(s, m, hT, uT, 'uT', wg, wu, wd, gh, bufs)
        for k in range(NK):
            s.dma('sp', hout[k * 128:(k + 1) * 128, :], hT[:, k, :], reads=[('hT', k, 0), ('hT', k, 1)], is_output=True)
        print("C ops", s.n_ops)
        s.emit()
    return nc


NEGM = -30000.0


def build_attn2(T=8192):
    nc = new_nc()
    NB = T // 256
    NT = T // 512
    TW = T + 128
    WV = TW + 127
    uT_d = nc.dram_tensor("uT", [D, T], BF16, kind="ExternalInput").ap()
    w_d = nc.dram_tensor("w", [2, D, 384], F32, kind="ExternalInput").ap()
    gn_d = nc.dram_tensor("gn", [128, 2], F32, kind="ExternalInput").ap()
    rb_d = nc.dram_tensor("rb", [32, 2], F32, kind="ExternalInput").ap()
    oh_d = nc.dram_tensor("oh", [32, T], F32, kind="ExternalInput").ap()
    wv_d = nc.dram_tensor("wvec", [2, WV + 1], BF16, kind="Internal").ap()
    o_d = nc.dram_tensor("o", [256, T], BF16, kind="ExternalOutput").ap()
    with ExitStack() as st:
        m = Mem(nc, st, sbuf_bytes=207 * 1024)
        s = Sched(nc)
        ones = setup_consts(s, m)
        ones_m, tri, ident = make_tri_ident(s, m)
        identb = m.alloc(128, BF16)
        s.op('pool', lambda e: e.tensor_copy(out=identb, in_=ident), reads=['ident'], writes=['identb'])
        gn = m.alloc(2, F32)
        s.dma('sp', gn, gn_d, writes=['gn'])
        gq = m.alloc(1, F32)
        s.op('dve', lambda e: e.tensor_scalar(out=gq, in0=gn[:, 0:1], scalar1=128.0 ** -0.5, scalar2=None, op0=ALU.mult), reads=['gn'], writes=['gq'])
        rb = m.alloc(2, F32, parts=32)
        s.dma('sp', rb, rb_d, writes=['rb'])
        m.push()
        oh = m.alloc(T, F32, parts=32)
        s.dma('sp', oh, oh_d, writes=['oh'])
        wrow = m.alloc(WV + 1, BF16, parts=2)
        s.op('dve', lambda e: e.memset(wrow, NEGM), writes=['wrow'])
        for cch in range(T // 512):
            pb = m.bank(5 + cch % 2)
            s.op('pe', lambda e, pb=pb, cch=cch: e.matmul(pb[0:2, :], lhsT=rb, rhs=oh[:, cch * 512:(cch + 1) * 512], start=True, stop=True),
                 reads=['rb', 'oh'], writes=[('ps', 5 + cch % 2)])
            s.op('act', lambda e, pb=pb, cch=cch: e.activation(out=wrow[:, 255 + cch * 512:255 + (cch + 1) * 512], in_=pb[0:2, :], func=AF.Copy),
                 reads=[('ps', 5 + cch % 2)], writes=['wrow'])
        s.dma('sp', wv_d, wrow, reads=['wrow'], writes=['wvd'])
        s.barrier()
        m.pop()
        Tbig = m.alloc(TW, BF16)
        QT = m.alloc(T, BF16)
        KT = m.alloc(T, BF16)
        Va = m.alloc((T // 128) * 130, BF16).rearrange("p (c d) -> p c d", d=130)
        sel = m.alloc((T // 128) * 32, F32).rearrange("p (c n) -> p c n", n=32)
        negm = m.alloc(32 * 32, F32).rearrange("p (b n) -> p b n", b=32)
        zer = m.alloc(32 * 32, F32)
        s.op('pool', lambda e: e.memset(zer, 0.0), writes=['zer'])
        s.op('pool', lambda e: e.affine_select(out=negm, in_=zer.rearrange("p (b n) -> p b n", b=32), pattern=[[1, 32], [-1, 32]], compare_op=ALU.is_ge,
                                               fill=-1e30, base=-1, channel_multiplier=0), reads=['zer'], writes=['negm'])
        ones32 = ones_m[:, 0:32]
        wb = m.alloc(NK * 384, BF16).rearrange("p (k c) -> p k c", k=NK)
        ub = [m.alloc(NK * 512, BF16).rearrange("p (k t) -> p k t", k=NK) for _ in range(2)]
        xf = [m.alloc(512, F32) for _ in range(2)]
        sq = m.alloc(512, F32)
        rstd = m.alloc(512, F32)
        kn = m.alloc(512, F32)
        qn = m.alloc(512, F32)
        kmean = m.alloc(32, F32)
        s.op('pool', lambda e: e.memset(kmean, 0.0), writes=['kmean'])
        gsb = m.alloc(32, F32)
        m8 = m.alloc(8, F32)
        thr = m.alloc(1, F32)
        LB = [dict(pT=[m.alloc(256, BF16) for _ in range(2)], acc=[m.alloc(130, F32) for _ in range(2)], rec=m.alloc(1, F32),
                   of=m.alloc(128, F32), oT=[m.alloc(256, BF16) for _ in range(2)]) for _ in range(2)]
        for h in range(2):
            s.dma('pool', wb, w_d[h].rearrange("(k p) c -> p k c", p=128), writes=['wb'])
            for i in range(128):
                s.dma('sp' if i % 2 == 0 else 'act', Tbig[i:i + 1, :], wv_d[h:h + 1, 127 - i:127 - i + TW], reads=['wvd'], writes=['Tbig'])
            s.op('pool', lambda e: e.memset(Va[:, :, 128:130], 1.0), writes=['Va'])
            for ti in range(NT):
                u = ub[ti % 2]
                ur = ('u', ti % 2)
                tsl = slice(ti * 512, (ti + 1) * 512)
                s.dma('sp', u, uT_d[:, tsl].rearrange("(k p) t -> p k t", p=128), writes=[ur])
                for which in (1, 0):
                    ps = m.bank(5)
                    for k in range(NK):
                        s.op('pe', lambda e, ps=ps, k=k, u=u, which=which: e.matmul(ps, lhsT=wb[:, k, which * 128:(which + 1) * 128], rhs=u[:, k, :],
                                                                                 start=(k == 0), stop=(k == NK - 1)), reads=['wb', ur], writes=[('ps', 5)])
                    x = xf[which]
                    s.op('act', lambda e, x=x, ps=ps: e.activation(out=x, in_=ps, func=AF.Copy), reads=[('ps', 5)], writes=[('xf', which)])
                    s.op('act', lambda e, x=x: e.activation(out=sq, in_=x, func=AF.Square), reads=[('xf', which)], writes=['sq'])
                    p2 = m.bank(6)
                    s.op('pe', lambda e, p2=p2: e.matmul(p2, lhsT=ones, rhs=sq, start=True, stop=True), reads=['sq', 'ones'], writes=[('ps', 6)])
                    s.op('act', lambda e, p2=p2: e.activation(out=rstd, in_=p2, func=AF.Sqrt, scale=1.0 / 128, bias=EPSB[0]), reads=[('ps', 6), 'epsb'], writes=['rstd'])
                    s.op('dve', lambda e: e.reciprocal(out=rstd, in_=rstd), reads=['rstd'], writes=['rstd'])
                    s.op('dve', lambda e, x=x: e.tensor_tensor(out=x, in0=x, in1=rstd, op=ALU.mult), reads=[('xf', which), 'rstd'], writes=[('xf', which)])
                    if which == 1:
                        s.op('act', lambda e, x=x: e.activation(out=kn, in_=x, func=AF.Copy, scale=gn[:, 1:2]), reads=[('xf', 1), 'gn'], writes=['kn'])
                        s.op('pool', lambda e, tsl=tsl: e.tensor_copy(out=KT[:, tsl], in_=kn), reads=['kn'], writes=['KT'])
                        for bb in range(2):
                            blk = ti * 2 + bb
                            s.op('dve', lambda e, blk=blk, bb=bb: e.tensor_reduce(out=kmean[:, blk:blk + 1], in_=kn[:, bb * 256:(bb + 1) * 256], axis=AX.X, op=ALU.add),
                                 reads=['kn'], writes=['kmean'])
                    else:
                        s.op('act', lambda e, x=x: e.activation(out=qn, in_=x, func=AF.Copy, scale=gn[:, 0:1]), reads=[('xf', 0), 'gn'], writes=['qn'])
                        s.op('act', lambda e, x=x, tsl=tsl: e.activation(out=QT[:, tsl], in_=x, func=AF.Copy, scale=gq), reads=[('xf', 0), 'gq'], writes=['QT'])
                        for qq in range(4):
                            qt = ti * 4 + qq
                            b = qt // 2
                            pg = m.bank(7)[:, 0:32]
                            s.op('pe', lambda e, pg=pg, qq=qq: e.matmul(pg, lhsT=qn[:, qq * 128:(qq + 1) * 128], rhs=kmean, start=True, stop=True),
                                 reads=['qn', 'kmean'], writes=[('ps', 7)])
                            s.op('dve', lambda e, pg=pg, b=b: e.tensor_tensor(out=gsb, in0=pg, in1=negm[:, b, :], op=ALU.add), reads=[('ps', 7), 'negm'], writes=['gsb'])
                            s.op('dve', lambda e: e.max(out=m8, in_=gsb), reads=['gsb'], writes=['m8'])
                            s.op('dve', lambda e: e.tensor_scalar(out=thr, in0=m8[:, 2:3], scalar1=-1e29, scalar2=None, op0=ALU.max), reads=['m8'], writes=['thr'])
                            s.op('dve', lambda e, qt=qt: e.scalar_tensor_tensor(out=sel[:, qt, :], in0=gsb, scalar=thr, in1=ones32, op0=ALU.is_ge, op1=ALU.mult),
                                 reads=['gsb', 'thr', 'ones_m'], writes=['sel'])
                for cc in range(4):
                    ck = ti * 4 + cc
                    pv = m.bank(7)[:, 128:256]
                    for k in range(NK):
                        s.op('pe', lambda e, pv=pv, k=k, u=u, cc=cc: e.matmul(pv, lhsT=u[:, k, cc * 128:(cc + 1) * 128], rhs=wb[:, k, 256:384],
                                                                           start=(k == 0), stop=(k == NK - 1)), reads=['wb', ur], writes=[('ps', 7)])
                    s.op('act', lambda e, pv=pv, ck=ck: e.activation(out=Va[:, ck, 0:128], in_=pv, func=AF.Copy), reads=[('ps', 7)], writes=['Va'])
            def lane_body(s, lane, h=h):
                pT, acc, rec, of, oT = LB[lane]['pT'], LB[lane]['acc'], LB[lane]['rec'], LB[lane]['of'], LB[lane]['oT']
                it = 0
                for b in range(lane, NB, 2):
                    qsl = slice(b * 256, (b + 1) * 256)
                    for qt2 in range(2):
                        s.op('pool', lambda e, qt2=qt2: e.memset(acc[qt2], 0.0), writes=[('acc', qt2)])
                    for n in range(b + 1):
                        pOb = [4 * lane + 1, 4 * lane + 2]
                        for kh in range(2):
                            kt = 2 * n + kh
                            D0 = b * 256 - kt * 128 + 128
                            pS = m.bank(4 * lane)[:, 0:256] if kh == 0 else m.bank(4 * lane)[:, 256:512]
                            sr = ('ps', 4 * lane)
                            s.op('pe', lambda e, pS=pS, kt=kt, qsl=qsl: e.matmul(pS, lhsT=KT[:, kt * 128:(kt + 1) * 128], rhs=QT[:, qsl], start=True, stop=False),
                                 reads=['KT', 'QT'], writes=[sr])
                            s.op('pe', lambda e, pS=pS, D0=D0: e.matmul(pS, lhsT=identb, rhs=Tbig[:, D0:D0 + 256], start=False, stop=True),
                                 reads=['identb', 'Tbig'], writes=[sr])
                            p = pT[kh]
                            s.op('act', lambda e, p=p, pS=pS: e.activation(out=p, in_=pS, func=AF.Exp), reads=[sr], writes=[('pT', kh)])
                            for qt2 in range(2):
                                po = m.bank(pOb[qt2])[:, 0:129]
                                s.op('pe', lambda e, po=po, p=p, qt2=qt2, kt=kt, kh=kh: e.matmul(po, lhsT=p[:, qt2 * 128:(qt2 + 1) * 128], rhs=Va[:, kt, 0:129],
                                                                                              start=(kh == 0), stop=(kh == 1)),
                                     reads=[('pT', kh), 'Va'], writes=[('ps', pOb[qt2])])
                        for qt2 in range(2):
                            po = m.bank(pOb[qt2])[:, 0:129]
                            a = acc[qt2][:, 0:129]
                            if n == b:
                                s.op('dve', lambda e, a=a, po=po: e.tensor_tensor(out=a, in0=a, in1=po, op=ALU.add), reads=[('acc', qt2), ('ps', pOb[qt2])], writes=[('acc', qt2)])
                            else:
                                s.op('dve', lambda e, a=a, po=po, qt2=qt2, b=b, n=n: e.scalar_tensor_tensor(out=a, in0=po, scalar=sel[:, b * 2 + qt2, n:n + 1], in1=a,
                                                                                                       op0=ALU.mult, op1=ALU.add),
                                     reads=[('acc', qt2), ('ps', pOb[qt2]), 'sel'], writes=[('acc', qt2)])
                        it += 1
                    ot = oT[(b // 2) % 2]
                    for qt2 in range(2):
                        s.op('dve', lambda e, qt2=qt2: e.reciprocal(out=rec, in_=acc[qt2][:, 128:129]), reads=[('acc', qt2)], writes=['rec'])
                        s.op('act', lambda e, qt2=qt2: e.activation(out=of, in_=acc[qt2][:, 0:128], func=AF.Copy, scale=rec), reads=[('acc', qt2), 'rec'], writes=['of'])
                        pt = m.bank(4 * lane + 3)[:, 0:128]
                        s.op('pe', lambda e, pt=pt: e.transpose(out=pt, in_=of, identity=ident), reads=['of', 'ident'], writes=[('ps', 4 * lane + 3)])
                        s.op('act', lambda e, pt=pt, ot=ot, qt2=qt2: e.activation(out=ot[:, qt2 * 128:(qt2 + 1) * 128], in_=pt, func=AF.Copy), reads=[('ps', 4 * lane + 3)], writes=[('oT', (b // 2) % 2)])
                    s.dma('sp', o_d[h * 128:(h + 1) * 128, qsl], ot, reads=[('oT', (b // 2) % 2)], is_output=True)
            L = Lanes(s, 2, shared=['KT', 'QT', 'Va', 'Tbig', 'sel', 'identb', 'ident', 'ones_m'])
            L.run([lambda P, ln=ln: lane_body(P, ln) for ln in range(2)])
            s.barrier()
        print("ATTN2 ops", s.n_ops)
        s.emit()
    return nc


class Lanes:
    SHARED = {'ones', 'ones_m', 'tri', 'ident', 'zeros_m', 'bd', 'mIU', 'mSU', 'gn', 'epsb'}

    def __init__(self, s, n, shared=None):
        self.s = s
        self.n = n
        if shared is not None:
            self.SHARED = set(shared)
        self.turn = 0
        self.done = [False] * n
        self.cv = threading.Condition()
        self.err = []

    def _ren(self, lane, keys):
        out = []
        for k in keys:
            if (isinstance(k, tuple) and k and k[0] == 'ps') or (not isinstance(k, tuple) and k in self.SHARED):
                out.append(k)
            else:
                out.append((k, 'lane', lane))
        return out

    def _advance(self, lane):
        for d in range(1, self.n + 1):
            nx = (lane + d) % self.n
            if not self.done[nx]:
                self.turn = nx
                return

    def proxy(self, lane):
        L = self

        class P:
            def op(self_, eng, fn, reads=(), writes=()):
                with L.cv:
                    while L.turn != lane:
                        L.cv.wait()
                    r = L.s.op(eng, fn, L._ren(lane, reads), L._ren(lane, writes))
                    L._advance(lane)
                    L.cv.notify_all()
                return r

            def dma(self_, q, out, in_, reads=(), writes=(), **kw):
                with L.cv:
                    while L.turn != lane:
                        L.cv.wait()
                    r = L.s.dma(q, out, in_, L._ren(lane, reads), L._ren(lane, writes), **kw)
                    L._advance(lane)
                    L.cv.notify_all()
                return r
        return P()

    def run(self, fns):
        def wrap(i, f):
            try:
                f(self.proxy(i))
            except BaseException as ex:
                self.err.append(ex)
            finally:
                with self.cv:
                    self.done[i] = True
                    if self.turn == i:
                        self._advance(i)
                    self.cv.notify_all()
        ths = [threading.Thread(target=wrap, args=(i, f)) for i, f in enumerate(fns)]
        for t in ths:
            t.start()
        for t in ths:
            t.join()
        if self.err:
            raise self.err[0]


def build_gdn2(T=8192):
    nc = new_nc()
    NT = T // 512
    uT_d = nc.dram_tensor("uT", [D, T], BF16, kind="ExternalInput").ap()
    w_d = nc.dram_tensor("w", [2, D, 514], F32, kind="ExternalInput").ap()
    cw_d = nc.dram_tensor("cw", [2, 128, 3, 4], F32, kind="ExternalInput").ap()
    hp_d = nc.dram_tensor("hp", [2, 128, 2], F32, kind="ExternalInput").ap()
    gn_d = nc.dram_tensor("gn", [128, 1], F32, kind="ExternalInput").ap()
    o_d = nc.dram_tensor("o", [256, T], BF16, kind="ExternalOutput").ap()
    with ExitStack() as st:
        m = Mem(nc, st)
        s = Sched(nc)
        ones = setup_consts(s, m)
        ones_m, tri, ident = make_tri_ident(s, m)
        onec = ones_m[:, 0:1]
        zeros_m = m.alloc(128, F32)
        s.op('pool', lambda e: e.memset(zeros_m, 0.0), writes=['zeros_m'])
        bd = m.alloc(128, F32)
        s.op('pool', lambda e: e.memset(bd, 0.0), writes=['bd'])
        s.op('pool', lambda e: e.memset(bd[0:64, 0:64], 1.0), reads=['bd'], writes=['bd'])
        s.op('pool', lambda e: e.memset(bd[64:128, 64:128], 1.0), reads=['bd'], writes=['bd'])
        mIU = m.alloc(128, F32)
        mSU = m.alloc(128, F32)
        s.op('pool', lambda e: e.tensor_tensor(out=mIU, in0=tri, in1=bd, op=ALU.mult), reads=['tri', 'bd'], writes=['mIU'])
        s.op('pool', lambda e: e.tensor_tensor(out=mSU, in0=mIU, in1=ident, op=ALU.subtract), reads=['mIU', 'ident'], writes=['mSU'])
        gn = m.alloc(1, F32)
        s.dma('sp', gn, gn_d, writes=['gn'])
        def alloc_head():
            wb = m.alloc(NK * 514, BF16).rearrange("p (k c) -> p k c", k=NK)
            cw = m.alloc(12, F32).rearrange("p (c k) -> p c k", c=3)
            hp = m.alloc(2, F32)
            negA = m.alloc(1, F32)
            ub = [m.alloc(NK * 512, BF16).rearrange("p (k t) -> p k t", k=NK) for _ in range(2)]
            pre = [m.alloc(515, F32) for _ in range(3)]
            acc = [m.alloc(512, F32) for _ in range(2)]
            qc = m.alloc(512, F32)
            kc = m.alloc(512, F32)
            vcf = m.alloc(512, F32)
            sz = m.alloc(512, F32)
            sq = m.alloc(512, F32)
            rstd = m.alloc(512, F32)
            ofull = m.alloc(512, F32)
            yout = [m.alloc(512, BF16) for _ in range(2)]
            c1 = lambda: m.alloc(1, F32)
            ab, et, spv, gcol, eb, beta, nbeta, gccol, glcol, ecol, bg, kdsc, egl0, egl1, tdiff = [c1() for _ in range(15)]
            ab = m.alloc(2, F32)
            ngccol = m.alloc(1, F32)
            grep = m.alloc(128, F32)
            egcrow = m.alloc(128, F32)
            tg = m.alloc(128, F32)
            ET = m.alloc(128, F32)
            NT0 = m.alloc(128, F32)
            qkT = m.alloc(128, F32)
            X = [m.alloc(128, F32) for _ in range(2)]
            XT = [m.alloc(128, F32) for _ in range(2)]
            PT = [m.alloc(128, F32) for _ in range(2)]
            kbg = m.alloc(128, F32)
            kdec = m.alloc(128, F32)
            vb = m.alloc(128, F32)
            u_sb = m.alloc(128, F32)
            wT = m.alloc(128, F32)
            qdT = m.alloc(128, F32)
            vc_sb = m.alloc(128, F32)
            S = m.alloc(128, F32)
            return dict(locals())
        def head_body(h, s, B, A_, B_, C_, D_):
            wb = B['wb']
            cw = B['cw']
            hp = B['hp']
            negA = B['negA']
            ub = B['ub']
            pre = B['pre']
            acc = B['acc']
            qc = B['qc']
            kc = B['kc']
            vcf = B['vcf']
            sz = B['sz']
            sq = B['sq']
            rstd = B['rstd']
            ofull = B['ofull']
            yout = B['yout']
            ab = B['ab']
            et = B['et']
            spv = B['spv']
            gcol = B['gcol']
            eb = B['eb']
            beta = B['beta']
            nbeta = B['nbeta']
            gccol = B['gccol']
            glcol = B['glcol']
            ecol = B['ecol']
            bg = B['bg']
            kdsc = B['kdsc']
            egl0 = B['egl0']
            egl1 = B['egl1']
            tdiff = B['tdiff']
            ngccol = B['ngccol']
            grep = B['grep']
            egcrow = B['egcrow']
            tg = B['tg']
            ET = B['ET']
            NT0 = B['NT0']
            qkT = B['qkT']
            X = B['X']
            XT = B['XT']
            PT = B['PT']
            kbg = B['kbg']
            kdec = B['kdec']
            vb = B['vb']
            u_sb = B['u_sb']
            wT = B['wT']
            qdT = B['qdT']
            vc_sb = B['vc_sb']
            S = B['S']
            s.dma('pool', wb, w_d[h].rearrange("(k p) c -> p k c", p=128), writes=['wb'])
            s.dma('sp', cw, cw_d[h], writes=['cw'])
            s.dma('sp', hp, hp_d[h], writes=['hp'])
            s.op('act', lambda e: e.activation(out=negA, in_=hp[:, 1:2], func=AF.Exp), reads=['hp'], writes=['negA'])
            s.op('dve', lambda e: e.tensor_scalar(out=negA, in0=negA, scalar1=-1.0, scalar2=None, op0=ALU.mult), reads=['negA'], writes=['negA'])
            s.op('pool', lambda e: e.memset(S, 0.0), writes=['S'])
            for c in range(3):
                s.op('pool', lambda e, c=c: e.memset(pre[c][:, 0:3], 0.0), writes=[('pre', c)])
            for ti in range(NT):
                u = ub[ti % 2]
                ur = ('u', ti % 2)
                s.dma('sp', u, uT_d[:, ti * 512:(ti + 1) * 512].rearrange("(k p) t -> p k t", p=128), writes=[ur])
                dst = [qc, kc, vcf]
                dname = ['qc', 'kc', 'vc']
                for g in range(4):
                    ps = m.bank(A_)
                    for k in range(NK):
                        s.op('pe', lambda e, ps=ps, g=g, k=k, u=u: e.matmul(ps, lhsT=wb[:, k, g * 128:(g + 1) * 128], rhs=u[:, k, :],
                                                                         start=(k == 0), stop=(k == NK - 1)), reads=['wb', ur], writes=[('ps', A_)])
                    if g == 3:
                        s.op('act', lambda e, ps=ps: e.activation(out=sz, in_=ps, func=AF.Silu), reads=[('ps', A_)], writes=['sz'])
                        continue
                    c = g
                    s.op('act', lambda e, ps=ps, c=c: e.activation(out=pre[c][:, 3:515], in_=ps, func=AF.Copy), reads=[('ps', A_)], writes=[('pre', c)])
                    a = acc[c % 2]
                    s.op('dve', lambda e, a=a, c=c: e.tensor_scalar(out=a, in0=pre[c][:, 0:512], scalar1=cw[:, c, 0:1], scalar2=zeros_m[:, 0:1],
                                                                    op0=ALU.mult, op1=ALU.add), reads=[('pre', c), 'cw', 'zeros_m'], writes=[('acc', c % 2)])
                    for kk in range(1, 4):
                        s.op('dve', lambda e, a=a, c=c, kk=kk: e.scalar_tensor_tensor(out=a, in0=pre[c][:, kk:kk + 512], scalar=cw[:, c, kk:kk + 1],
                                                                                      in1=a, op0=ALU.mult, op1=ALU.add),
                             reads=[('pre', c), 'cw', ('acc', c % 2)], writes=[('acc', c % 2)])
                    s.op('pool', lambda e, c=c: e.tensor_copy(out=pre[c][:, 0:3], in_=pre[c][:, 512:515]), reads=[('pre', c)], writes=[('pre', c)])
                    s.op('act', lambda e, a=a, c=c: e.activation(out=dst[c], in_=a, func=AF.Silu), reads=[('acc', c % 2)], writes=[dname[c]])
                    if c < 2:
                        s.op('act', lambda e, c=c: e.activation(out=sq, in_=dst[c], func=AF.Square), reads=[dname[c]], writes=['sq'])
                        p2 = m.bank(B_)
                        s.op('pe', lambda e, p2=p2: e.matmul(p2, lhsT=ones, rhs=sq, start=True, stop=True), reads=['sq', 'ones'], writes=[('ps', B_)])
                        s.op('act', lambda e, p2=p2: e.activation(out=rstd, in_=p2, func=AF.Sqrt, scale=1.0, bias=EPSB[0]), reads=[('ps', B_), 'epsb'], writes=['rstd'])
                        s.op('dve', lambda e: e.reciprocal(out=rstd, in_=rstd), reads=['rstd'], writes=['rstd'])
                        if c == 0:
                            s.op('dve', lambda e: e.scalar_tensor_tensor(out=qc, in0=qc, scalar=128.0 ** -0.5, in1=rstd, op0=ALU.mult, op1=ALU.mult),
                                 reads=['qc', 'rstd'], writes=['qc'])
                        else:
                            s.op('dve', lambda e: e.tensor_tensor(out=kc, in0=kc, in1=rstd, op=ALU.mult), reads=['kc', 'rstd'], writes=['kc'])
                for sti in range(4):
                    cs = slice(sti * 128, (sti + 1) * 128)
                    pab = m.bank(B_)[:, 0:2]
                    for k in range(NK):
                        s.op('pe', lambda e, pab=pab, k=k, u=u, cs=cs: e.matmul(pab, lhsT=u[:, k, cs], rhs=wb[:, k, 512:514], start=(k == 0), stop=(k == NK - 1)),
                             reads=['wb', ur], writes=[('ps', B_)])
                    s.op('act', lambda e, pab=pab: e.activation(out=ab, in_=pab, func=AF.Copy), reads=[('ps', B_)], writes=['ab'])
                    s.op('act', lambda e: e.activation(out=et, in_=ab[:, 0:1], func=AF.Exp, bias=hp[:, 0:1], scale=1.0), reads=['ab', 'hp'], writes=['et'])
                    s.op('act', lambda e: e.activation(out=spv, in_=et, func=AF.Ln, bias=onec, scale=1.0), reads=['et', 'ones_m'], writes=['spv'])
                    s.op('dve', lambda e: e.tensor_tensor(out=gcol, in0=spv, in1=negA, op=ALU.mult), reads=['spv', 'negA'], writes=['gcol'])
                    s.op('act', lambda e: e.activation(out=eb, in_=ab[:, 1:2], func=AF.Exp, scale=-1.0), reads=['ab'], writes=['eb'])
                    s.op('dve', lambda e: e.tensor_scalar(out=eb, in0=eb, scalar1=1.0, scalar2=None, op0=ALU.add), reads=['eb'], writes=['eb'])
                    s.op('dve', lambda e: e.reciprocal(out=beta, in_=eb), reads=['eb'], writes=['beta'])
                    s.op('pool', lambda e: e.tensor_scalar(out=grep, in0=ones_m, scalar1=gcol, scalar2=None, op0=ALU.mult), reads=['ones_m', 'gcol'], writes=['grep'])
                    prow = m.bank(B_)[:, 128:256]
                    plrow = m.bank(B_)[:, 256:384]
                    pcol = m.bank(B_)[:, 8:9]
                    plcol = m.bank(B_)[:, 16:17]
                    s.op('pe', lambda e, prow=prow: e.matmul(prow, lhsT=grep, rhs=mIU, start=True, stop=True), reads=['grep', 'mIU'], writes=[('ps', B_)])
                    s.op('pe', lambda e, plrow=plrow: e.matmul(plrow, lhsT=grep, rhs=bd, start=True, stop=True), reads=['grep', 'bd'], writes=[('ps', B_)])
                    s.op('pe', lambda e, pcol=pcol: e.matmul(pcol, lhsT=mIU, rhs=gcol, start=True, stop=True), reads=['gcol', 'mIU'], writes=[('ps', B_)])
                    s.op('pe', lambda e, plcol=plcol: e.matmul(plcol, lhsT=bd, rhs=gcol, start=True, stop=True), reads=['gcol', 'bd'], writes=[('ps', B_)])
                    s.op('act', lambda e, pcol=pcol: e.activation(out=gccol, in_=pcol, func=AF.Copy), reads=[('ps', B_)], writes=['gccol'])
                    s.op('act', lambda e, pcol=pcol: e.activation(out=ecol, in_=pcol, func=AF.Exp), reads=[('ps', B_)], writes=['ecol'])
                    s.op('dve', lambda e, plcol=plcol: e.tensor_tensor(out=tdiff, in0=plcol, in1=gccol, op=ALU.subtract), reads=[('ps', B_), 'gccol'], writes=['tdiff'])
                    s.op('act', lambda e: e.activation(out=kdsc, in_=tdiff, func=AF.Exp), reads=['tdiff'], writes=['kdsc'])
                    s.op('dve', lambda e: e.tensor_tensor(out=bg, in0=beta, in1=ecol, op=ALU.mult), reads=['beta', 'ecol'], writes=['bg'])
                    s.op('act', lambda e, plrow=plrow: e.activation(out=egl0, in_=plrow[:, 0:1], func=AF.Exp), reads=[('ps', B_)], writes=['egl0'])
                    s.op('act', lambda e, plrow=plrow: e.activation(out=egl1, in_=plrow[:, 64:65], func=AF.Exp), reads=[('ps', B_)], writes=['egl1'])
                    s.op('act', lambda e, prow=prow: e.activation(out=egcrow, in_=prow, func=AF.Exp), reads=[('ps', B_)], writes=['egcrow'])
                    s.op('dve', lambda e: e.tensor_scalar(out=ngccol, in0=gccol, scalar1=-1.0, scalar2=None, op0=ALU.mult), reads=['gccol'], writes=['ngccol'])
                    s.op('dve', lambda e, prow=prow: e.scalar_tensor_tensor(out=tg, in0=prow, scalar=ngccol, in1=zeros_m, op0=ALU.add, op1=ALU.min),
                         reads=[('ps', B_), 'ngccol', 'zeros_m'], writes=['tg'])
                    s.op('act', lambda e: e.activation(out=ET, in_=tg, func=AF.Exp), reads=['tg'], writes=['ET'])
                    pkk = m.bank(C_)[:, 0:128]
                    pqk = m.bank(C_)[:, 128:256]
                    s.op('pe', lambda e, pkk=pkk, cs=cs: e.matmul(pkk, lhsT=kc[:, cs], rhs=kc[:, cs], start=True, stop=True), reads=['kc'], writes=[('ps', C_)])
                    s.op('pe', lambda e, pqk=pqk, cs=cs: e.matmul(pqk, lhsT=kc[:, cs], rhs=qc[:, cs], start=True, stop=True), reads=['kc', 'qc'], writes=[('ps', C_)])
                    s.op('dve', lambda e, pkk=pkk: e.scalar_tensor_tensor(out=NT0, in0=pkk, scalar=-1.0, in1=ET, op0=ALU.mult, op1=ALU.mult),
                         reads=[('ps', C_), 'ET'], writes=['NT0'])
                    s.op('pool', lambda e: e.tensor_tensor(out=NT0, in0=NT0, in1=mSU, op=ALU.mult), reads=['NT0', 'mSU'], writes=['NT0'])
                    s.op('dve', lambda e, pqk=pqk: e.tensor_tensor(out=qkT, in0=pqk, in1=ET, op=ALU.mult), reads=[('ps', C_), 'ET'], writes=['qkT'])
                    s.op('pool', lambda e: e.tensor_tensor(out=qkT, in0=qkT, in1=mIU, op=ALU.mult), reads=['qkT', 'mIU'], writes=['qkT'])
                    pt = m.bank(C_)[:, 384:512]
                    s.op('pe', lambda e, pt=pt: e.transpose(out=pt, in_=NT0, identity=ident), reads=['NT0', 'ident'], writes=[('ps', A_), ('ps', C_), ('ps', D_)])
                    s.op('act', lambda e, pt=pt: e.activation(out=X[0], in_=pt, func=AF.Copy, scale=beta), reads=[('ps', A_), ('ps', C_), ('ps', D_), 'beta'], writes=[('X', 0)])
                    pt2 = m.bank(D_)[:, 0:128]
                    s.op('pe', lambda e, pt2=pt2: e.transpose(out=pt2, in_=X[0], identity=ident), reads=[('X', 0), 'ident'], writes=[('ps', A_), ('ps', D_)])
                    s.op('act', lambda e, pt2=pt2: e.activation(out=XT[0], in_=pt2, func=AF.Copy), reads=[('ps', A_), ('ps', D_)], writes=[('XT', 0)])
                    s.op('dve', lambda e: e.tensor_tensor(out=PT[0], in0=XT[0], in1=ident, op=ALU.add), reads=[('XT', 0), 'ident'], writes=[('PT', 0)])
                    cur = 0
                    for kk in range(1, 6):
                        nx = 1 - cur
                        pa = m.bank(C_)[:, 384:512]
                        s.op('pe', lambda e, pa=pa, cur=cur: e.matmul(pa, lhsT=XT[cur], rhs=X[cur], start=True, stop=True),
                             reads=[('X', cur), ('XT', cur)], writes=[('ps', A_), ('ps', C_), ('ps', D_)])
                        s.op('act', lambda e, pa=pa, nx=nx: e.activation(out=X[nx], in_=pa, func=AF.Copy), reads=[('ps', A_), ('ps', C_), ('ps', D_)], writes=[('X', nx)])
                        if kk < 5:
                            pb_ = m.bank(D_)[:, 0:128]
                            s.op('pe', lambda e, pb_=pb_, cur=cur: e.matmul(pb_, lhsT=X[cur], rhs=XT[cur], start=True, stop=True),
                                 reads=[('X', cur), ('XT', cur)], writes=[('ps', A_), ('ps', D_)])
                            s.op('dve', lambda e, pb_=pb_, nx=nx: e.tensor_copy(out=XT[nx], in_=pb_), reads=[('ps', A_), ('ps', D_)], writes=[('XT', nx)])
                        pc = m.bank(D_)[:, 128:256]
                        s.op('pe', lambda e, pc=pc, nx=nx, cur=cur: e.matmul(pc, lhsT=X[nx], rhs=PT[cur], start=True, stop=True),
                             reads=[('X', nx), ('PT', cur)], writes=[('ps', A_), ('ps', D_)])
                        s.op('dve', lambda e, pc=pc, nx=nx, cur=cur: e.tensor_tensor(out=PT[nx], in0=pc, in1=PT[cur], op=ALU.add),
                             reads=[('ps', A_), ('ps', D_), ('PT', cur)], writes=[('PT', nx)])
                        cur = nx
                    PTf = PT[cur]
                    ptr = ('PT', cur)
                    pk = m.bank(A_)[:, 0:128]
                    s.op('pe', lambda e, pk=pk, cs=cs: e.transpose(out=pk, in_=kc[:, cs], identity=ident), reads=['kc', 'ident'], writes=[('ps', A_), ('ps', C_), ('ps', D_)])
                    s.op('act', lambda e, pk=pk: e.activation(out=kbg, in_=pk, func=AF.Copy, scale=bg), reads=[('ps', A_), ('ps', C_), ('ps', D_), 'bg'], writes=['kbg'])
                    s.op('act', lambda e, pk=pk: e.activation(out=kdec, in_=pk, func=AF.Copy, scale=kdsc), reads=[('ps', A_), ('ps', C_), ('ps', D_), 'kdsc'], writes=['kdec'])
                    pv = m.bank(A_)[:, 128:256]
                    s.op('pe', lambda e, pv=pv, cs=cs: e.transpose(out=pv, in_=vcf[:, cs], identity=ident), reads=['vc', 'ident'], writes=[('ps', A_), ('ps', D_)])
                    s.op('act', lambda e, pv=pv: e.activation(out=vb, in_=pv, func=AF.Copy, scale=beta), reads=[('ps', A_), ('ps', D_), 'beta'], writes=['vb'])
                    pu = m.bank(A_)[:, 256:384]
                    s.op('pe', lambda e, pu=pu, PTf=PTf: e.matmul(pu, lhsT=PTf, rhs=vb, start=True, stop=True), reads=[ptr, 'vb'], writes=[('ps', A_), ('ps', D_)])
                    s.op('act', lambda e, pu=pu: e.activation(out=u_sb, in_=pu, func=AF.Copy), reads=[('ps', A_), ('ps', D_)], writes=['u_sb'])
                    pw = m.bank(A_)[:, 384:512]
                    s.op('pe', lambda e, pw=pw, PTf=PTf: e.matmul(pw, lhsT=kbg, rhs=PTf, start=True, stop=True), reads=[ptr, 'kbg'], writes=[('ps', A_), ('ps', D_)])
                    s.op('dve', lambda e, pw=pw: e.tensor_copy(out=wT, in_=pw), reads=[('ps', A_), ('ps', D_)], writes=['wT'])
                    s.op('pool', lambda e, cs=cs: e.tensor_tensor(out=qdT, in0=qc[:, cs], in1=egcrow, op=ALU.mult), reads=['qc', 'egcrow'], writes=['qdT'])
                    po = m.bank(C_)[:, 256:384]
                    for c in range(2):
                        rows = slice(c * 64, (c + 1) * 64)
                        pvs = m.bank(D_)[rows, 256:384]
                        s.op('pe', lambda e, pvs=pvs, rows=rows: e.matmul(pvs, lhsT=wT[:, rows], rhs=S, start=True, stop=True), reads=['wT', 'S'], writes=[('ps', A_), ('ps', C_), ('ps', D_)])
                        s.op('dve', lambda e, pvs=pvs, rows=rows: e.tensor_tensor(out=vc_sb[rows, :], in0=u_sb[rows, :], in1=pvs, op=ALU.subtract),
                             reads=['u_sb', ('ps', A_), ('ps', C_), ('ps', D_)], writes=['vc_sb'])
                        s.op('pe', lambda e, po=po, rows=rows: e.matmul(po[:, rows], lhsT=S, rhs=qdT[:, rows], start=True, stop=False), reads=['S', 'qdT'], writes=[('ps', C_)])
                        s.op('pe', lambda e, po=po, rows=rows: e.matmul(po[:, rows], lhsT=vc_sb[rows, :], rhs=qkT[rows, rows], start=False, stop=True),
                             reads=['vc_sb', 'qkT'], writes=[('ps', C_)])
                        pks = m.bank(D_)[:, 384:512]
                        s.op('pe', lambda e, pks=pks, rows=rows: e.matmul(pks, lhsT=kdec[rows, :], rhs=vc_sb[rows, :], start=True, stop=True),
                             reads=['kdec', 'vc_sb'], writes=[('ps', A_), ('ps', D_)])
                        eg = egl0 if c == 0 else egl1
                        s.op('dve', lambda e, pks=pks, eg=eg: e.scalar_tensor_tensor(out=S, in0=S, scalar=eg, in1=pks, op0=ALU.mult, op1=ALU.add),
                             reads=['S', 'egl0', 'egl1', ('ps', A_), ('ps', D_)], writes=['S'])
                    s.op('act', lambda e, po=po, cs=cs: e.activation(out=ofull[:, cs], in_=po, func=AF.Copy), reads=[('ps', C_)], writes=['ofull'])
                s.op('act', lambda e: e.activation(out=sq, in_=ofull, func=AF.Square), reads=['ofull'], writes=['sq'])
                p2 = m.bank(B_)
                s.op('pe', lambda e, p2=p2: e.matmul(p2, lhsT=ones, rhs=sq, start=True, stop=True), reads=['sq', 'ones'], writes=[('ps', B_)])
                s.op('act', lambda e, p2=p2: e.activation(out=rstd, in_=p2, func=AF.Sqrt, scale=1.0 / 128, bias=EPSB[0]), reads=[('ps', B_), 'epsb'], writes=['rstd'])
                s.op('dve', lambda e: e.reciprocal(out=rstd, in_=rstd), reads=['rstd'], writes=['rstd'])
                s.op('dve', lambda e: e.scalar_tensor_tensor(out=ofull, in0=ofull, scalar=gn, in1=rstd, op0=ALU.mult, op1=ALU.mult), reads=['ofull', 'gn', 'rstd'], writes=['ofull'])
                yo = yout[ti % 2]
                s.op('pool', lambda e, yo=yo: e.tensor_tensor(out=yo, in0=ofull, in1=sz, op=ALU.mult), reads=['ofull', 'sz'], writes=[('yout', ti % 2)])
                s.dma('sp', o_d[h * 128:(h + 1) * 128, ti * 512:(ti + 1) * 512], yo, reads=[('yout', ti % 2)], is_output=True)
        HB = [alloc_head() for _ in range(2)]
        L = Lanes(s, 2)
        L.run([lambda P, h=h: head_body(h, P, HB[h], 4 * h + 0, 4 * h + 1, 4 * h + 2, 4 * h + 3) for h in range(2)])
        print("GDN2 ops", s.n_ops)
        s.emit()
    return nc


def ssd_inputs(W, l, i, uT_full):
    g = i // 2
    w_in = W['w_in'][l]
    cols = np.concatenate([np.arange(256*i, 256*i+256), 2048 + np.arange(256*i, 256*i+256), 4096 + np.arange(128*g, 128*g+128),
                           4608 + np.arange(128*g, 128*g+128), 5120 + np.arange(4*i, 4*i+4)])
    w = np.ascontiguousarray(w_in[:, cols])
    chans = np.stack([256*i + np.arange(128), 256*i + 128 + np.arange(128), 2048 + 128*g + np.arange(128), 2560 + 128*g + np.arange(128)], axis=1)
    cwl = W['ssd_conv_w'][l]; cbl = W['ssd_conv_b'][l]
    cw = np.zeros((128, 4, 5), np.float32)
    for k in range(4):
        cw[:, :, k] = cwl[k][chans]
    cw[:, :, 4] = cbl[chans]
    hp = np.zeros((128, 3, 16), np.float32)
    hp[:, 0, :] = np.tile(W['ssd_dt_bias'][l][4*i:4*i+4], 4)[None, :]
    hp[:, 1, :] = np.tile(W['ssd_a_log'][l][4*i:4*i+4], 4)[None, :]
    dcol = np.zeros((128, 2), np.float32)
    dl = W['ssd_d'][l]
    for c in range(2):
        dcol[:64, c] = dl[4*i + 2*c]; dcol[64:, c] = dl[4*i + 2*c + 1]
    return {"uT": uT_full, "w": w, "cw": cw, "hp": hp, "dcol": dcol}
def t5_onehot(T):
    n = np.arange(T)
    nf = np.maximum(n, 1).astype(np.float32)
    large = 16 + (np.log(nf / np.float32(16)) / np.float32(math.log(4096 / 16)) * np.float32(16)).astype(np.int32)
    bkt = np.where(n < 16, n, np.minimum(large, 31))
    oh = np.zeros((32, T), np.float32)
    oh[bkt, n] = 1.0
    return oh
def attn_inputs(W, l, i, uT_full, oh):
    w_in = W['w_in'][l]
    ws = []
    for hh in range(2):
        hd = 2 * i + hh
        cols = np.concatenate([13376 + 128 * hd + np.arange(128), 15424 + 128 * hd + np.arange(128), 17472 + 128 * hd + np.arange(128)])
        ws.append(w_in[:, cols])
    gn = np.ascontiguousarray(np.stack([W['attn_q_norm'][l], W['attn_k_norm'][l]], axis=1))
    rb = np.ascontiguousarray(W['rel_bias'][:, 2 * i:2 * i + 2])
    return {"uT": uT_full, "w": np.ascontiguousarray(np.stack(ws)), "gn": gn, "rb": rb, "oh": oh}
def gdn_inputs(W, l, i, uT_full):
    w_in = W['w_in'][l]
    ws, cws, hps = [], [], []
    for hh in range(2):
        hd = 2 * i + hh
        cols = np.concatenate([5152 + 128 * hd + np.arange(128), 7200 + 128 * hd + np.arange(128), 9248 + 128 * hd + np.arange(128),
                               11296 + 128 * hd + np.arange(128), [13344 + hd], [13360 + hd]])
        ws.append(w_in[:, cols])
        chans = np.stack([128 * hd + np.arange(128), 2048 + 128 * hd + np.arange(128), 4096 + 128 * hd + np.arange(128)], axis=1)
        cwl = W['gdn_conv_w'][l]
        cw = np.zeros((128, 3, 4), np.float32)
        for k in range(4):
            cw[:, :, k] = cwl[k][chans]
        cws.append(cw)
        hp = np.zeros((128, 2), np.float32)
        hp[:, 0] = W['gdn_dt_bias'][l][hd]
        hp[:, 1] = W['gdn_a_log'][l][hd]
        hps.append(hp)
    return {"uT": uT_full, "w": np.ascontiguousarray(np.stack(ws)), "cw": np.stack(cws), "hp": np.stack(hps),
            "gn": np.ascontiguousarray(W['gdn_norm'][l].reshape(128, 1))}


_NC = {}


def _get(name, fn):
    if name not in _NC:
        _NC[name] = fn()
    return _NC[name]


def kernel(**I):
    I = {k: np.asarray(v) for k, v in I.items()}
    n = 8
    cores = list(range(n))
    x = I['x']
    c = I['c']
    cT = np.ascontiguousarray(c.reshape(16, 128).T)
    w_mod, b_mod = I['w_mod'], I['b_mod']
    ins = []
    for i in range(n):
        ins.append({"wm": np.ascontiguousarray(w_mod[:, :, i * 2304:(i + 1) * 2304]),
                    "bm": np.ascontiguousarray(np.stack([b_mod[l, i * 2304:(i + 1) * 2304].reshape(18, 128).T for l in range(4)])),
                    "cT": cT})
    res = run_bass_kernel_spmd(_get('mod', build_mod), ins, core_ids=cores)
    modT = np.concatenate([r["modp"] for r in res.results], axis=2)
    hT = [np.ascontiguousarray(x[0, i * TL:(i + 1) * TL, :].T) for i in range(n)]
    oh = t5_onehot(T)
    for l in range(4):
        gains = np.ascontiguousarray(np.concatenate([I['norm_ffn1'][l].reshape(16, 128).T, I['norm_mix'][l].reshape(16, 128).T], axis=1))
        ins = [{"hin": hT[i], "modT": modT[l], "gains": gains, "wg": I['ffn1_w_gate'][l], "wu": I['ffn1_w_up'][l],
                "wd": I['ffn1_w_down'][l]} for i in range(n)]
        res = run_bass_kernel_spmd(_get('A', build_A), ins, core_ids=cores)
        hT = [r["hout"] for r in res.results]
        uT = [r["uout"] for r in res.results]
        uT_full = np.ascontiguousarray(np.concatenate(uT, axis=1))
        ins = [ssd_inputs(I, l, i, uT_full) for i in range(n)]
        res = run_bass_kernel_spmd(_get('ssd', build_ssd), ins, core_ids=cores)
        ys_full = np.concatenate([r["y"] for r in res.results], axis=0)
        ins = [gdn_inputs(I, l, i, uT_full) for i in range(n)]
        res = run_bass_kernel_spmd(_get('gdn', build_gdn2), ins, core_ids=cores)
        yg_full = np.concatenate([r["o"] for r in res.results], axis=0)
        ins = [attn_inputs(I, l, i, uT_full, oh) for i in range(n)]
        res = run_bass_kernel_spmd(_get('attn', build_attn2), ins, core_ids=cores)
        ya_full = np.concatenate([r["o"] for r in res.results], axis=0)
        gains = np.ascontiguousarray(np.concatenate([I['ssd_norm'][l].reshape(16, 128).T, I['norm_ffn2'][l].reshape(16, 128).T], axis=1))
        wgt = np.ascontiguousarray(I['w_in'][l][:, 19520:25664])
        ins = []
        for i in range(n):
            tsl = slice(i * TL, (i + 1) * TL)
            ins.append({"hin": hT[i], "uin": uT[i], "ys": np.ascontiguousarray(ys_full[:, tsl]), "yg": np.ascontiguousarray(yg_full[:, tsl]),
                        "ya": np.ascontiguousarray(ya_full[:, tsl]),
                        "modT": modT[l], "gains": gains, "wos": I['w_o_ssd'][l], "wog": I['w_o_gdn'][l], "woa": I['w_o_attn'][l],
                        "wgt": wgt, "wout": I['w_out'][l], "wg": I['ffn2_w_gate'][l], "wu": I['ffn2_w_up'][l], "wd": I['ffn2_w_down'][l]})
        res = run_bass_kernel_spmd(_get('C', build_C), ins, core_ids=cores)
        hT = [r["hout"] for r in res.results]
    out = np.concatenate([h.T for h in hT], axis=0)[None]
    return np.ascontiguousarray(out.astype(np.float32))
```
